# Optimizing a Trainium2 kernel written in Bass

```python
import math
import jax, jax.numpy as jnp
from jax import lax
import numpy as np

D_MODEL = 1024
BATCH = 8
SEQ = 4096
DEPTH = 1

GRID_W = 64
CTX_LEN = 256
MIX_WIDTH = D_MODEL
ATT_WIDTH = MIX_WIDTH // 2
POOL_WIDTH = MIX_WIDTH - ATT_WIDTH
ATT_HEADS = 4
ATT_HEAD_DIM = ATT_WIDTH // ATT_HEADS
QK_DIM = ATT_HEAD_DIM // 2
POOL_WINDOWS = (2, 4, 8, 16)
POOL_GROUPS = len(POOL_WINDOWS)
POOL_GROUP_DIM = POOL_WIDTH // POOL_GROUPS
IN_WIDTH = 3 * ATT_WIDTH + POOL_WIDTH
D_FF = ((8 * D_MODEL // 3 + 255) // 256) * 256
N_MOD = 9
ROPE_BASE = 10000.0
Q_BLOCK = 128
EPS = 1e-6

kernel_name = "hybrid_diffattn_pool_macaron_dit_layer"


def _lambda_init(layer):
    return 0.8 - 0.6 * math.exp(-0.3 * layer)


def rms_norm(x, g):
    xf = x.astype(jnp.float32)
    y = xf * lax.rsqrt(jnp.mean(xf * xf, axis=-1, keepdims=True) + EPS)
    return (y * g.astype(jnp.float32)).astype(x.dtype)


def adaln_params(cond, w, b):
    m = (jax.nn.silu(cond) @ w + b).reshape(-1, 1, N_MOD, D_MODEL)
    return [m[:, :, i] for i in range(N_MOD)]


def modulate(h, shift, scale):
    return h * (1 + scale) + shift


def swiglu(h, w_in, w_out):
    gate, up = jnp.split(h @ w_in, 2, axis=-1)
    return (jax.nn.silu(gate) * up) @ w_out


def axial_rope_tables(n):
    rows = n // GRID_W
    row = jnp.repeat(jnp.arange(rows, dtype=jnp.float32), GRID_W)
    col = jnp.tile(jnp.arange(GRID_W, dtype=jnp.float32), rows)
    nf = QK_DIM // 4
    freqs = ROPE_BASE ** (-jnp.arange(nf, dtype=jnp.float32) / nf)
    ar = row[:, None] * freqs
    ac = col[:, None] * freqs
    ang = jnp.concatenate([ar, ar, ac, ac], axis=-1)
    return jnp.cos(ang), jnp.sin(ang)


def apply_axial_rope(t, cos, sin):
    tf = t.astype(jnp.float32)
    tr = tf.reshape(tf.shape[:-1] + (2, 2, QK_DIM // 4))
    rot = jnp.stack([-tr[..., 1, :], tr[..., 0, :]], axis=-2).reshape(tf.shape)
    c = cos[None, :, None, None, :]
    s = sin[None, :, None, None, :]
    return (tf * c + rot * s).astype(t.dtype)


def split_qk(t):
    b, n, _ = t.shape
    return t.reshape(b, n, ATT_HEADS, 2, QK_DIM)


def split_v(t):
    b, n, _ = t.shape
    return t.reshape(b, n, ATT_HEADS, ATT_HEAD_DIM).transpose(0, 2, 1, 3)


def diff_attention(q, k, v, lam, g_sub, lam_init):
    s = jnp.einsum('bhmqd,bhmkd->bhmqk', q, k).astype(jnp.float32) * (QK_DIM ** -0.5)
    p = jax.nn.softmax(s, axis=-1)
    a = p[:, :, 0] - lam * p[:, :, 1]
    o = jnp.einsum('bhqk,bhkv->bhqv', a.astype(v.dtype), v)
    return rms_norm(o, g_sub) * (1.0 - lam_init)


def multiscale_pool(u, w_pool, pool_scale):
    b, n, _ = u.shape
    uf = u.astype(jnp.float32)
    csum = jnp.concatenate([jnp.zeros((b, 1, POOL_WIDTH), jnp.float32), jnp.cumsum(uf, axis=1)], axis=1)
    t = jnp.arange(n)
    outs = []
    for gi, w in enumerate(POOL_WINDOWS):
        lo = w // 2
        hi = w - w // 2 - 1
        start = jnp.maximum(t - lo, 0)
        end = jnp.minimum(t + hi, n - 1)
        sl = slice(gi * POOL_GROUP_DIM, (gi + 1) * POOL_GROUP_DIM)
        seg = csum[..., sl]
        total = jnp.take(seg, end + 1, axis=1) - jnp.take(seg, start, axis=1)
        cnt = (end - start + 1).astype(jnp.float32)
        diff = (total / cnt[None, :, None] - uf[..., sl]).astype(u.dtype)
        outs.append(diff @ w_pool[gi])
    return jnp.concatenate(outs, axis=-1) * pool_scale


def setup_inputs(seed: int = 0) -> dict:
    key = jax.random.key(seed)
    ks = jax.random.split(key, 24)
    f32 = jnp.float32
    nrm = lambda k, shape, s: jax.random.normal(k, shape, f32) * s
    gain = lambda k, shape: 1.0 + 0.05 * jax.random.normal(k, shape, f32)
    L, D = DEPTH, D_MODEL
    return {
        "x": nrm(ks[0], (BATCH, SEQ, D), 1.0),
        "c": nrm(ks[1], (BATCH, D), 1.0),
        "ctx": nrm(ks[2], (BATCH, CTX_LEN, D), 1.0),
        "c_ctx": nrm(ks[3], (D,), 1.0),
        "w_mod": nrm(ks[4], (L, D, N_MOD * D), 0.5 * D ** -0.5),
        "b_mod": nrm(ks[5], (L, N_MOD * D), 0.01),
        "g_ffn1": gain(ks[6], (L, D)),
        "ffn1_w_in": nrm(ks[7], (L, D, 2 * D_FF), D ** -0.5),
        "ffn1_w_out": nrm(ks[8], (L, D_FF, D), D_FF ** -0.5),
        "g_mix": gain(ks[9], (L, D)),
        "w_in": nrm(ks[10], (L, D, IN_WIDTH), D ** -0.5),
        "lambda_q1": nrm(ks[11], (L, QK_DIM), 0.1),
        "lambda_k1": nrm(ks[12], (L, QK_DIM), 0.1),
        "lambda_q2": nrm(ks[13], (L, QK_DIM), 0.1),
        "lambda_k2": nrm(ks[14], (L, QK_DIM), 0.1),
        "g_sub": gain(ks[15], (L, ATT_HEAD_DIM)),
        "w_pool": nrm(ks[16], (L, POOL_GROUPS, POOL_GROUP_DIM, POOL_GROUP_DIM), POOL_GROUP_DIM ** -0.5),
        "pool_scale": gain(ks[17], (L, POOL_WIDTH)),
        "w_out": nrm(ks[18], (L, MIX_WIDTH, D), MIX_WIDTH ** -0.5),
        "g_ffn2": gain(ks[19], (L, D)),
        "ffn2_w_in": nrm(ks[20], (L, D, 2 * D_FF), D ** -0.5),
        "ffn2_w_out": nrm(ks[21], (L, D_FF, D), D_FF ** -0.5),
        "g_final": gain(ks[22], (D,)),
    }


def reference(x, c, ctx, c_ctx, w_mod, b_mod, g_ffn1, ffn1_w_in, ffn1_w_out, g_mix, w_in,
              lambda_q1, lambda_k1, lambda_q2, lambda_k2, g_sub, w_pool, pool_scale, w_out,
              g_ffn2, ffn2_w_in, ffn2_w_out, g_final):
    b, n, _ = x.shape
    nblk = n // Q_BLOCK
    cos, sin = axial_rope_tables(n)
    cx = ctx
    for l in range(DEPTH):
        last = l == DEPTH - 1
        sh1, sc1, gt1, sh2, sc2, gt2, sh3, sc3, gt3 = adaln_params(c, w_mod[l], b_mod[l])
        ch1, cc1, cg1, ch2, cc2, cg2, ch3, cc3, cg3 = adaln_params(c_ctx, w_mod[l], b_mod[l])

        x = x + 0.5 * gt1 * swiglu(modulate(rms_norm(x, g_ffn1[l]), sh1, sc1), ffn1_w_in[l], ffn1_w_out[l])
        cx = cx + 0.5 * cg1 * swiglu(modulate(rms_norm(cx, g_ffn1[l]), ch1, cc1), ffn1_w_in[l], ffn1_w_out[l])

        hx = modulate(rms_norm(x, g_mix[l]), sh2, sc2) @ w_in[l]
        hc = modulate(rms_norm(cx, g_mix[l]), ch2, cc2) @ w_in[l]
        qx, kx, vx, ux = jnp.split(hx, [ATT_WIDTH, 2 * ATT_WIDTH, 3 * ATT_WIDTH], axis=-1)
        qc, kc, vc, uc = jnp.split(hc, [ATT_WIDTH, 2 * ATT_WIDTH, 3 * ATT_WIDTH], axis=-1)

        lam_init = _lambda_init(l)
        lam = (jnp.exp(jnp.sum(lambda_q1[l].astype(jnp.float32) * lambda_k1[l].astype(jnp.float32)))
               - jnp.exp(jnp.sum(lambda_q2[l].astype(jnp.float32) * lambda_k2[l].astype(jnp.float32)))
               + lam_init)

        q_lat = apply_axial_rope(split_qk(qx), cos, sin).transpose(0, 2, 3, 1, 4)
        k_lat = apply_axial_rope(split_qk(kx), cos, sin).transpose(0, 2, 3, 1, 4)
        q_ctx = split_qk(qc).transpose(0, 2, 3, 1, 4)
        k_ctx = split_qk(kc).transpose(0, 2, 3, 1, 4)
        v_lat, v_ctx = split_v(vx), split_v(vc)
        k_all = jnp.concatenate([k_lat, k_ctx], axis=3)
        v_all = jnp.concatenate([v_lat, v_ctx], axis=2)

        qb = q_lat.reshape(b, ATT_HEADS, 2, nblk, Q_BLOCK, QK_DIM).transpose(3, 0, 1, 2, 4, 5)
        att = lax.map(lambda qq: diff_attention(qq, k_all, v_all, lam, g_sub[l], lam_init), qb)
        att_lat = att.transpose(1, 0, 3, 2, 4).reshape(b, n, ATT_WIDTH)
        pool_lat = multiscale_pool(ux, w_pool[l], pool_scale[l])
        x = x + gt2 * (jnp.concatenate([att_lat, pool_lat], axis=-1) @ w_out[l])

        if not last:
            att_c = diff_attention(q_ctx, k_ctx, v_ctx, lam, g_sub[l], lam_init)
            att_c = att_c.transpose(0, 2, 1, 3).reshape(cx.shape[0], cx.shape[1], ATT_WIDTH)
            pool_c = multiscale_pool(uc, w_pool[l], pool_scale[l])
            cx = cx + cg2 * (jnp.concatenate([att_c, pool_c], axis=-1) @ w_out[l])
            cx = cx + 0.5 * cg3 * swiglu(modulate(rms_norm(cx, g_ffn2[l]), ch3, cc3), ffn2_w_in[l], ffn2_w_out[l])

        x = x + 0.5 * gt3 * swiglu(modulate(rms_norm(x, g_ffn2[l]), sh3, sc3), ffn2_w_in[l], ffn2_w_out[l])

    return rms_norm(x, g_final)
```

```python
import math
from contextlib import ExitStack
import numpy as np
import concourse.bass as bass
import concourse.mybir as mybir
from concourse.bass_utils import run_bass_kernel_spmd

F32 = mybir.dt.float32
BF16 = mybir.dt.bfloat16
ALU = mybir.AluOpType
AF = mybir.ActivationFunctionType

D = 1024
SEQ = 4096
CTX = 256
NTOK = SEQ + CTX
DFF = 2816
NF = DFF // 128
EPS = 1e-6
LAM_INIT = 0.8 - 0.6 * math.exp(-0.3 * 0)
POOL_W = (2, 4, 8, 16)
G = 256
NG_LAT = SEQ // G
NKT = NTOK // 128

QATTR = {"pe": "tensor", "act": "scalar", "dve": "vector", "pool": "gpsimd", "sp": "sync"}
COMPUTE = ("pe", "act", "dve", "pool")


class Res:
    __slots__ = ("name", "lw", "rs")

    def __init__(self, name):
        self.name = name
        self.lw = None
        self.rs = []


class Op:
    __slots__ = ("q", "stream", "fn", "deps", "signal", "ms", "is_dma")


class Prog:
    def __init__(self, nc):
        self.nc = nc
        self.queues = {q: [] for q in QATTR}
        self.dma_count = {}
        self.last_dma = {}

    def _track(self, op, reads, writes):
        deps = []
        for r in reads:
            if r.lw is not None:
                deps.append(r.lw)
        for w in writes:
            if w.lw is not None:
                deps.append(w.lw)
            deps.extend(w.rs)
        out = []
        seen = set()
        for d in deps:
            if id(d) in seen or d is op:
                continue
            seen.add(id(d))
            if (not op.is_dma) and (not d.is_dma) and d.q == op.q and op.q == "pe":
                continue
            d.signal = True
            out.append(d)
        op.deps = out
        for r in reads:
            r.rs.append(op)
        for w in writes:
            w.lw = op
            w.rs = []

    def op(self, q, fn, reads=(), writes=()):
        o = Op()
        o.q = q
        o.stream = q
        o.fn = fn
        o.signal = False
        o.ms = None
        o.is_dma = False
        self._track(o, reads, writes)
        self.queues[q].append(o)
        return o

    def dma(self, stream, out, in_, reads=(), writes=(), q="sp", **kw):
        o = Op()
        o.q = q
        o.stream = "d_" + stream
        o.fn = lambda eng, out=out, in_=in_, kw=kw: eng.dma_start(out=out, in_=in_, **kw)
        o.signal = True
        o.is_dma = True
        n = self.dma_count.get(o.stream, 0) + 1
        self.dma_count[o.stream] = n
        o.ms = 16 * n
        self._track(o, reads, writes)
        self.queues[q].append(o)
        self.last_dma[o.stream] = o
        return o

    def barrier(self):
        lasts = []
        for q in COMPUTE:
            for o in reversed(self.queues[q]):
                if not o.is_dma and o.fn is not None:
                    lasts.append(o)
                    break
        lasts.extend(self.last_dma.values())
        for q in QATTR:
            o = Op()
            o.q = q
            o.stream = q
            o.fn = None
            o.signal = False
            o.ms = None
            o.is_dma = False
            o.deps = []
            for d in lasts:
                if d.q == q and not d.is_dma:
                    continue
                d.signal = True
                o.deps.append(d)
            self.queues[q].append(o)

    def emit(self):
        nc = self.nc
        for q in COMPUTE:
            c = 0
            for o in self.queues[q]:
                if o.is_dma or o.fn is None:
                    continue
                if o.signal:
                    c += 1
                    o.ms = c
        with ExitStack() as es:
            sems = {}
            for q in COMPUTE:
                sems[q] = es.enter_context(nc.semaphore("s_" + q))
            for s in self.dma_count:
                sems[s] = es.enter_context(nc.semaphore("s_" + s))
            block = es.enter_context(nc.Block())
            for q, attr in QATTR.items():
                ops = self.queues[q]
                final_waits = []
                if q == "sp":
                    final_waits = [(s, 16 * n) for s, n in self.dma_count.items()]

                def section(eng, ops=ops, final_waits=final_waits):
                    seen = {}
                    for o in ops:
                        for d in o.deps:
                            if seen.get(d.stream, 0) >= d.ms:
                                continue
                            seen[d.stream] = d.ms
                            eng.wait_ge(sems[d.stream], d.ms)
                        if o.fn is None:
                            continue
                        ins = o.fn(eng)
                        if o.is_dma:
                            ins.then_inc(sems[o.stream], 16)
                        elif o.signal:
                            ins.then_inc(sems[o.stream], 1)
                    for s, v in final_waits:
                        eng.wait_ge(sems[s], v)

                getattr(block, attr)(section)


class Arena:
    def __init__(self, nc, nbytes):
        self.t = nc.alloc_sbuf_tensor("arena", [128, nbytes // 4], F32)
        self.cap = nbytes
        self.off = 0
        self.peak = 0

    def alloc(self, shape, dt):
        ne = int(np.prod(shape[1:]))
        n = ne * (4 if dt == F32 else 2)
        n = (n + 63) // 64 * 64
        assert self.off + n <= self.cap, ("SBUF arena overflow", self.off, n, self.cap)
        v = self.t[0:shape[0], self.off // 4:(self.off + n) // 4]
        if dt != F32:
            v = v.bitcast(dt)
        v = v[:, 0:ne]
        if len(shape) == 3:
            v = v.rearrange("p (a b) -> p a b", a=shape[1])
        elif len(shape) == 4:
            v = v.rearrange("p (a b c) -> p a b c", a=shape[1], b=shape[2])
        self.off += n
        self.peak = max(self.peak, self.off)
        return v


def build_nc(debug=False, phases=4):
    nc = bass.Bass("TRN2", target_bir_lowering=False)

    def din(name, shape):
        return nc.dram_tensor(name, list(shape), F32, kind="ExternalInput").ap()

    x_d = din("x", [SEQ, D])
    ctx_d = din("ctx", [CTX, D])
    cvec_d = din("cvec", [2, D])
    wmod_d = din("w_mod", [D, 9 * D])
    bmod_d = din("b_mod", [9 * D])
    g1_d = din("g_ffn1", [D])
    f1in_d = din("ffn1_w_in", [D, 2 * DFF])
    f1out_d = din("ffn1_w_out", [DFF, D])
    gmix_d = din("g_mix", [D])
    win_d = din("w_in", [D, 2048])
    lam_d = din("lam4", [4 * 64])
    gsub_d = din("g_sub", [128])
    wpool_d = din("w_pool", [4, 128, 128])
    pscale_d = din("pool_scale", [512])
    wout_d = din("w_out", [D, D])
    g2_d = din("g_ffn2", [D])
    f2in_d = din("ffn2_w_in", [D, 2 * DFF])
    f2out_d = din("ffn2_w_out", [DFF, D])
    gfin_d = din("g_final", [D])
    ident_d = din("c_ident", [128, 128])
    ropeP_d = din("c_ropeP", [128, 128])
    cos_d = din("c_cos", [128, SEQ])
    sin_d = din("c_sin", [128, SEQ])
    band_d = din("c_band", [128, 20 * 128])
    out_d = nc.dram_tensor("out", [SEQ, D], F32, kind="ExternalOutput").ap()
    x1s_d = nc.dram_tensor("x1s", [NTOK, D], F32,
                           kind="ExternalOutput" if debug else "Internal").ap()
    mscr_d = nc.dram_tensor("mscr", [2, 9 * D], F32,
                            kind="ExternalOutput" if debug else "Internal").ap()

    P = Prog(nc)
    cap = nc.sbuf_bytes_remaining - 256
    cap = cap // 64 * 64
    A = Arena(nc, cap)
    PS = [nc.alloc_psum_tensor("ps%d" % i, [128, 512], F32) for i in range(8)]
    PSR = [Res("ps%d" % i) for i in range(8)]
    PSB = [PS[i][:, :].bitcast(BF16) for i in range(8)]

    ident_bf = A.alloc([128, 128], BF16)
    ones_bf = A.alloc([128, 128], BF16)
    onesdiv_f = A.alloc([128, 128], F32)
    ropeP_f = A.alloc([128, 128], F32)
    modT = A.alloc([128, 2, 9, 8], F32)
    gT = A.alloc([128, 3, 8], F32)
    psT = A.alloc([128, 4], F32)
    gsubs = A.alloc([128, 1], F32)
    neglam = A.alloc([128, 1], F32)
    ABt = A.alloc([128, 12, 8], F32)
    stat = A.alloc([128, 16], F32)
    xt = [A.alloc([128, D], F32) for _ in range(4)]
    xn = [A.alloc([128, D], BF16) for _ in range(2)]
    junk = A.alloc([128, D], BF16)
    hT = A.alloc([128, 8, G], BF16)
    stage = [A.alloc([128, D], F32) for _ in range(4)]
    tmpf = [A.alloc([128, 512], F32) for _ in range(2)]
    R_ident, R_ones, R_onesdiv, R_ropeP = Res("ident"), Res("ones"), Res("onesdiv"), Res("ropeP")
    R_modT, R_gT, R_psT, R_gsubs, R_neglam, R_AB, R_stat = (Res("modT"), Res("gT"), Res("psT"),
                                                           Res("gsubs"), Res("neglam"), Res("AB"), Res("stat"))
    R_xt = [Res("xt%d" % i) for i in range(4)]
    R_xn = [Res("xn%d" % i) for i in range(2)]
    R_junk = Res("junk")
    R_hT = Res("hT")
    R_stage = [Res("stage%d" % i) for i in range(4)]
    R_tmpf = [Res("tmpf%d" % i) for i in range(2)]
    R_x1s = [Res("x1s%d" % i) for i in range(NKT)]
    R_mscr = Res("mscr")
    persist_off = A.off

    stage_ctr = [0]

    def load_cast(dst_ap, src_ap, shape3, R_dst):
        s = stage_ctr[0] % 4
        stage_ctr[0] += 1
        ne = int(np.prod(shape3[1:]))
        sv = stage[s][:, 0:ne]
        if len(shape3) == 3:
            sv = sv.rearrange("p (a b) -> p a b", a=shape3[1])
        P.dma("stage%d" % s, sv, src_ap, writes=[R_stage[s]])
        P.op("pool", lambda e, d=dst_ap, v=sv: e.tensor_copy(d, v), reads=[R_stage[s]], writes=[R_dst])

    P.dma("c_rp", ropeP_f[:, :], ropeP_d, writes=[R_ropeP])
    load_cast(ident_bf[:, :], ident_d, [128, 128], R_ident)
    P.op("pool", lambda e: e.memset(ones_bf[:, :], 1.0), writes=[R_ones])
    P.op("pool", lambda e: e.memset(onesdiv_f[:, :], 1.0 / 128), writes=[R_onesdiv])
    P.dma("c_g", gT[:, 0, :], g1_d.rearrange("(c p) -> p c", p=128), writes=[R_gT], allow_slow_non_contiguous=True)
    P.dma("c_g", gT[:, 1, :], gmix_d.rearrange("(c p) -> p c", p=128), writes=[R_gT], allow_slow_non_contiguous=True)
    P.dma("c_g", gT[:, 2, :], g2_d.rearrange("(c p) -> p c", p=128), writes=[R_gT], allow_slow_non_contiguous=True)
    P.dma("c_ps", psT[:, :], pscale_d.rearrange("(c p) -> p c", p=128), writes=[R_psT], allow_slow_non_contiguous=True)
    P.dma("c_gs", gsubs[:, :], gsub_d.rearrange("(p o) -> p o", o=1), writes=[R_gsubs])
    P.op("dve", lambda e: e.tensor_scalar(gsubs[:, :], gsubs[:, :], 1.0 - LAM_INIT, None, ALU.mult),
         reads=[R_gsubs], writes=[R_gsubs])

    lamt = tmpf[0][:, 0:256]
    P.dma("c_lam", lamt, lam_d.partition_broadcast(128), writes=[R_tmpf[0]])
    P.op("dve", lambda e: e.scalar_tensor_tensor(junk[:, 0:64], lamt[:, 0:64], 1.0, lamt[:, 64:128], ALU.mult, ALU.mult,
                                                 accum_out=stat[:, 0:1]), reads=[R_tmpf[0]], writes=[R_stat, R_junk])
    P.op("dve", lambda e: e.scalar_tensor_tensor(junk[:, 0:64], lamt[:, 128:192], 1.0, lamt[:, 192:256], ALU.mult, ALU.mult,
                                                 accum_out=stat[:, 1:2]), reads=[R_tmpf[0], R_junk, R_stat], writes=[R_stat, R_junk])
    P.op("act", lambda e: e.activation(stat[:, 2:4], stat[:, 0:2], AF.Exp), reads=[R_stat], writes=[R_stat])
    P.op("dve", lambda e: e.tensor_tensor(neglam[:, :], stat[:, 3:4], stat[:, 2:3], ALU.subtract),
         reads=[R_stat], writes=[R_neglam])
    P.op("dve", lambda e: e.tensor_scalar(neglam[:, :], neglam[:, :], -LAM_INIT, None, ALU.add),
         reads=[R_neglam], writes=[R_neglam])

    craw = tmpf[1][:, 16:32].rearrange("p (k c) -> p k c", k=2)
    P.dma("c_c", craw[:, 0, :], cvec_d[0, :].rearrange("(c p) -> p c", p=128), writes=[R_tmpf[1]], allow_slow_non_contiguous=True)
    P.dma("c_c2", craw[:, 1, :], cvec_d[1, :].rearrange("(c p) -> p c", p=128), writes=[R_tmpf[1]], allow_slow_non_contiguous=True)

    p0_mark = A.off
    scT = A.alloc([128, 8, 128], F32)
    R_scT = Res("scT")
    P.op("pool", lambda e: e.memset(scT[:, :, :], 0.0), writes=[R_scT])
    for k in range(2):
        P.op("act", lambda e, k=k: e.activation(scT[:, :, k], craw[:, k, :], AF.Silu), reads=[R_tmpf[1], R_scT], writes=[R_scT])
    bm2 = A.alloc([2, 9 * D], F32)
    msb = A.alloc([2, 3 * D], F32)
    wms = [A.alloc([128, 3 * D], F32) for _ in range(3)]
    R_bm2, R_msb = Res("bm2"), Res("msb")
    R_wms = [Res("wms%d" % i) for i in range(3)]
    for k in range(2):
        P.dma("c_bm%d" % k, bm2[k:k + 1, :], bmod_d.rearrange("(o n) -> o n", o=1), writes=[R_bm2])
    wm_ctr = 0
    for gq in range(3):
        for kc in range(8):
            s = wm_ctr % 3
            wm_ctr += 1
            P.dma("wms%d" % s, wms[s][:, :], wmod_d[kc * 128:(kc + 1) * 128, gq * 3 * D:(gq + 1) * 3 * D],
                  writes=[R_wms[s]])
            for j in range(6):
                P.op("pe", lambda e, j=j, s=s, kc=kc: e.matmul(PS[j][:, :], scT[:, kc, :], wms[s][:, j * 512:(j + 1) * 512],
                                                                 start=(kc == 0), stop=(kc == 7)),
                     reads=[R_scT, R_wms[s]], writes=[PSR[j]])
        for j in range(6):
            P.op("dve", lambda e, j=j, gq=gq: e.tensor_tensor(msb[:, j * 512:(j + 1) * 512], PS[j][0:2, :],
                                                             bm2[:, gq * 3 * D + j * 512: gq * 3 * D + (j + 1) * 512], ALU.add),
                 reads=[PSR[j], R_bm2], writes=[R_msb])
        P.dma("mscr_w", mscr_d[:, gq * 3 * D:(gq + 1) * 3 * D], msb[:, :], reads=[R_msb], writes=[R_mscr])
        for k in range(2):
            P.dma("modT%d" % k, modT[:, k, 3 * gq:3 * gq + 3, :],
                  mscr_d[k, gq * 3 * D:(gq + 1) * 3 * D].rearrange("(v c p) -> p v c", p=128, c=8),
                  reads=[R_mscr], writes=[R_modT], allow_slow_non_contiguous=True)
        for k in range(2):
            ia = (k * 3 + gq) * 2
            P.op("dve", lambda e, k=k, gq=gq, ia=ia: e.scalar_tensor_tensor(ABt[:, ia, :], modT[:, k, 3 * gq + 1, :], 1.0,
                                                                           gT[:, gq, :], ALU.add, ALU.mult),
                 reads=[R_modT, R_gT], writes=[R_AB])
            P.op("dve", lambda e, k=k, gq=gq, ia=ia: e.tensor_copy(ABt[:, ia + 1, :], modT[:, k, 3 * gq, :]),
                 reads=[R_modT], writes=[R_AB])
    P.barrier()
    A.off = p0_mark

    def prep(ntile, slots, kind, s, hTv):
        ia = (kind * 3 + s) * 2
        for tt in range(ntile):
            sl = slots[tt]
            xs = xt[sl]
            c0 = 4 + 3 * tt
            P.op("dve", lambda e, xs=xs, c0=c0: e.scalar_tensor_tensor(junk[:, :], xs[:, :], 1.0, xs[:, :], ALU.mult, ALU.mult,
                                                                      accum_out=stat[:, c0:c0 + 1]),
                 reads=[R_xt[sl]], writes=[R_junk, R_stat])
            P.op("act", lambda e, c0=c0: e.activation(stat[:, c0 + 1:c0 + 2], stat[:, c0:c0 + 1], AF.Ln, bias=EPS, scale=1.0 / D),
                 reads=[R_stat], writes=[R_stat])
            P.op("act", lambda e, c0=c0: e.activation(stat[:, c0 + 2:c0 + 3], stat[:, c0 + 1:c0 + 2], AF.Exp, scale=-0.5),
                 reads=[R_stat], writes=[R_stat])
            P.op("dve", lambda e, xs=xs, c0=c0, tt=tt: e.tensor_scalar(xn[tt][:, :], xs[:, :], stat[:, c0 + 2:c0 + 3], None, ALU.mult),
                 reads=[R_xt[sl], R_stat], writes=[R_xn[tt]])
            for c in range(8):
                P.op("pe", lambda e, c=c, tt=tt: e.transpose(PSB[6 + tt][:, c * 128:(c + 1) * 128], xn[tt][:, c * 128:(c + 1) * 128],
                                                             ident_bf[:, :]),
                     reads=[R_xn[tt], R_ident], writes=[PSR[6 + tt]])
            for c in range(8):
                P.op("dve", lambda e, c=c, tt=tt, ia=ia: e.tensor_scalar(hTv[:, c, tt * 128:(tt + 1) * 128],
                                                                        PSB[6 + tt][:, c * 128:(c + 1) * 128],
                                                                        ABt[:, ia, c:c + 1], ABt[:, ia + 1, c:c + 1], ALU.mult, ALU.add),
                     reads=[PSR[6 + tt], R_AB], writes=[R_hT])

    def tok_rows(g, tt):
        r0 = g * G + tt * 128
        return r0

    def ffn_phase(which):
        fin_d, fout_d = (f1in_d, f1out_d) if which == 1 else (f2in_d, f2out_d)
        s_idx = 0 if which == 1 else 2
        groups = list(range(NG_LAT + 1)) if which == 1 else list(range(NG_LAT))
        mark = A.off
        w1 = A.alloc([128, 8, 2 * DFF], BF16)
        w2 = A.alloc([128, NF, D], BF16)
        gate_bc = [A.alloc([128, D], F32) for _ in range(2 if which == 1 else 1)]
        gfin_bc = A.alloc([128, D], F32) if which == 2 else None
        sg = [A.alloc([128, G], F32) for _ in range(2)]
        aT = [A.alloc([128, G], BF16) for _ in range(2)]
        R_w1g = [Res("w1g%d" % f) for f in range(NF)]
        R_w1u = [Res("w1u%d" % f) for f in range(NF)]
        R_w2 = [Res("w2_%d" % f) for f in range(NF)]
        R_gbc = [Res("gbc0"), Res("gbc1")]
        R_gfin = Res("gfin")
        R_sg = [Res("sg0"), Res("sg1")]
        R_aT = [Res("aT0"), Res("aT1")]

        def src_rows(g, tt):
            if which == 1:
                if g < NG_LAT:
                    return x_d[g * G + tt * 128: g * G + (tt + 1) * 128, :]
                return ctx_d[tt * 128:(tt + 1) * 128, :]
            r0 = g * G + tt * 128
            return x1s_d[r0:r0 + 128, :]

        def load_group(g):
            for tt in range(2):
                sl = (g % 2) * 2 + tt
                rds = [R_x1s[g * 2 + tt]] if which == 2 else []
                P.dma("xt%d" % sl, xt[sl][:, :], src_rows(g, tt), reads=rds, writes=[R_xt[sl]])

        v_gate = 3 * s_idx + 2
        P.dma("gbc0", gate_bc[0][:, :], mscr_d[0, v_gate * D:(v_gate + 1) * D].partition_broadcast(128),
              reads=[R_mscr], writes=[R_gbc[0]])
        P.op("dve", lambda e: e.tensor_scalar(gate_bc[0][:, :], gate_bc[0][:, :], 0.5, None, ALU.mult),
             reads=[R_gbc[0]], writes=[R_gbc[0]])
        if which == 1:
            P.dma("gbc1", gate_bc[1][:, :], mscr_d[1, v_gate * D:(v_gate + 1) * D].partition_broadcast(128),
                  reads=[R_mscr], writes=[R_gbc[1]])
            P.op("dve", lambda e: e.tensor_scalar(gate_bc[1][:, :], gate_bc[1][:, :], 0.5, None, ALU.mult),
                 reads=[R_gbc[1]], writes=[R_gbc[1]])
        else:
            P.dma("gfin", gfin_bc[:, :], gfin_d.partition_broadcast(128), writes=[R_gfin])

        load_group(groups[0])
        load_group(groups[1])
        fin_v = fin_d.rearrange("(kc p) n -> p kc n", p=128)
        for f in range(NF):
            load_cast(w1[:, :, f * 128:(f + 1) * 128], fin_v[:, :, f * 128:(f + 1) * 128], [128, 8, 128], R_w1g[f])
            load_cast(w1[:, :, DFF + f * 128:DFF + (f + 1) * 128], fin_v[:, :, DFF + f * 128:DFF + (f + 1) * 128],
                      [128, 8, 128], R_w1u[f])
            load_cast(w2[:, f, :], fout_d[f * 128:(f + 1) * 128, :], [128, D], R_w2[f])

        for gi_, g in enumerate(groups):
            kind = 1 if (which == 1 and g == NG_LAT) else 0
            slots = [(g % 2) * 2, (g % 2) * 2 + 1]
            prep(2, slots, kind, s_idx, hT)

            def GU(f):
                b = 4 + (f % 2)
                for kc in range(8):
                    P.op("pe", lambda e, b=b, kc=kc, f=f: e.matmul(PS[b][:, 0:G], w1[:, kc, f * 128:(f + 1) * 128], hT[:, kc, :],
                                                                   start=(kc == 0), stop=(kc == 7)),
                         reads=[R_w1g[f], R_hT], writes=[PSR[b]])
                for kc in range(8):
                    P.op("pe", lambda e, b=b, kc=kc, f=f: e.matmul(PS[b][:, G:2 * G], w1[:, kc, DFF + f * 128:DFF + (f + 1) * 128],
                                                                   hT[:, kc, :], start=(kc == 0), stop=(kc == 7)),
                         reads=[R_w1u[f], R_hT], writes=[PSR[b]])
                P.op("act", lambda e, b=b, f=f: e.activation(sg[f % 2][:, :], PS[b][:, 0:G], AF.Silu),
                     reads=[PSR[b]], writes=[R_sg[f % 2]])
                P.op("dve", lambda e, b=b, f=f: e.tensor_tensor(aT[f % 2][:, :], PS[b][:, G:2 * G], sg[f % 2][:, :], ALU.mult),
                     reads=[PSR[b], R_sg[f % 2]], writes=[R_aT[f % 2]])

            def OUT(f):
                for tt in range(2):
                    for dh in range(2):
                        b = tt * 2 + dh
                        P.op("pe", lambda e, b=b, tt=tt, dh=dh, f=f: e.matmul(PS[b][:, :], aT[f % 2][:, tt * 128:(tt + 1) * 128],
                                                                              w2[:, f, dh * 512:(dh + 1) * 512],
                                                                              start=(f == 0), stop=(f == NF - 1)),
                             reads=[R_aT[f % 2], R_w2[f]], writes=[PSR[b]])

            GU(0)
            for f in range(NF):
                if f + 1 < NF:
                    GU(f + 1)
                OUT(f)
            gb = gate_bc[kind]
            for tt in range(2):
                sl = slots[tt]
                for dh in range(2):
                    b = tt * 2 + dh
                    tb = tmpf[dh]
                    P.op("dve", lambda e, b=b, dh=dh, gb=gb, tb=tb: e.tensor_tensor(tb[:, :], PS[b][:, :], gb[:, dh * 512:(dh + 1) * 512],
                                                                                     ALU.mult),
                         reads=[PSR[b], R_gbc[kind]], writes=[R_tmpf[dh]])
                    P.op("dve", lambda e, sl=sl, dh=dh, tb=tb: e.tensor_tensor(xt[sl][:, dh * 512:(dh + 1) * 512], tb[:, :],
                                                                              xt[sl][:, dh * 512:(dh + 1) * 512], ALU.add),
                         reads=[R_tmpf[dh], R_xt[sl]], writes=[R_xt[sl]])
                if which == 1:
                    r0 = g * G + tt * 128
                    P.dma("xt%d" % sl, x1s_d[r0:r0 + 128, :], xt[sl][:, :], reads=[R_xt[sl]], writes=[R_x1s[g * 2 + tt]])
                else:
                    c0 = 10 + 3 * tt
                    P.op("dve", lambda e, sl=sl, c0=c0: e.scalar_tensor_tensor(junk[:, :], xt[sl][:, :], 1.0, xt[sl][:, :], ALU.mult, ALU.mult,
                                                                              accum_out=stat[:, c0:c0 + 1]),
                         reads=[R_xt[sl]], writes=[R_junk, R_stat])
                    P.op("act", lambda e, c0=c0: e.activation(stat[:, c0 + 1:c0 + 2], stat[:, c0:c0 + 1], AF.Ln, bias=EPS, scale=1.0 / D),
                         reads=[R_stat], writes=[R_stat])
                    P.op("act", lambda e, c0=c0: e.activation(stat[:, c0 + 2:c0 + 3], stat[:, c0 + 1:c0 + 2], AF.Exp, scale=-0.5),
                         reads=[R_stat], writes=[R_stat])
                    P.op("dve", lambda e, sl=sl, c0=c0: e.scalar_tensor_tensor(xt[sl][:, :], xt[sl][:, :], stat[:, c0 + 2:c0 + 3],
                                                                              gfin_bc[:, :], ALU.mult, ALU.mult),
                         reads=[R_xt[sl], R_stat, R_gfin], writes=[R_xt[sl]])
                    r0 = g * G + tt * 128
                    P.dma("xt%d" % sl, out_d[r0:r0 + 128, :], xt[sl][:, :], reads=[R_xt[sl]])
            if gi_ + 2 < len(groups):
                load_group(groups[gi_ + 2])
        P.barrier()
        A.off = mark

    ffn_phase(1)

    if phases >= 2:
        p23_mark = A.off
        KT = A.alloc([128, 4, NTOK], BF16)
        Vt = A.alloc([128, NKT, 512], BF16)
        Ut = A.alloc([128, SEQ // 128, 512], BF16)
        R_KT = [Res("KT%d" % g) for g in range(NG_LAT + 1)]
        R_V = [Res("V%d" % i) for i in range(NKT)]
        R_U = [Res("U%d" % i) for i in range(SEQ // 128)]
        p2_mark = A.off
        wkvu = A.alloc([128, 8, 1536], BF16)
        rope_sb = [A.alloc([128, 2, G], F32) for _ in range(2)]
        kf = A.alloc([128, G], F32)
        kt1 = A.alloc([128, G], F32)
        kt2 = A.alloc([128, G], F32)
        R_wkvu = [Res("wkvu%d" % j) for j in range(12)]
        R_rope = [Res("rope0"), Res("rope1")]
        R_kf, R_kt1, R_kt2 = Res("kf"), Res("kt1"), Res("kt2")
        win_v = win_d.rearrange("(kc p) n -> p kc n", p=128)

        def load_x1_group(g):
            for tt in range(2):
                sl = (g % 2) * 2 + tt
                r0 = g * G + tt * 128
                P.dma("xt%d" % sl, xt[sl][:, :], x1s_d[r0:r0 + 128, :], reads=[R_x1s[g * 2 + tt]], writes=[R_xt[sl]])

        def load_rope(g):
            s = g % 2
            P.dma("rope%d" % s, rope_sb[s][:, 0, :], cos_d[:, g * G:(g + 1) * G], writes=[R_rope[s]])
            P.dma("rope%d" % s, rope_sb[s][:, 1, :], sin_d[:, g * G:(g + 1) * G], reads=[], writes=[R_rope[s]])

        load_x1_group(0)
        load_x1_group(1)
        load_rope(0)
        for j in range(12):
            load_cast(wkvu[:, :, j * 128:(j + 1) * 128], win_v[:, :, 512 + j * 128:512 + (j + 1) * 128], [128, 8, 128], R_wkvu[j])

        def rope_apply(src_ps, src_res, s, dst_writer):
            P.op("dve", lambda e: e.tensor_copy(kf[:, :], src_ps), reads=[src_res], writes=[R_kf])
            P.op("pe", lambda e: e.matmul(PS[2][:, 0:G], ropeP_f[:, :], kf[:, :], start=True, stop=True),
                 reads=[R_ropeP, R_kf], writes=[PSR[2]])
            P.op("dve", lambda e: e.tensor_tensor(kt1[:, :], PS[2][:, 0:G], rope_sb[s][:, 1, :], ALU.mult),
                 reads=[PSR[2], R_rope[s]], writes=[R_kt1])
            P.op("dve", lambda e: e.tensor_tensor(kt2[:, :], kf[:, :], rope_sb[s][:, 0, :], ALU.mult),
                 reads=[R_kf, R_rope[s]], writes=[R_kt2])
            dst_writer()

        for g in range(NG_LAT + 1):
            is_ctx = g == NG_LAT
            kind = 1 if is_ctx else 0
            slots = [(g % 2) * 2, (g % 2) * 2 + 1]
            if g + 1 < NG_LAT:
                load_rope(g + 1)
            prep(2, slots, kind, 1, hT)
            for h in range(4):
                b = h % 2
                for kc in range(8):
                    P.op("pe", lambda e, b=b, kc=kc, h=h: e.matmul(PS[b][:, 0:G], wkvu[:, kc, h * 128:(h + 1) * 128], hT[:, kc, :],
                                                                   start=(kc == 0), stop=(kc == 7)),
                         reads=[R_wkvu[h], R_hT], writes=[PSR[b]])
                dst = KT[:, h, g * G:(g + 1) * G]
                if is_ctx:
                    P.op("dve", lambda e, b=b, dst=dst: e.tensor_copy(dst, PS[b][:, 0:G]), reads=[PSR[b]], writes=[R_KT[g]])
                else:
                    def wr(dst=dst, g=g):
                        P.op("dve", lambda e: e.tensor_tensor(dst, kt1[:, :], kt2[:, :], ALU.add),
                             reads=[R_kt1, R_kt2], writes=[R_KT[g]])
                    rope_apply(PS[b][:, 0:G], PSR[b], g % 2, wr)
            for tt in range(2):
                ti = g * 2 + tt
                for kc in range(8):
                    P.op("pe", lambda e, kc=kc, tt=tt: e.matmul(PS[4][:, :], hT[:, kc, tt * 128:(tt + 1) * 128], wkvu[:, kc, 512:1024],
                                                                start=(kc == 0), stop=(kc == 7)),
                         reads=[R_hT] + R_wkvu[4:8], writes=[PSR[4]])
                P.op("dve", lambda e, ti=ti: e.tensor_copy(Vt[:, ti, :], PS[4][:, :]), reads=[PSR[4]], writes=[R_V[ti]])
                if not is_ctx:
                    for kc in range(8):
                        P.op("pe", lambda e, kc=kc, tt=tt: e.matmul(PS[5][:, :], hT[:, kc, tt * 128:(tt + 1) * 128], wkvu[:, kc, 1024:1536],
                                                                    start=(kc == 0), stop=(kc == 7)),
                             reads=[R_hT] + R_wkvu[8:12], writes=[PSR[5]])
                    P.op("dve", lambda e, ti=ti: e.tensor_copy(Ut[:, ti, :], PS[5][:, :]), reads=[PSR[5]], writes=[R_U[ti]])
            if g + 2 < NG_LAT + 1:
                load_x1_group(g + 2)
        P.barrier()
        A.off = p2_mark

    if phases >= 3:
        wq = A.alloc([128, 8, 512], BF16)
        wo = A.alloc([128, 8, D], BF16)
        wpl = A.alloc([128, 4, 128], BF16)
        band = A.alloc([128, 20, 128], BF16)
        gt2_bc = A.alloc([128, D], F32)
        Qpad = A.alloc([128, 4, 2 * G], BF16)
        Et = [A.alloc([128, 2 * G], BF16) for _ in range(2)]
        Rz = A.alloc([128, 2 * G], F32)
        o0 = A.alloc([128, G], F32)
        o1 = A.alloc([128, G], F32)
        osq = A.alloc([128, G], F32)
        rr = A.alloc([128, G], F32)
        attT = A.alloc([128, 8, G], BF16)
        dT = A.alloc([128, G], BF16)
        rope3_sb = [A.alloc([128, 2, G], F32) for _ in range(2)]
        qf = A.alloc([128, G], F32)
        q_t1 = A.alloc([128, G], F32)
        q_t2 = A.alloc([128, G], F32)
        R_wq = [Res("wq%d" % j) for j in range(4)]
        R_wo = [Res("wo%d" % j) for j in range(8)]
        R_wpl, R_band, R_gt2, R_Qpad = Res("wpl"), Res("band"), Res("gt2"), [Res("Qpad%d" % h) for h in range(4)]
        R_E = [Res("E0"), Res("E1")]
        R_Rz, R_o0, R_o1, R_osq, R_rr = Res("Rz"), Res("o0"), Res("o1"), Res("osq"), Res("rr")
        R_attT = [Res("attT%d" % c) for c in range(8)]
        R_dT = Res("dT")
        R_rope3 = [Res("rope0"), Res("rope1")]
        R_qf, R_kt1, R_kt2 = Res("qf"), Res("q_t1"), Res("q_t2")
        win_v = win_d.rearrange("(kc p) n -> p kc n", p=128)

        def load_x1_group3(g):
            for tt in range(2):
                sl = (g % 2) * 2 + tt
                r0 = g * G + tt * 128
                P.dma("xt%d" % sl, xt[sl][:, :], x1s_d[r0:r0 + 128, :], reads=[R_x1s[g * 2 + tt]], writes=[R_xt[sl]])

        def load_rope3(g):
            s = g % 2
            P.dma("rope%d" % s, rope3_sb[s][:, 0, :], cos_d[:, g * G:(g + 1) * G], writes=[R_rope3[s]])
            P.dma("rope%d" % s, rope3_sb[s][:, 1, :], sin_d[:, g * G:(g + 1) * G], writes=[R_rope3[s]])

        load_x1_group3(0)
        load_x1_group3(1)
        load_rope3(0)
        P.dma("gt2", gt2_bc[:, :], mscr_d[0, 5 * D:6 * D].partition_broadcast(128), reads=[R_mscr], writes=[R_gt2])
        for j in range(4):
            load_cast(wq[:, :, j * 128:(j + 1) * 128], win_v[:, :, j * 128:(j + 1) * 128], [128, 8, 128], R_wq[j])
        for c in range(8):
            load_cast(wo[:, c, :], wout_d[c * 128:(c + 1) * 128, :], [128, D], R_wo[c])
        load_cast(wpl[:, :, :], wpool_d.rearrange("g i o -> i g o"), [128, 4, 128], R_wpl)
        band_v = band_d.rearrange("p (b t) -> p b t", t=128)
        for j in range(3):
            nb = 8 if j < 2 else 4
            load_cast(band[:, j * 8:j * 8 + nb, :], band_v[:, j * 8:j * 8 + nb, :], [128, nb, 128], R_band)
        for h in range(4):
            P.op("pool", lambda e, h=h: e.memset(Qpad[:, h, :], 0.0), writes=[R_Qpad[h]])

        for g in range(NG_LAT):
            slots = [(g % 2) * 2, (g % 2) * 2 + 1]
            if g + 1 < NG_LAT:
                load_rope3(g + 1)
            prep(2, slots, 0, 1, hT)
            for h in range(4):
                for kc in range(8):
                    P.op("pe", lambda e, kc=kc, h=h: e.matmul(PS[5][:, 0:G], wq[:, kc, h * 128:(h + 1) * 128], hT[:, kc, :],
                                                              start=(kc == 0), stop=(kc == 7)),
                         reads=[R_wq[h], R_hT], writes=[PSR[5]])
                s = g % 2
                P.op("dve", lambda e: e.tensor_copy(qf[:, :], PS[5][:, 0:G]), reads=[PSR[5]], writes=[R_qf])
                P.op("pe", lambda e: e.matmul(PS[6][:, 0:G], ropeP_f[:, :], qf[:, :], start=True, stop=True),
                     reads=[R_ropeP, R_qf], writes=[PSR[6]])
                P.op("dve", lambda e, s=s: e.tensor_tensor(q_t1[:, :], PS[6][:, 0:G], rope3_sb[s][:, 1, :], ALU.mult),
                     reads=[PSR[6], R_rope3[s]], writes=[R_kt1])
                P.op("dve", lambda e, s=s: e.tensor_tensor(q_t2[:, :], qf[:, :], rope3_sb[s][:, 0, :], ALU.mult),
                     reads=[R_qf, R_rope3[s]], writes=[R_kt2])
                P.op("dve", lambda e, h=h: e.tensor_tensor(Qpad[0:64, h, 0:G], q_t1[0:64, :], q_t2[0:64, :], ALU.add),
                     reads=[R_kt1, R_kt2], writes=[R_Qpad[h]])
                P.op("dve", lambda e, h=h: e.tensor_tensor(Qpad[64:128, h, G:2 * G], q_t1[64:128, :], q_t2[64:128, :], ALU.add),
                     reads=[R_kt1, R_kt2], writes=[R_Qpad[h]])
            for h in range(4):
                def QK(kt, h=h):
                    b = kt % 2
                    kg = kt // 2
                    for m in range(2):
                        P.op("pe", lambda e, b=b, m=m, kt=kt: e.matmul(PS[b][:, m * G:(m + 1) * G], KT[:, h, kt * 128:(kt + 1) * 128],
                                                                       Qpad[:, h, m * G:(m + 1) * G], start=True, stop=True),
                             reads=[R_KT[kg], R_Qpad[h]], writes=[PSR[b]])
                    P.op("act", lambda e, b=b: e.activation(Et[b][:, :], PS[b][:, :], AF.Exp, scale=0.125),
                         reads=[PSR[b]], writes=[R_E[b]])

                def AVZ(kt, h=h):
                    b = kt % 2
                    for m in range(2):
                        P.op("pe", lambda e, b=b, m=m, kt=kt: e.matmul(PS[2 + m][:, 0:G], Vt[:, kt, h * 128:(h + 1) * 128],
                                                                       Et[b][:, m * G:(m + 1) * G], start=(kt == 0), stop=(kt == NKT - 1)),
                             reads=[R_V[kt], R_E[b]], writes=[PSR[2 + m]])
                    P.op("pe", lambda e, b=b, kt=kt: e.matmul(PS[4][:, :], ones_bf[:, :], Et[b][:, :],
                                                              start=(kt == 0), stop=(kt == NKT - 1)),
                         reads=[R_ones, R_E[b]], writes=[PSR[4]])

                QK(0)
                for kt in range(NKT):
                    if kt + 1 < NKT:
                        QK(kt + 1)
                    AVZ(kt)
                P.op("dve", lambda e: e.reciprocal(Rz[:, :], PS[4][:, :]), reads=[PSR[4]], writes=[R_Rz])
                P.op("dve", lambda e: e.tensor_tensor(o0[:, :], PS[2][:, 0:G], Rz[:, 0:G], ALU.mult),
                     reads=[PSR[2], R_Rz], writes=[R_o0])
                P.op("dve", lambda e: e.tensor_tensor(o1[:, :], PS[3][:, 0:G], Rz[:, G:2 * G], ALU.mult),
                     reads=[PSR[3], R_Rz], writes=[R_o1])
                P.op("dve", lambda e: e.scalar_tensor_tensor(o0[:, :], o1[:, :], neglam[:, 0:1], o0[:, :], ALU.mult, ALU.add),
                     reads=[R_o0, R_o1, R_neglam], writes=[R_o0])
                P.op("dve", lambda e: e.tensor_tensor(osq[:, :], o0[:, :], o0[:, :], ALU.mult), reads=[R_o0], writes=[R_osq])
                P.op("pe", lambda e: e.matmul(PS[5][:, 0:G], onesdiv_f[:, :], osq[:, :], start=True, stop=True),
                     reads=[R_onesdiv, R_osq], writes=[PSR[5]])
                P.op("act", lambda e: e.activation(rr[:, :], PS[5][:, 0:G], AF.Ln, bias=EPS), reads=[PSR[5]], writes=[R_rr])
                P.op("act", lambda e: e.activation(rr[:, :], rr[:, :], AF.Exp, scale=-0.5), reads=[R_rr], writes=[R_rr])
                P.op("dve", lambda e, h=h: e.scalar_tensor_tensor(attT[:, h, :], o0[:, :], gsubs[:, 0:1], rr[:, :], ALU.mult, ALU.mult),
                     reads=[R_o0, R_gsubs, R_rr], writes=[R_attT[h]])
            for gi in range(4):
                for tt in range(2):
                    ti = g * 2 + tt
                    rs = [r for r in (-1, 0, 1) if 0 <= ti + r < SEQ // 128]
                    for n_, r in enumerate(rs):
                        blk = r + 1
                        if r == 0 and ti == 0:
                            blk = 3
                        if r == 0 and ti == SEQ // 128 - 1:
                            blk = 4
                        P.op("pe", lambda e, gi=gi, tt=tt, ti=ti, r=r, blk=blk, n_=n_, last=(n_ == len(rs) - 1):
                             e.matmul(PS[6][:, tt * 128:(tt + 1) * 128], Ut[:, ti + r, gi * 128:(gi + 1) * 128],
                                      band[:, gi * 5 + blk, :], start=(n_ == 0), stop=last),
                             reads=[R_U[ti + r], R_band], writes=[PSR[6]])
                P.op("dve", lambda e: e.tensor_copy(dT[:, :], PS[6][:, 0:G]), reads=[PSR[6]], writes=[R_dT])
                P.op("pe", lambda e, gi=gi: e.matmul(PS[7][:, 0:G], wpl[:, gi, :], dT[:, :], start=True, stop=True),
                     reads=[R_wpl, R_dT], writes=[PSR[7]])
                P.op("dve", lambda e, gi=gi: e.tensor_scalar(attT[:, 4 + gi, :], PS[7][:, 0:G], psT[:, gi:gi + 1], None, ALU.mult),
                     reads=[PSR[7], R_psT], writes=[R_attT[4 + gi]])
            for tt in range(2):
                sl = slots[tt]
                for dh in range(2):
                    b = 5 + dh
                    for c in range(8):
                        P.op("pe", lambda e, b=b, c=c, tt=tt, dh=dh: e.matmul(PS[b][:, :], attT[:, c, tt * 128:(tt + 1) * 128],
                                                                              wo[:, c, dh * 512:(dh + 1) * 512],
                                                                              start=(c == 0), stop=(c == 7)),
                             reads=[R_attT[c], R_wo[c]], writes=[PSR[b]])
                    tb = tmpf[dh]
                    P.op("dve", lambda e, b=b, dh=dh, tb=tb: e.tensor_tensor(tb[:, :], PS[b][:, :], gt2_bc[:, dh * 512:(dh + 1) * 512], ALU.mult),
                         reads=[PSR[b], R_gt2], writes=[R_tmpf[dh]])
                    P.op("dve", lambda e, sl=sl, dh=dh, tb=tb: e.tensor_tensor(xt[sl][:, dh * 512:(dh + 1) * 512], tb[:, :],
                                                                              xt[sl][:, dh * 512:(dh + 1) * 512], ALU.add),
                         reads=[R_tmpf[dh], R_xt[sl]], writes=[R_xt[sl]])
                r0 = g * G + tt * 128
                P.dma("xt%d" % sl, x1s_d[r0:r0 + 128, :], xt[sl][:, :], reads=[R_xt[sl]], writes=[R_x1s[g * 2 + tt]])
            if g + 2 < NG_LAT:
                load_x1_group3(g + 2)
        P.barrier()
        A.off = p23_mark

    if phases >= 4:
        ffn_phase(2)

    P.emit()
    return nc, A.peak


def _consts():
    ident = np.eye(128, dtype=np.float32)
    Pm = np.zeros((128, 128), np.float32)
    for dst in range(128):
        if dst % 32 < 16:
            Pm[dst, dst + 16] = -1.0
        else:
            Pm[dst, dst - 16] = 1.0
    ropeP = np.ascontiguousarray(Pm.T)
    rows = SEQ // 64
    row = np.repeat(np.arange(rows, dtype=np.float32), 64)
    col = np.tile(np.arange(64, dtype=np.float32), rows)
    nf = 16
    freqs = (np.float32(10000.0) ** (-np.arange(nf, dtype=np.float32) / np.float32(nf))).astype(np.float32)
    ar = row[:, None] * freqs
    ac = col[:, None] * freqs
    ang = np.concatenate([ar, ar, ac, ac], axis=-1).astype(np.float32)
    cos = np.cos(ang).astype(np.float32).T
    sin = np.sin(ang).astype(np.float32).T
    cos128 = np.ascontiguousarray(np.concatenate([cos, cos], axis=0))
    sin128 = np.ascontiguousarray(np.concatenate([sin, sin], axis=0))
    n = SEQ
    band = np.zeros((128, 20, 128), np.float32)

    def mval(w, t, s):
        lo = w // 2
        hi = w - w // 2 - 1
        st = max(t - lo, 0)
        en = min(t + hi, n - 1)
        v = 0.0
        if st <= s <= en:
            v = 1.0 / (en - st + 1)
        if s == t:
            v -= 1.0
        return v

    for gi, w in enumerate(POOL_W):
        for blk, (ti, r) in enumerate([(5, -1), (5, 0), (5, 1), (0, 0), (n // 128 - 1, 0)]):
            for tp in range(128):
                t = ti * 128 + tp
                for s in range(max(0, t - 17), min(n, t + 17)):
                    sp = s - (ti + r) * 128
                    if 0 <= sp < 128:
                        band[sp, gi * 5 + blk, tp] = mval(w, t, s)
    return dict(c_ident=ident, c_ropeP=ropeP, c_cos=cos128, c_sin=sin128,
                c_band=np.ascontiguousarray(band.reshape(128, 20 * 128)))


_CACHE = {}


def _get_nc(debug=False, phases=4):
    key = (debug, phases)
    if key not in _CACHE:
        _CACHE[key] = build_nc(debug=debug, phases=phases)[0]
    return _CACHE[key]


def make_in_maps(inputs):
    f = lambda a: np.ascontiguousarray(np.asarray(a, dtype=np.float32))
    consts = _consts()
    shared = dict(
        w_mod=f(inputs["w_mod"][0]), b_mod=f(inputs["b_mod"][0]), g_ffn1=f(inputs["g_ffn1"][0]),
        ffn1_w_in=f(inputs["ffn1_w_in"][0]), ffn1_w_out=f(inputs["ffn1_w_out"][0]), g_mix=f(inputs["g_mix"][0]),
        w_in=f(inputs["w_in"][0]),
        lam4=f(np.concatenate([np.asarray(inputs["lambda_q1"][0]), np.asarray(inputs["lambda_k1"][0]),
                               np.asarray(inputs["lambda_q2"][0]), np.asarray(inputs["lambda_k2"][0])])),
        g_sub=f(inputs["g_sub"][0]), w_pool=f(inputs["w_pool"][0]), pool_scale=f(inputs["pool_scale"][0]),
        w_out=f(inputs["w_out"][0]), g_ffn2=f(inputs["g_ffn2"][0]), ffn2_w_in=f(inputs["ffn2_w_in"][0]),
        ffn2_w_out=f(inputs["ffn2_w_out"][0]), g_final=f(inputs["g_final"]),
    )
    shared.update(consts)
    x = np.asarray(inputs["x"], dtype=np.float32)
    c = np.asarray(inputs["c"], dtype=np.float32)
    ctx = np.asarray(inputs["ctx"], dtype=np.float32)
    c_ctx = np.asarray(inputs["c_ctx"], dtype=np.float32)
    maps = []
    for b in range(8):
        m = dict(shared)
        m["x"] = np.ascontiguousarray(x[b])
        m["ctx"] = np.ascontiguousarray(ctx[b])
        m["cvec"] = np.ascontiguousarray(np.stack([c[b], c_ctx], axis=0))
        maps.append(m)
    return maps


def kernel(**inputs):
    nc = _get_nc()
    in_maps = make_in_maps(inputs)
    res = run_bass_kernel_spmd(nc, in_maps, core_ids=list(range(8)))
    return np.stack([np.asarray(r["out"], dtype=np.float32) for r in res.results], axis=0)
```

```python
import math
from contextlib import ExitStack
import numpy as np
import concourse.bass as bass
import concourse.mybir as mybir
from concourse.bass_utils import run_bass_kernel_spmd

F32 = mybir.dt.float32
BF16 = mybir.dt.bfloat16
ALU = mybir.AluOpType
AF = mybir.ActivationFunctionType

D = 1024
SEQ = 4096
CTX = 256
NTOK = SEQ + CTX
DFF = 2816
NF = DFF // 128
EPS = 1e-6
LAM_INIT = 0.8 - 0.6 * math.exp(-0.3 * 0)
POOL_W = (2, 4, 8, 16)
G = 256
NG_LAT = SEQ // G
NKT = NTOK // 128

QATTR = {"pe": "tensor", "act": "scalar", "dve": "vector", "pool": "gpsimd", "sp": "sync"}
COMPUTE = ("pe", "act", "dve", "pool")


class Res:
    __slots__ = ("name", "lw", "rs")

    def __init__(self, name):
        self.name = name
        self.lw = None
        self.rs = []


class Op:
    __slots__ = ("q", "stream", "fn", "deps", "signal", "ms", "is_dma")


class Prog:
    def __init__(self, nc):
        self.nc = nc
        self.queues = {q: [] for q in QATTR}
        self.dma_count = {}
        self.last_dma = {}

    def _track(self, op, reads, writes):
        deps = []
        for r in reads:
            if r.lw is not None:
                deps.append(r.lw)
        for w in writes:
            if w.lw is not None:
                deps.append(w.lw)
            deps.extend(w.rs)
        out = []
        seen = set()
        for d in deps:
            if id(d) in seen or d is op:
                continue
            seen.add(id(d))
            if (not op.is_dma) and (not d.is_dma) and d.q == op.q and op.q == "pe":
                continue
            d.signal = True
            out.append(d)
        op.deps = out
        for r in reads:
            r.rs.append(op)
        for w in writes:
            w.lw = op
            w.rs = []

    def op(self, q, fn, reads=(), writes=()):
        o = Op()
        o.q = q
        o.stream = q
        o.fn = fn
        o.signal = False
        o.ms = None
        o.is_dma = False
        self._track(o, reads, writes)
        self.queues[q].append(o)
        return o

    def dma(self, stream, out, in_, reads=(), writes=(), q="sp", **kw):
        o = Op()
        o.q = q
        o.stream = "d_" + stream
        o.fn = lambda eng, out=out, in_=in_, kw=kw: eng.dma_start(out=out, in_=in_, **kw)
        o.signal = True
        o.is_dma = True
        n = self.dma_count.get(o.stream, 0) + 1
        self.dma_count[o.stream] = n
        o.ms = 16 * n
        self._track(o, reads, writes)
        self.queues[q].append(o)
        self.last_dma[o.stream] = o
        return o

    def barrier(self):
        lasts = []
        for q in COMPUTE:
            for o in reversed(self.queues[q]):
                if not o.is_dma and o.fn is not None:
                    lasts.append(o)
                    break
        lasts.extend(self.last_dma.values())
        for q in QATTR:
            o = Op()
            o.q = q
            o.stream = q
            o.fn = None
            o.signal = False
            o.ms = None
            o.is_dma = False
            o.deps = []
            for d in lasts:
                if d.q == q and not d.is_dma:
                    continue
                d.signal = True
                o.deps.append(d)
            self.queues[q].append(o)

    def emit(self):
        nc = self.nc
        for q in COMPUTE:
            c = 0
            for o in self.queues[q]:
                if o.is_dma or o.fn is None:
                    continue
                if o.signal:
                    c += 1
                    o.ms = c
        with ExitStack() as es:
            sems = {}
            for q in COMPUTE:
                sems[q] = es.enter_context(nc.semaphore("s_" + q))
            for s in self.dma_count:
                sems[s] = es.enter_context(nc.semaphore("s_" + s))
            block = es.enter_context(nc.Block())
            for q, attr in QATTR.items():
                ops = self.queues[q]
                final_waits = []
                if q == "sp":
                    final_waits = [(s, 16 * n) for s, n in self.dma_count.items()]

                def section(eng, ops=ops, final_waits=final_waits):
                    seen = {}
                    for o in ops:
                        for d in o.deps:
                            if seen.get(d.stream, 0) >= d.ms:
                                continue
                            seen[d.stream] = d.ms
                            eng.wait_ge(sems[d.stream], d.ms)
                        if o.fn is None:
                            continue
                        ins = o.fn(eng)
                        if o.is_dma:
                            ins.then_inc(sems[o.stream], 16)
                        elif o.signal:
                            ins.then_inc(sems[o.stream], 1)
                    for s, v in final_waits:
                        eng.wait_ge(sems[s], v)

                getattr(block, attr)(section)


class Arena:
    def __init__(self, nc, nbytes):
        self.t = nc.alloc_sbuf_tensor("arena", [128, nbytes // 4], F32)
        self.cap = nbytes
        self.off = 0
        self.peak = 0

    def alloc(self, shape, dt):
        ne = int(np.prod(shape[1:]))
        n = ne * (4 if dt == F32 else 2)
        n = (n + 63) // 64 * 64
        assert self.off + n <= self.cap, ("SBUF arena overflow", self.off, n, self.cap)
        v = self.t[0:shape[0], self.off // 4:(self.off + n) // 4]
        if dt != F32:
            v = v.bitcast(dt)
        v = v[:, 0:ne]
        if len(shape) == 3:
            v = v.rearrange("p (a b) -> p a b", a=shape[1])
        elif len(shape) == 4:
            v = v.rearrange("p (a b c) -> p a b c", a=shape[1], b=shape[2])
        self.off += n
        self.peak = max(self.peak, self.off)
        return v


def build_nc(debug=False, phases=4):
    nc = bass.Bass("TRN2", target_bir_lowering=False)

    def din(name, shape):
        return nc.dram_tensor(name, list(shape), F32, kind="ExternalInput").ap()

    x_d = din("x", [SEQ, D])
    ctx_d = din("ctx", [CTX, D])
    cvec_d = din("cvec", [2, D])
    wmod_d = din("w_mod", [D, 9 * D])
    bmod_d = din("b_mod", [9 * D])
    g1_d = din("g_ffn1", [D])
    f1in_d = din("ffn1_w_in", [D, 2 * DFF])
    f1out_d = din("ffn1_w_out", [DFF, D])
    gmix_d = din("g_mix", [D])
    win_d = din("w_in", [D, 2048])
    lam_d = din("lam4", [4 * 64])
    gsub_d = din("g_sub", [128])
    wpool_d = din("w_pool", [4, 128, 128])
    pscale_d = din("pool_scale", [512])
    wout_d = din("w_out", [D, D])
    g2_d = din("g_ffn2", [D])
    f2in_d = din("ffn2_w_in", [D, 2 * DFF])
    f2out_d = din("ffn2_w_out", [DFF, D])
    gfin_d = din("g_final", [D])
    ident_d = din("c_ident", [128, 128])
    ropeP_d = din("c_ropeP", [128, 128])
    cos_d = din("c_cos", [128, SEQ])
    sin_d = din("c_sin", [128, SEQ])
    band_d = din("c_band", [128, 20 * 128])
    out_d = nc.dram_tensor("out", [SEQ, D], F32, kind="ExternalOutput").ap()
    x1s_d = nc.dram_tensor("x1s", [NTOK, D], F32,
                           kind="ExternalOutput" if debug else "Internal").ap()
    mscr_d = nc.dram_tensor("mscr", [2, 9 * D], F32,
                            kind="ExternalOutput" if debug else "Internal").ap()

    P = Prog(nc)
    cap = nc.sbuf_bytes_remaining - 256
    cap = cap // 64 * 64
    A = Arena(nc, cap)
    PS = [nc.alloc_psum_tensor("ps%d" % i, [128, 512], F32) for i in range(8)]
    PSR = [Res("ps%d" % i) for i in range(8)]
    PSB = [PS[i][:, :].bitcast(BF16) for i in range(8)]

    ident_bf = A.alloc([128, 128], BF16)
    ones_bf = A.alloc([128, 128], BF16)
    onesdiv_f = A.alloc([128, 128], F32)
    ropeP_f = A.alloc([128, 128], F32)
    modT = A.alloc([128, 2, 9, 8], F32)
    gT = A.alloc([128, 3, 8], F32)
    psT = A.alloc([128, 4], F32)
    gsubs = A.alloc([128, 1], F32)
    neglam = A.alloc([128, 1], F32)
    ABt = A.alloc([128, 12, 8], F32)
    stat = A.alloc([128, 16], F32)
    xt = [A.alloc([128, D], F32) for _ in range(4)]
    xn = [A.alloc([128, D], BF16) for _ in range(2)]
    junk = A.alloc([128, D], BF16)
    hT = A.alloc([128, 8, G], BF16)
    stage = [A.alloc([128, D], F32) for _ in range(4)]
    tmpf = [A.alloc([128, 512], F32) for _ in range(2)]
    R_ident, R_ones, R_onesdiv, R_ropeP = Res("ident"), Res("ones"), Res("onesdiv"), Res("ropeP")
    R_modT, R_gT, R_psT, R_gsubs, R_neglam, R_AB, R_stat = (Res("modT"), Res("gT"), Res("psT"),
                                                           Res("gsubs"), Res("neglam"), Res("AB"), Res("stat"))
    R_xt = [Res("xt%d" % i) for i in range(4)]
    R_xn = [Res("xn%d" % i) for i in range(2)]
    R_junk = Res("junk")
    R_hT = Res("hT")
    R_stage = [Res("stage%d" % i) for i in range(4)]
    R_tmpf = [Res("tmpf%d" % i) for i in range(2)]
    R_x1s = [Res("x1s%d" % i) for i in range(NKT)]
    R_mscr = Res("mscr")
    persist_off = A.off

    stage_ctr = [0]

    def load_cast(dst_ap, src_ap, shape3, R_dst):
        s = stage_ctr[0] % 4
        stage_ctr[0] += 1
        ne = int(np.prod(shape3[1:]))
        sv = stage[s][:, 0:ne]
        if len(shape3) == 3:
            sv = sv.rearrange("p (a b) -> p a b", a=shape3[1])
        P.dma("stage%d" % s, sv, src_ap, writes=[R_stage[s]])
        P.op("pool", lambda e, d=dst_ap, v=sv: e.tensor_copy(d, v), reads=[R_stage[s]], writes=[R_dst])

    P.dma("c_rp", ropeP_f[:, :], ropeP_d, writes=[R_ropeP])
    load_cast(ident_bf[:, :], ident_d, [128, 128], R_ident)
    P.op("pool", lambda e: e.memset(ones_bf[:, :], 1.0), writes=[R_ones])
    P.op("pool", lambda e: e.memset(onesdiv_f[:, :], 1.0 / 128), writes=[R_onesdiv])
    P.dma("c_g", gT[:, 0, :], g1_d.rearrange("(c p) -> p c", p=128), writes=[R_gT], allow_slow_non_contiguous=True)
    P.dma("c_g", gT[:, 1, :], gmix_d.rearrange("(c p) -> p c", p=128), writes=[R_gT], allow_slow_non_contiguous=True)
    P.dma("c_g", gT[:, 2, :], g2_d.rearrange("(c p) -> p c", p=128), writes=[R_gT], allow_slow_non_contiguous=True)
    P.dma("c_ps", psT[:, :], pscale_d.rearrange("(c p) -> p c", p=128), writes=[R_psT], allow_slow_non_contiguous=True)
    P.dma("c_gs", gsubs[:, :], gsub_d.rearrange("(p o) -> p o", o=1), writes=[R_gsubs])
    P.op("dve", lambda e: e.tensor_scalar(gsubs[:, :], gsubs[:, :], 1.0 - LAM_INIT, None, ALU.mult),
         reads=[R_gsubs], writes=[R_gsubs])

    lamt = tmpf[0][:, 0:256]
    P.dma("c_lam", lamt, lam_d.partition_broadcast(128), writes=[R_tmpf[0]])
    P.op("dve", lambda e: e.scalar_tensor_tensor(junk[:, 0:64], lamt[:, 0:64], 1.0, lamt[:, 64:128], ALU.mult, ALU.mult,
                                                 accum_out=stat[:, 0:1]), reads=[R_tmpf[0]], writes=[R_stat, R_junk])
    P.op("dve", lambda e: e.scalar_tensor_tensor(junk[:, 0:64], lamt[:, 128:192], 1.0, lamt[:, 192:256], ALU.mult, ALU.mult,
                                                 accum_out=stat[:, 1:2]), reads=[R_tmpf[0], R_junk, R_stat], writes=[R_stat, R_junk])
    P.op("act", lambda e: e.activation(stat[:, 2:4], stat[:, 0:2], AF.Exp), reads=[R_stat], writes=[R_stat])
    P.op("dve", lambda e: e.tensor_tensor(neglam[:, :], stat[:, 3:4], stat[:, 2:3], ALU.subtract),
         reads=[R_stat], writes=[R_neglam])
    P.op("dve", lambda e: e.tensor_scalar(neglam[:, :], neglam[:, :], -LAM_INIT, None, ALU.add),
         reads=[R_neglam], writes=[R_neglam])

    craw = tmpf[1][:, 16:32].rearrange("p (k c) -> p k c", k=2)
    P.dma("c_c", craw[:, 0, :], cvec_d[0, :].rearrange("(c p) -> p c", p=128), writes=[R_tmpf[1]], allow_slow_non_contiguous=True)
    P.dma("c_c2", craw[:, 1, :], cvec_d[1, :].rearrange("(c p) -> p c", p=128), writes=[R_tmpf[1]], allow_slow_non_contiguous=True)

    p0_mark = A.off
    scT = A.alloc([128, 8, 128], F32)
    R_scT = Res("scT")
    P.op("pool", lambda e: e.memset(scT[:, :, :], 0.0), writes=[R_scT])
    for k in range(2):
        P.op("act", lambda e, k=k: e.activation(scT[:, :, k], craw[:, k, :], AF.Silu), reads=[R_tmpf[1], R_scT], writes=[R_scT])
    bm2 = A.alloc([2, 9 * D], F32)
    msb = A.alloc([2, 3 * D], F32)
    wms = [A.alloc([128, 3 * D], F32) for _ in range(3)]
    R_bm2, R_msb = Res("bm2"), Res("msb")
    R_wms = [Res("wms%d" % i) for i in range(3)]
    for k in range(2):
        P.dma("c_bm%d" % k, bm2[k:k + 1, :], bmod_d.rearrange("(o n) -> o n", o=1), writes=[R_bm2])
    wm_ctr = 0
    for gq in range(3):
        for kc in range(8):
            s = wm_ctr % 3
            wm_ctr += 1
            P.dma("wms%d" % s, wms[s][:, :], wmod_d[kc * 128:(kc + 1) * 128, gq * 3 * D:(gq + 1) * 3 * D],
                  writes=[R_wms[s]])
            for j in range(6):
                P.op("pe", lambda e, j=j, s=s, kc=kc: e.matmul(PS[j][:, :], scT[:, kc, :], wms[s][:, j * 512:(j + 1) * 512],
                                                                 start=(kc == 0), stop=(kc == 7)),
                     reads=[R_scT, R_wms[s]], writes=[PSR[j]])
        for j in range(6):
            P.op("dve", lambda e, j=j, gq=gq: e.tensor_tensor(msb[:, j * 512:(j + 1) * 512], PS[j][0:2, :],
                                                             bm2[:, gq * 3 * D + j * 512: gq * 3 * D + (j + 1) * 512], ALU.add),
                 reads=[PSR[j], R_bm2], writes=[R_msb])
        P.dma("mscr_w", mscr_d[:, gq * 3 * D:(gq + 1) * 3 * D], msb[:, :], reads=[R_msb], writes=[R_mscr])
        for k in range(2):
            P.dma("modT%d" % k, modT[:, k, 3 * gq:3 * gq + 3, :],
                  mscr_d[k, gq * 3 * D:(gq + 1) * 3 * D].rearrange("(v c p) -> p v c", p=128, c=8),
                  reads=[R_mscr], writes=[R_modT], allow_slow_non_contiguous=True)
        for k in range(2):
            ia = (k * 3 + gq) * 2
            P.op("dve", lambda e, k=k, gq=gq, ia=ia: e.scalar_tensor_tensor(ABt[:, ia, :], modT[:, k, 3 * gq + 1, :], 1.0,
                                                                           gT[:, gq, :], ALU.add, ALU.mult),
                 reads=[R_modT, R_gT], writes=[R_AB])
            P.op("dve", lambda e, k=k, gq=gq, ia=ia: e.tensor_copy(ABt[:, ia + 1, :], modT[:, k, 3 * gq, :]),
                 reads=[R_modT], writes=[R_AB])
    P.barrier()
    A.off = p0_mark

    def prep_A(slots):
        for tt in range(2):
            sl = slots[tt]
            xs = xt[sl]
            c0 = 4 + 3 * tt
            P.op("dve", lambda e, xs=xs, c0=c0: e.scalar_tensor_tensor(junk[:, :], xs[:, :], 1.0, xs[:, :], ALU.mult, ALU.mult,
                                                                      accum_out=stat[:, c0:c0 + 1]),
                 reads=[R_xt[sl]], writes=[R_junk, R_stat])
            P.op("act", lambda e, c0=c0: e.activation(stat[:, c0 + 1:c0 + 2], stat[:, c0:c0 + 1], AF.Ln, bias=EPS, scale=1.0 / D),
                 reads=[R_stat], writes=[R_stat])
            P.op("act", lambda e, c0=c0: e.activation(stat[:, c0 + 2:c0 + 3], stat[:, c0 + 1:c0 + 2], AF.Exp, scale=-0.5),
                 reads=[R_stat], writes=[R_stat])
            P.op("dve", lambda e, xs=xs, c0=c0, tt=tt: e.tensor_scalar(xn[tt][:, :], xs[:, :], stat[:, c0 + 2:c0 + 3], None, ALU.mult),
                 reads=[R_xt[sl], R_stat], writes=[R_xn[tt]])

    def prep_B(kind, s, hTv, R_hTv, banks=(6, 7)):
        ia = (kind * 3 + s) * 2
        for tt in range(2):
            bk = banks[tt]
            for c in range(8):
                P.op("pe", lambda e, c=c, tt=tt, bk=bk: e.transpose(PSB[bk][:, c * 128:(c + 1) * 128], xn[tt][:, c * 128:(c + 1) * 128],
                                                                    ident_bf[:, :]),
                     reads=[R_xn[tt], R_ident], writes=[PSR[bk]])
            for c in range(8):
                P.op("dve", lambda e, c=c, tt=tt, ia=ia, bk=bk, hTv=hTv: e.tensor_scalar(hTv[:, c, tt * 128:(tt + 1) * 128],
                                                                                       PSB[bk][:, c * 128:(c + 1) * 128],
                                                                                       ABt[:, ia, c:c + 1], ABt[:, ia + 1, c:c + 1],
                                                                                       ALU.mult, ALU.add),
                     reads=[PSR[bk], R_AB], writes=[R_hTv])

    def prep(ntile, slots, kind, s, hTv, R_hTv=None, banks=(6, 7)):
        prep_A(slots)
        prep_B(kind, s, hTv, R_hT if R_hTv is None else R_hTv, banks)

    def tok_rows(g, tt):
        r0 = g * G + tt * 128
        return r0

    def ffn_phase(which):
        fin_d, fout_d = (f1in_d, f1out_d) if which == 1 else (f2in_d, f2out_d)
        s_idx = 0 if which == 1 else 2
        groups = list(range(NG_LAT + 1)) if which == 1 else list(range(NG_LAT))
        mark = A.off
        w1 = A.alloc([128, 8, 2 * DFF], BF16)
        w2 = A.alloc([128, NF, D], BF16)
        gate_bc = [A.alloc([128, D], F32) for _ in range(2 if which == 1 else 1)]
        gfin_bc = A.alloc([128, D], F32) if which == 2 else None
        sg = [A.alloc([128, G], F32) for _ in range(2)]
        aT = [A.alloc([128, G], BF16) for _ in range(2)]
        hT2 = A.alloc([128, 8, G], BF16)
        hTs = [hT, hT2]
        R_hTs = [R_hT, Res("hT2")]
        R_w1g = [Res("w1g%d" % f) for f in range(NF)]
        R_w1u = [Res("w1u%d" % f) for f in range(NF)]
        R_w2 = [Res("w2_%d" % f) for f in range(NF)]
        R_gbc = [Res("gbc0"), Res("gbc1")]
        R_gfin = Res("gfin")
        R_sg = [Res("sg0"), Res("sg1")]
        R_aT = [Res("aT0"), Res("aT1")]

        def src_rows(g, tt):
            if which == 1:
                if g < NG_LAT:
                    return x_d[g * G + tt * 128: g * G + (tt + 1) * 128, :]
                return ctx_d[tt * 128:(tt + 1) * 128, :]
            r0 = g * G + tt * 128
            return x1s_d[r0:r0 + 128, :]

        def load_group(g):
            for tt in range(2):
                sl = (g % 2) * 2 + tt
                rds = [R_x1s[g * 2 + tt]] if which == 2 else []
                P.dma("xt%d" % sl, xt[sl][:, :], src_rows(g, tt), reads=rds, writes=[R_xt[sl]])

        v_gate = 3 * s_idx + 2
        P.dma("gbc0", gate_bc[0][:, :], mscr_d[0, v_gate * D:(v_gate + 1) * D].partition_broadcast(128),
              reads=[R_mscr], writes=[R_gbc[0]])
        P.op("dve", lambda e: e.tensor_scalar(gate_bc[0][:, :], gate_bc[0][:, :], 0.5, None, ALU.mult),
             reads=[R_gbc[0]], writes=[R_gbc[0]])
        if which == 1:
            P.dma("gbc1", gate_bc[1][:, :], mscr_d[1, v_gate * D:(v_gate + 1) * D].partition_broadcast(128),
                  reads=[R_mscr], writes=[R_gbc[1]])
            P.op("dve", lambda e: e.tensor_scalar(gate_bc[1][:, :], gate_bc[1][:, :], 0.5, None, ALU.mult),
                 reads=[R_gbc[1]], writes=[R_gbc[1]])
        else:
            P.dma("gfin", gfin_bc[:, :], gfin_d.partition_broadcast(128), writes=[R_gfin])

        load_group(groups[0])
        load_group(groups[1])
        fin_v = fin_d.rearrange("(kc p) n -> p kc n", p=128)
        for f in range(NF):
            load_cast(w1[:, :, f * 128:(f + 1) * 128], fin_v[:, :, f * 128:(f + 1) * 128], [128, 8, 128], R_w1g[f])
            load_cast(w1[:, :, DFF + f * 128:DFF + (f + 1) * 128], fin_v[:, :, DFF + f * 128:DFF + (f + 1) * 128],
                      [128, 8, 128], R_w1u[f])
            load_cast(w2[:, f, :], fout_d[f * 128:(f + 1) * 128, :], [128, D], R_w2[f])

        def kind_of(g):
            return 1 if (which == 1 and g == NG_LAT) else 0

        def slots_of(g):
            return [(g % 2) * 2, (g % 2) * 2 + 1]

        prep(2, slots_of(groups[0]), kind_of(groups[0]), s_idx, hTs[0], R_hTs[0])
        for gi_, g in enumerate(groups):
            kind = kind_of(g)
            slots = slots_of(g)
            hTc = hTs[gi_ % 2]
            R_hTc = R_hTs[gi_ % 2]
            nxt = groups[gi_ + 1] if gi_ + 1 < len(groups) else None

            def GU(f, hTc=hTc, R_hTc=R_hTc):
                b = 4 + (f % 2)
                for kc in range(8):
                    P.op("pe", lambda e, b=b, kc=kc, f=f: e.matmul(PS[b][:, 0:G], w1[:, kc, f * 128:(f + 1) * 128], hTc[:, kc, :],
                                                                   start=(kc == 0), stop=(kc == 7)),
                         reads=[R_w1g[f], R_hTc], writes=[PSR[b]])
                for kc in range(8):
                    P.op("pe", lambda e, b=b, kc=kc, f=f: e.matmul(PS[b][:, G:2 * G], w1[:, kc, DFF + f * 128:DFF + (f + 1) * 128],
                                                                   hTc[:, kc, :], start=(kc == 0), stop=(kc == 7)),
                         reads=[R_w1u[f], R_hTc], writes=[PSR[b]])
                P.op("act", lambda e, b=b, f=f: e.activation(sg[f % 2][:, :], PS[b][:, 0:G], AF.Silu),
                     reads=[PSR[b]], writes=[R_sg[f % 2]])
                P.op("dve", lambda e, b=b, f=f: e.tensor_tensor(aT[f % 2][:, :], PS[b][:, G:2 * G], sg[f % 2][:, :], ALU.mult),
                     reads=[PSR[b], R_sg[f % 2]], writes=[R_aT[f % 2]])

            def OUT(f):
                for tt in range(2):
                    for dh in range(2):
                        b = tt * 2 + dh
                        P.op("pe", lambda e, b=b, tt=tt, dh=dh, f=f: e.matmul(PS[b][:, :], aT[f % 2][:, tt * 128:(tt + 1) * 128],
                                                                              w2[:, f, dh * 512:(dh + 1) * 512],
                                                                              start=(f == 0), stop=(f == NF - 1)),
                             reads=[R_aT[f % 2], R_w2[f]], writes=[PSR[b]])

            GU(0)
            for f in range(NF):
                if f + 1 < NF:
                    GU(f + 1)
                OUT(f)
                if nxt is not None and f == 2:
                    prep_A(slots_of(nxt))
                if nxt is not None and f == 9:
                    prep_B(kind_of(nxt), s_idx, hTs[(gi_ + 1) % 2], R_hTs[(gi_ + 1) % 2])
            gb = gate_bc[kind]
            for tt in range(2):
                sl = slots[tt]
                for dh in range(2):
                    b = tt * 2 + dh
                    tb = tmpf[dh]
                    P.op("dve", lambda e, b=b, dh=dh, gb=gb, tb=tb: e.tensor_tensor(tb[:, :], PS[b][:, :], gb[:, dh * 512:(dh + 1) * 512],
                                                                                     ALU.mult),
                         reads=[PSR[b], R_gbc[kind]], writes=[R_tmpf[dh]])
                    P.op("dve", lambda e, sl=sl, dh=dh, tb=tb: e.tensor_tensor(xt[sl][:, dh * 512:(dh + 1) * 512], tb[:, :],
                                                                              xt[sl][:, dh * 512:(dh + 1) * 512], ALU.add),
                         reads=[R_tmpf[dh], R_xt[sl]], writes=[R_xt[sl]])
                if which == 1:
                    r0 = g * G + tt * 128
                    P.dma("xt%d" % sl, x1s_d[r0:r0 + 128, :], xt[sl][:, :], reads=[R_xt[sl]], writes=[R_x1s[g * 2 + tt]])
                else:
                    c0 = 10 + 3 * tt
                    P.op("dve", lambda e, sl=sl, c0=c0: e.scalar_tensor_tensor(junk[:, :], xt[sl][:, :], 1.0, xt[sl][:, :], ALU.mult, ALU.mult,
                                                                              accum_out=stat[:, c0:c0 + 1]),
                         reads=[R_xt[sl]], writes=[R_junk, R_stat])
                    P.op("act", lambda e, c0=c0: e.activation(stat[:, c0 + 1:c0 + 2], stat[:, c0:c0 + 1], AF.Ln, bias=EPS, scale=1.0 / D),
                         reads=[R_stat], writes=[R_stat])
                    P.op("act", lambda e, c0=c0: e.activation(stat[:, c0 + 2:c0 + 3], stat[:, c0 + 1:c0 + 2], AF.Exp, scale=-0.5),
                         reads=[R_stat], writes=[R_stat])
                    P.op("dve", lambda e, sl=sl, c0=c0: e.scalar_tensor_tensor(xt[sl][:, :], xt[sl][:, :], stat[:, c0 + 2:c0 + 3],
                                                                              gfin_bc[:, :], ALU.mult, ALU.mult),
                         reads=[R_xt[sl], R_stat, R_gfin], writes=[R_xt[sl]])
                    r0 = g * G + tt * 128
                    P.dma("xt%d" % sl, out_d[r0:r0 + 128, :], xt[sl][:, :], reads=[R_xt[sl]])
            if gi_ + 2 < len(groups):
                load_group(groups[gi_ + 2])
        P.barrier()
        A.off = mark

    ffn_phase(1)

    if phases >= 2:
        p23_mark = A.off
        KT = A.alloc([128, 4, NTOK], BF16)
        Vt = A.alloc([128, NKT, 512], BF16)
        Ut = A.alloc([128, SEQ // 128, 512], BF16)
        R_KT = [Res("KT%d" % g) for g in range(NG_LAT + 1)]
        R_V = [Res("V%d" % i) for i in range(NKT)]
        R_U = [Res("U%d" % i) for i in range(SEQ // 128)]
        p2_mark = A.off
        wkvu = A.alloc([128, 8, 1536], BF16)
        rope_sb = [A.alloc([128, 2, G], F32) for _ in range(2)]
        kf = A.alloc([128, G], F32)
        kt1 = A.alloc([128, G], F32)
        kt2 = A.alloc([128, G], F32)
        R_wkvu = [Res("wkvu%d" % j) for j in range(12)]
        R_rope = [Res("rope0"), Res("rope1")]
        R_kf, R_kt1, R_kt2 = Res("kf"), Res("kt1"), Res("kt2")
        win_v = win_d.rearrange("(kc p) n -> p kc n", p=128)

        def load_x1_group(g):
            for tt in range(2):
                sl = (g % 2) * 2 + tt
                r0 = g * G + tt * 128
                P.dma("xt%d" % sl, xt[sl][:, :], x1s_d[r0:r0 + 128, :], reads=[R_x1s[g * 2 + tt]], writes=[R_xt[sl]])

        def load_rope(g):
            s = g % 2
            P.dma("rope%d" % s, rope_sb[s][:, 0, :], cos_d[:, g * G:(g + 1) * G], writes=[R_rope[s]])
            P.dma("rope%d" % s, rope_sb[s][:, 1, :], sin_d[:, g * G:(g + 1) * G], reads=[], writes=[R_rope[s]])

        load_x1_group(0)
        load_x1_group(1)
        load_rope(0)
        for j in range(12):
            load_cast(wkvu[:, :, j * 128:(j + 1) * 128], win_v[:, :, 512 + j * 128:512 + (j + 1) * 128], [128, 8, 128], R_wkvu[j])

        def rope_apply(src_ps, src_res, s, dst_writer):
            P.op("dve", lambda e: e.tensor_copy(kf[:, :], src_ps), reads=[src_res], writes=[R_kf])
            P.op("pe", lambda e: e.matmul(PS[2][:, 0:G], ropeP_f[:, :], kf[:, :], start=True, stop=True),
                 reads=[R_ropeP, R_kf], writes=[PSR[2]])
            P.op("dve", lambda e: e.tensor_tensor(kt1[:, :], PS[2][:, 0:G], rope_sb[s][:, 1, :], ALU.mult),
                 reads=[PSR[2], R_rope[s]], writes=[R_kt1])
            P.op("dve", lambda e: e.tensor_tensor(kt2[:, :], kf[:, :], rope_sb[s][:, 0, :], ALU.mult),
                 reads=[R_kf, R_rope[s]], writes=[R_kt2])
            dst_writer()

        for g in range(NG_LAT + 1):
            is_ctx = g == NG_LAT
            kind = 1 if is_ctx else 0
            slots = [(g % 2) * 2, (g % 2) * 2 + 1]
            if g + 1 < NG_LAT:
                load_rope(g + 1)
            prep(2, slots, kind, 1, hT)
            for h in range(4):
                b = h % 2
                for kc in range(8):
                    P.op("pe", lambda e, b=b, kc=kc, h=h: e.matmul(PS[b][:, 0:G], wkvu[:, kc, h * 128:(h + 1) * 128], hT[:, kc, :],
                                                                   start=(kc == 0), stop=(kc == 7)),
                         reads=[R_wkvu[h], R_hT], writes=[PSR[b]])
                dst = KT[:, h, g * G:(g + 1) * G]
                if is_ctx:
                    P.op("dve", lambda e, b=b, dst=dst: e.tensor_copy(dst, PS[b][:, 0:G]), reads=[PSR[b]], writes=[R_KT[g]])
                else:
                    def wr(dst=dst, g=g):
                        P.op("dve", lambda e: e.tensor_tensor(dst, kt1[:, :], kt2[:, :], ALU.add),
                             reads=[R_kt1, R_kt2], writes=[R_KT[g]])
                    rope_apply(PS[b][:, 0:G], PSR[b], g % 2, wr)
            for tt in range(2):
                ti = g * 2 + tt
                for kc in range(8):
                    P.op("pe", lambda e, kc=kc, tt=tt: e.matmul(PS[4][:, :], hT[:, kc, tt * 128:(tt + 1) * 128], wkvu[:, kc, 512:1024],
                                                                start=(kc == 0), stop=(kc == 7)),
                         reads=[R_hT] + R_wkvu[4:8], writes=[PSR[4]])
                P.op("dve", lambda e, ti=ti: e.tensor_copy(Vt[:, ti, :], PS[4][:, :]), reads=[PSR[4]], writes=[R_V[ti]])
                if not is_ctx:
                    for kc in range(8):
                        P.op("pe", lambda e, kc=kc, tt=tt: e.matmul(PS[5][:, :], hT[:, kc, tt * 128:(tt + 1) * 128], wkvu[:, kc, 1024:1536],
                                                                    start=(kc == 0), stop=(kc == 7)),
                             reads=[R_hT] + R_wkvu[8:12], writes=[PSR[5]])
                    P.op("dve", lambda e, ti=ti: e.tensor_copy(Ut[:, ti, :], PS[5][:, :]), reads=[PSR[5]], writes=[R_U[ti]])
            if g + 2 < NG_LAT + 1:
                load_x1_group(g + 2)
        P.barrier()
        A.off = p2_mark

    if phases >= 3:
        wq = A.alloc([128, 8, 512], BF16)
        wo = A.alloc([128, 8, D], BF16)
        wpl = A.alloc([128, 4, 128], BF16)
        band = A.alloc([128, 20, 128], BF16)
        gt2_bc = A.alloc([128, D], F32)
        Qpad = A.alloc([128, 4, 2 * G], BF16)
        Et = [A.alloc([128, 2 * G], BF16) for _ in range(3)]
        Rz = A.alloc([128, 2 * G], F32)
        o0 = A.alloc([128, G], F32)
        o1 = A.alloc([128, G], F32)
        osq = A.alloc([128, G], F32)
        rr = A.alloc([128, G], F32)
        attT = A.alloc([128, 8, G], BF16)
        dT = A.alloc([128, G], BF16)
        rope3_sb = [A.alloc([128, 2, G], F32) for _ in range(2)]
        qf = A.alloc([128, G], F32)
        q_t1 = A.alloc([128, G], F32)
        q_t2 = A.alloc([128, G], F32)
        R_wq = [Res("wq%d" % j) for j in range(4)]
        R_wo = [Res("wo%d" % j) for j in range(8)]
        R_wpl, R_band, R_gt2, R_Qpad = Res("wpl"), Res("band"), Res("gt2"), [Res("Qpad%d" % h) for h in range(4)]
        R_E = [Res("E0"), Res("E1"), Res("E2")]
        R_Rz, R_o0, R_o1, R_osq, R_rr = Res("Rz"), Res("o0"), Res("o1"), Res("osq"), Res("rr")
        R_attT = [Res("attT%d" % c) for c in range(8)]
        R_dT = Res("dT")
        R_rope3 = [Res("rope0"), Res("rope1")]
        R_qf, R_qt1, R_qt2 = Res("qf"), Res("q_t1"), Res("q_t2")
        win_v3 = win_d.rearrange("(kc p) n -> p kc n", p=128)
        SBK = (0, 1, 7)
        NT = SEQ // 128

        def slots3(g):
            return [(g % 2) * 2, (g % 2) * 2 + 1]

        def load_x1_group3(g):
            for tt in range(2):
                sl = (g % 2) * 2 + tt
                r0 = g * G + tt * 128
                P.dma("xt%d" % sl, xt[sl][:, :], x1s_d[r0:r0 + 128, :], reads=[R_x1s[g * 2 + tt]], writes=[R_xt[sl]])

        def load_rope3(g):
            s = g % 2
            P.dma("rope%d" % s, rope3_sb[s][:, 0, :], cos_d[:, g * G:(g + 1) * G], writes=[R_rope3[s]])
            P.dma("rope%d" % s, rope3_sb[s][:, 1, :], sin_d[:, g * G:(g + 1) * G], writes=[R_rope3[s]])

        load_x1_group3(0)
        load_x1_group3(1)
        load_rope3(0)
        load_rope3(1)
        P.dma("gt2", gt2_bc[:, :], mscr_d[0, 5 * D:6 * D].partition_broadcast(128), reads=[R_mscr], writes=[R_gt2])
        for j in range(4):
            load_cast(wq[:, :, j * 128:(j + 1) * 128], win_v3[:, :, j * 128:(j + 1) * 128], [128, 8, 128], R_wq[j])
        for c in range(8):
            load_cast(wo[:, c, :], wout_d[c * 128:(c + 1) * 128, :], [128, D], R_wo[c])
        load_cast(wpl[:, :, :], wpool_d.rearrange("g i o -> i g o"), [128, 4, 128], R_wpl)
        band_v = band_d.rearrange("p (b t) -> p b t", t=128)
        for j in range(3):
            nb = 8 if j < 2 else 4
            load_cast(band[:, j * 8:j * 8 + nb, :], band_v[:, j * 8:j * 8 + nb, :], [128, nb, 128], R_band)
        for h in range(4):
            P.op("pool", lambda e, h=h: e.memset(Qpad[:, h, :], 0.0), writes=[R_Qpad[h]])

        def q_proj_1(g, h):
            for kc in range(8):
                P.op("pe", lambda e, kc=kc, h=h: e.matmul(PS[5][:, 0:G], wq[:, kc, h * 128:(h + 1) * 128], hT[:, kc, :],
                                                          start=(kc == 0), stop=(kc == 7)),
                     reads=[R_wq[h], R_hT], writes=[PSR[5]])
            P.op("dve", lambda e: e.tensor_copy(qf[:, :], PS[5][:, 0:G]), reads=[PSR[5]], writes=[R_qf])

        def q_proj_2(g, h):
            s = g % 2
            P.op("pe", lambda e: e.matmul(PS[6][:, 0:G], ropeP_f[:, :], qf[:, :], start=True, stop=True),
                 reads=[R_ropeP, R_qf], writes=[PSR[6]])
            P.op("dve", lambda e, s=s: e.tensor_tensor(q_t1[:, :], PS[6][:, 0:G], rope3_sb[s][:, 1, :], ALU.mult),
                 reads=[PSR[6], R_rope3[s]], writes=[R_qt1])
            P.op("dve", lambda e, s=s: e.tensor_tensor(q_t2[:, :], qf[:, :], rope3_sb[s][:, 0, :], ALU.mult),
                 reads=[R_qf, R_rope3[s]], writes=[R_qt2])
            P.op("dve", lambda e, h=h: e.tensor_tensor(Qpad[0:64, h, 0:G], q_t1[0:64, :], q_t2[0:64, :], ALU.add),
                 reads=[R_qt1, R_qt2], writes=[R_Qpad[h]])
            P.op("dve", lambda e, h=h: e.tensor_tensor(Qpad[64:128, h, G:2 * G], q_t1[64:128, :], q_t2[64:128, :], ALU.add),
                 reads=[R_qt1, R_qt2], writes=[R_Qpad[h]])

        def epi_A():
            P.op("dve", lambda e: e.reciprocal(Rz[:, :], PS[4][:, :]), reads=[PSR[4]], writes=[R_Rz])
            P.op("dve", lambda e: e.tensor_tensor(o0[:, :], PS[2][:, 0:G], Rz[:, 0:G], ALU.mult),
                 reads=[PSR[2], R_Rz], writes=[R_o0])
            P.op("dve", lambda e: e.tensor_tensor(o1[:, :], PS[3][:, 0:G], Rz[:, G:2 * G], ALU.mult),
                 reads=[PSR[3], R_Rz], writes=[R_o1])

        def epi_B1():
            P.op("dve", lambda e: e.scalar_tensor_tensor(o0[:, :], o1[:, :], neglam[:, 0:1], o0[:, :], ALU.mult, ALU.add),
                 reads=[R_o0, R_o1, R_neglam], writes=[R_o0])
            P.op("dve", lambda e: e.tensor_tensor(osq[:, :], o0[:, :], o0[:, :], ALU.mult), reads=[R_o0], writes=[R_osq])

        def epi_B2(h):
            P.op("pe", lambda e: e.matmul(PS[5][:, 0:G], onesdiv_f[:, :], osq[:, :], start=True, stop=True),
                 reads=[R_onesdiv, R_osq], writes=[PSR[5]])
            P.op("act", lambda e: e.activation(rr[:, :], PS[5][:, 0:G], AF.Ln, bias=EPS), reads=[PSR[5]], writes=[R_rr])
            P.op("act", lambda e: e.activation(rr[:, :], rr[:, :], AF.Exp, scale=-0.5), reads=[R_rr], writes=[R_rr])
            P.op("dve", lambda e, h=h: e.scalar_tensor_tensor(attT[:, h, :], o0[:, :], gsubs[:, 0:1], rr[:, :], ALU.mult, ALU.mult),
                 reads=[R_o0, R_gsubs, R_rr], writes=[R_attT[h]])

        def pool_1(g, gi):
            for tt in range(2):
                ti = g * 2 + tt
                rs = [r for r in (-1, 0, 1) if 0 <= ti + r < NT]
                for n_, r in enumerate(rs):
                    blk = r + 1
                    if r == 0 and ti == 0:
                        blk = 3
                    if r == 0 and ti == NT - 1:
                        blk = 4
                    P.op("pe", lambda e, gi=gi, tt=tt, ti=ti, r=r, blk=blk, n_=n_, last=(n_ == len(rs) - 1):
                         e.matmul(PS[6][:, tt * 128:(tt + 1) * 128], Ut[:, ti + r, gi * 128:(gi + 1) * 128],
                                  band[:, gi * 5 + blk, :], start=(n_ == 0), stop=last),
                         reads=[R_U[ti + r], R_band], writes=[PSR[6]])
            P.op("dve", lambda e: e.tensor_copy(dT[:, :], PS[6][:, 0:G]), reads=[PSR[6]], writes=[R_dT])

        def pool_2(g, gi):
            P.op("pe", lambda e, gi=gi: e.matmul(PS[5][:, 0:G], wpl[:, gi, :], dT[:, :], start=True, stop=True),
                 reads=[R_wpl, R_dT], writes=[PSR[5]])
            P.op("dve", lambda e, gi=gi: e.tensor_scalar(attT[:, 4 + gi, :], PS[5][:, 0:G], psT[:, gi:gi + 1], None, ALU.mult),
                 reads=[PSR[5], R_psT], writes=[R_attT[4 + gi]])

        def out_proj(g, tt):
            sl = slots3(g)[tt]
            for dh in range(2):
                b = 5 + dh
                for c in range(8):
                    P.op("pe", lambda e, b=b, c=c, tt=tt, dh=dh: e.matmul(PS[b][:, :], attT[:, c, tt * 128:(tt + 1) * 128],
                                                                          wo[:, c, dh * 512:(dh + 1) * 512],
                                                                          start=(c == 0), stop=(c == 7)),
                         reads=[R_attT[c], R_wo[c]], writes=[PSR[b]])
                tb = tmpf[dh]
                P.op("dve", lambda e, b=b, dh=dh, tb=tb: e.tensor_tensor(tb[:, :], PS[b][:, :], gt2_bc[:, dh * 512:(dh + 1) * 512], ALU.mult),
                     reads=[PSR[b], R_gt2], writes=[R_tmpf[dh]])
                P.op("dve", lambda e, sl=sl, dh=dh, tb=tb: e.tensor_tensor(xt[sl][:, dh * 512:(dh + 1) * 512], tb[:, :],
                                                                          xt[sl][:, dh * 512:(dh + 1) * 512], ALU.add),
                     reads=[R_tmpf[dh], R_xt[sl]], writes=[R_xt[sl]])
            r0 = g * G + tt * 128
            P.dma("xt%d" % sl, x1s_d[r0:r0 + 128, :], xt[sl][:, :], reads=[R_xt[sl]], writes=[R_x1s[g * 2 + tt]])
            if tt == 1 and g + 2 < NG_LAT:
                load_x1_group3(g + 2)
                load_rope3(g + 2)

        def head_loop(h, hooks):
            def QK(kt):
                b = SBK[kt % 3]
                kg = kt // 2
                for m in range(2):
                    P.op("pe", lambda e, b=b, m=m, kt=kt: e.matmul(PS[b][:, m * G:(m + 1) * G], KT[:, h, kt * 128:(kt + 1) * 128],
                                                                   Qpad[:, h, m * G:(m + 1) * G], start=True, stop=True),
                         reads=[R_KT[kg], R_Qpad[h]], writes=[PSR[b]])
                P.op("act", lambda e, b=b, kt=kt: e.activation(Et[kt % 3][:, :], PS[b][:, :], AF.Exp, scale=0.125),
                     reads=[PSR[b]], writes=[R_E[kt % 3]])

            def AVZ(kt):
                eb = kt % 3
                for m in range(2):
                    P.op("pe", lambda e, eb=eb, m=m, kt=kt: e.matmul(PS[2 + m][:, 0:G], Vt[:, kt, h * 128:(h + 1) * 128],
                                                                     Et[eb][:, m * G:(m + 1) * G], start=(kt == 0), stop=(kt == NKT - 1)),
                         reads=[R_V[kt], R_E[eb]], writes=[PSR[2 + m]])
                P.op("pe", lambda e, eb=eb, kt=kt: e.matmul(PS[4][:, :], ones_bf[:, :], Et[eb][:, :],
                                                            start=(kt == 0), stop=(kt == NKT - 1)),
                     reads=[R_ones, R_E[eb]], writes=[PSR[4]])

            QK(0)
            QK(1)
            for kt in range(NKT):
                if kt + 2 < NKT:
                    QK(kt + 2)
                AVZ(kt)
                for fn in hooks.get(kt, ()):
                    fn()

        prep_A(slots3(0))
        prep_B(0, 1, hT, R_hT, banks=(5, 6))
        for h in range(4):
            q_proj_1(0, h)
            q_proj_2(0, h)

        from functools import partial as _p
        for g in range(NG_LAT):
            for h in range(4):
                hooks = {}

                def add(kt, fn):
                    hooks.setdefault(kt, []).append(fn)

                if h > 0 or g > 0:
                    ph = (h - 1) % 4
                    add(1, epi_B1)
                    add(4, _p(epi_B2, ph))
                if h == 0 and g > 0:
                    add(6, _p(q_proj_1, g, 3))
                    add(8, _p(q_proj_2, g, 3))
                    for gi in range(4):
                        add(10 + 3 * gi, _p(pool_1, g - 1, gi))
                        add(12 + 3 * gi, _p(pool_2, g - 1, gi))
                    add(24, _p(out_proj, g - 1, 0))
                    add(29, _p(out_proj, g - 1, 1))
                if h == 3 and g + 1 < NG_LAT:
                    add(6, _p(prep_A, slots3(g + 1)))
                    add(10, _p(prep_B, 0, 1, hT, R_hT, (5, 6)))
                    for hh in range(3):
                        add(14 + 6 * hh, _p(q_proj_1, g + 1, hh))
                        add(16 + 6 * hh, _p(q_proj_2, g + 1, hh))
                head_loop(h, hooks)
                epi_A()
        epi_B1()
        epi_B2(3)
        for gi in range(4):
            pool_1(NG_LAT - 1, gi)
            pool_2(NG_LAT - 1, gi)
        out_proj(NG_LAT - 1, 0)
        out_proj(NG_LAT - 1, 1)
        P.barrier()
        A.off = p23_mark

    if phases >= 4:
        ffn_phase(2)

    P.emit()
    return nc, A.peak


def _consts():
    ident = np.eye(128, dtype=np.float32)
    Pm = np.zeros((128, 128), np.float32)
    for dst in range(128):
        if dst % 32 < 16:
            Pm[dst, dst + 16] = -1.0
        else:
            Pm[dst, dst - 16] = 1.0
    ropeP = np.ascontiguousarray(Pm.T)
    rows = SEQ // 64
    row = np.repeat(np.arange(rows, dtype=np.float32), 64)
    col = np.tile(np.arange(64, dtype=np.float32), rows)
    nf = 16
    freqs = (np.float32(10000.0) ** (-np.arange(nf, dtype=np.float32) / np.float32(nf))).astype(np.float32)
    ar = row[:, None] * freqs
    ac = col[:, None] * freqs
    ang = np.concatenate([ar, ar, ac, ac], axis=-1).astype(np.float32)
    cos = np.cos(ang).astype(np.float32).T
    sin = np.sin(ang).astype(np.float32).T
    cos128 = np.ascontiguousarray(np.concatenate([cos, cos], axis=0))
    sin128 = np.ascontiguousarray(np.concatenate([sin, sin], axis=0))
    n = SEQ
    band = np.zeros((128, 20, 128), np.float32)

    def mval(w, t, s):
        lo = w // 2
        hi = w - w // 2 - 1
        st = max(t - lo, 0)
        en = min(t + hi, n - 1)
        v = 0.0
        if st <= s <= en:
            v = 1.0 / (en - st + 1)
        if s == t:
            v -= 1.0
        return v

    for gi, w in enumerate(POOL_W):
        for blk, (ti, r) in enumerate([(5, -1), (5, 0), (5, 1), (0, 0), (n // 128 - 1, 0)]):
            for tp in range(128):
                t = ti * 128 + tp
                for s in range(max(0, t - 17), min(n, t + 17)):
                    sp = s - (ti + r) * 128
                    if 0 <= sp < 128:
                        band[sp, gi * 5 + blk, tp] = mval(w, t, s)
    return dict(c_ident=ident, c_ropeP=ropeP, c_cos=cos128, c_sin=sin128,
                c_band=np.ascontiguousarray(band.reshape(128, 20 * 128)))


_CACHE = {}


def _get_nc(debug=False, phases=4):
    key = (debug, phases)
    if key not in _CACHE:
        _CACHE[key] = build_nc(debug=debug, phases=phases)[0]
    return _CACHE[key]


def make_in_maps(inputs):
    f = lambda a: np.ascontiguousarray(np.asarray(a, dtype=np.float32))
    consts = _consts()
    shared = dict(
        w_mod=f(inputs["w_mod"][0]), b_mod=f(inputs["b_mod"][0]), g_ffn1=f(inputs["g_ffn1"][0]),
        ffn1_w_in=f(inputs["ffn1_w_in"][0]), ffn1_w_out=f(inputs["ffn1_w_out"][0]), g_mix=f(inputs["g_mix"][0]),
        w_in=f(inputs["w_in"][0]),
        lam4=f(np.concatenate([np.asarray(inputs["lambda_q1"][0]), np.asarray(inputs["lambda_k1"][0]),
                               np.asarray(inputs["lambda_q2"][0]), np.asarray(inputs["lambda_k2"][0])])),
        g_sub=f(inputs["g_sub"][0]), w_pool=f(inputs["w_pool"][0]), pool_scale=f(inputs["pool_scale"][0]),
        w_out=f(inputs["w_out"][0]), g_ffn2=f(inputs["g_ffn2"][0]), ffn2_w_in=f(inputs["ffn2_w_in"][0]),
        ffn2_w_out=f(inputs["ffn2_w_out"][0]), g_final=f(inputs["g_final"]),
    )
    shared.update(consts)
    x = np.asarray(inputs["x"], dtype=np.float32)
    c = np.asarray(inputs["c"], dtype=np.float32)
    ctx = np.asarray(inputs["ctx"], dtype=np.float32)
    c_ctx = np.asarray(inputs["c_ctx"], dtype=np.float32)
    maps = []
    for b in range(8):
        m = dict(shared)
        m["x"] = np.ascontiguousarray(x[b])
        m["ctx"] = np.ascontiguousarray(ctx[b])
        m["cvec"] = np.ascontiguousarray(np.stack([c[b], c_ctx], axis=0))
        maps.append(m)
    return maps


def kernel(**inputs):
    nc = _get_nc()
    in_maps = make_in_maps(inputs)
    res = run_bass_kernel_spmd(nc, in_maps, core_ids=list(range(8)))
    return np.stack([np.asarray(r["out"], dtype=np.float32) for r in res.results], axis=0)
```

```python
import math
from contextlib import ExitStack
import numpy as np
import concourse.bass as bass
import concourse.mybir as mybir
from concourse.bass_utils import run_bass_kernel_spmd

F32 = mybir.dt.float32
BF16 = mybir.dt.bfloat16
ALU = mybir.AluOpType
AF = mybir.ActivationFunctionType

D = 1024
SEQ = 4096
CTX = 256
NTOK = SEQ + CTX
DFF = 2816
NF = DFF // 128
EPS = 1e-6
LAM_INIT = 0.8 - 0.6 * math.exp(-0.3 * 0)
POOL_W = (2, 4, 8, 16)
G = 256
NG_LAT = SEQ // G
NKT = NTOK // 128

QATTR = {"pe": "tensor", "act": "scalar", "dve": "vector", "pool": "gpsimd", "sp": "sync"}
COMPUTE = ("pe", "act", "dve", "pool")


class Res:
    __slots__ = ("name", "lw", "rs")

    def __init__(self, name):
        self.name = name
        self.lw = None
        self.rs = []


class Op:
    __slots__ = ("q", "stream", "fn", "deps", "signal", "ms", "is_dma")


class Prog:
    def __init__(self, nc):
        self.nc = nc
        self.queues = {q: [] for q in QATTR}
        self.dma_count = {}
        self.last_dma = {}

    def _track(self, op, reads, writes):
        deps = []
        for r in reads:
            if r.lw is not None:
                deps.append(r.lw)
        for w in writes:
            if w.lw is not None:
                deps.append(w.lw)
            deps.extend(w.rs)
        out = []
        seen = set()
        for d in deps:
            if id(d) in seen or d is op:
                continue
            seen.add(id(d))
            if (not op.is_dma) and (not d.is_dma) and d.q == op.q and op.q == "pe":
                continue
            d.signal = True
            out.append(d)
        op.deps = out
        for r in reads:
            r.rs.append(op)
        for w in writes:
            w.lw = op
            w.rs = []

    def op(self, q, fn, reads=(), writes=()):
        o = Op()
        o.q = q
        o.stream = q
        o.fn = fn
        o.signal = False
        o.ms = None
        o.is_dma = False
        self._track(o, reads, writes)
        self.queues[q].append(o)
        return o

    def dma(self, stream, out, in_, reads=(), writes=(), q="sp", **kw):
        o = Op()
        o.q = q
        o.stream = "d_" + stream
        o.fn = lambda eng, out=out, in_=in_, kw=kw: eng.dma_start(out=out, in_=in_, **kw)
        o.signal = True
        o.is_dma = True
        n = self.dma_count.get(o.stream, 0) + 1
        self.dma_count[o.stream] = n
        o.ms = 16 * n
        self._track(o, reads, writes)
        self.queues[q].append(o)
        self.last_dma[o.stream] = o
        return o

    def barrier(self):
        lasts = []
        for q in COMPUTE:
            for o in reversed(self.queues[q]):
                if not o.is_dma and o.fn is not None:
                    lasts.append(o)
                    break
        lasts.extend(self.last_dma.values())
        for q in QATTR:
            o = Op()
            o.q = q
            o.stream = q
            o.fn = None
            o.signal = False
            o.ms = None
            o.is_dma = False
            o.deps = []
            for d in lasts:
                if d.q == q and not d.is_dma:
                    continue
                d.signal = True
                o.deps.append(d)
            self.queues[q].append(o)

    def emit(self):
        nc = self.nc
        for q in COMPUTE:
            c = 0
            for o in self.queues[q]:
                if o.is_dma or o.fn is None:
                    continue
                if o.signal:
                    c += 1
                    o.ms = c
        with ExitStack() as es:
            sems = {}
            for q in COMPUTE:
                sems[q] = es.enter_context(nc.semaphore("s_" + q))
            for s in self.dma_count:
                sems[s] = es.enter_context(nc.semaphore("s_" + s))
            block = es.enter_context(nc.Block())
            for q, attr in QATTR.items():
                ops = self.queues[q]
                final_waits = []
                if q == "sp":
                    final_waits = [(s, 16 * n) for s, n in self.dma_count.items()]

                def section(eng, ops=ops, final_waits=final_waits):
                    seen = {}
                    for o in ops:
                        for d in o.deps:
                            if seen.get(d.stream, 0) >= d.ms:
                                continue
                            seen[d.stream] = d.ms
                            eng.wait_ge(sems[d.stream], d.ms)
                        if o.fn is None:
                            continue
                        ins = o.fn(eng)
                        if o.is_dma:
                            ins.then_inc(sems[o.stream], 16)
                        elif o.signal:
                            ins.then_inc(sems[o.stream], 1)
                    for s, v in final_waits:
                        eng.wait_ge(sems[s], v)

                getattr(block, attr)(section)


class Arena:
    def __init__(self, nc, nbytes):
        self.t = nc.alloc_sbuf_tensor("arena", [128, nbytes // 4], F32)
        self.cap = nbytes
        self.off = 0
        self.peak = 0

    def alloc(self, shape, dt):
        ne = int(np.prod(shape[1:]))
        n = ne * (4 if dt == F32 else 2)
        n = (n + 63) // 64 * 64
        assert self.off + n <= self.cap, ("SBUF arena overflow", self.off, n, self.cap)
        v = self.t[0:shape[0], self.off // 4:(self.off + n) // 4]
        if dt != F32:
            v = v.bitcast(dt)
        v = v[:, 0:ne]
        if len(shape) == 3:
            v = v.rearrange("p (a b) -> p a b", a=shape[1])
        elif len(shape) == 4:
            v = v.rearrange("p (a b c) -> p a b c", a=shape[1], b=shape[2])
        self.off += n
        self.peak = max(self.peak, self.off)
        return v


def build_nc(debug=False, phases=4):
    nc = bass.Bass("TRN2", target_bir_lowering=False)

    def din(name, shape):
        return nc.dram_tensor(name, list(shape), F32, kind="ExternalInput").ap()

    x_d = din("x", [SEQ, D])
    ctx_d = din("ctx", [CTX, D])
    cvec_d = din("cvec", [2, D])
    wmod_d = din("w_mod", [D, 9 * D])
    bmod_d = din("b_mod", [9 * D])
    g1_d = din("g_ffn1", [D])
    f1in_d = din("ffn1_w_in", [D, 2 * DFF])
    f1out_d = din("ffn1_w_out", [DFF, D])
    gmix_d = din("g_mix", [D])
    win_d = din("w_in", [D, 2048])
    lam_d = din("lam4", [4 * 64])
    gsub_d = din("g_sub", [128])
    wpool_d = din("w_pool", [4, 128, 128])
    pscale_d = din("pool_scale", [512])
    wout_d = din("w_out", [D, D])
    g2_d = din("g_ffn2", [D])
    f2in_d = din("ffn2_w_in", [D, 2 * DFF])
    f2out_d = din("ffn2_w_out", [DFF, D])
    gfin_d = din("g_final", [D])
    ident_d = din("c_ident", [128, 128])
    ropeP_d = din("c_ropeP", [128, 128])
    cos_d = din("c_cos", [128, SEQ])
    sin_d = din("c_sin", [128, SEQ])
    band_d = din("c_band", [128, 20 * 128])
    out_d = nc.dram_tensor("out", [SEQ, D], F32, kind="ExternalOutput").ap()
    x1s_d = nc.dram_tensor("x1s", [NTOK, D], F32,
                           kind="ExternalOutput" if debug else "Internal").ap()
    mscr_d = nc.dram_tensor("mscr", [2, 9 * D], F32,
                            kind="ExternalOutput" if debug else "Internal").ap()

    P = Prog(nc)
    cap = nc.sbuf_bytes_remaining - 256
    cap = cap // 64 * 64
    A = Arena(nc, cap)
    PS = [nc.alloc_psum_tensor("ps%d" % i, [128, 512], F32) for i in range(8)]
    PSR = [Res("ps%d" % i) for i in range(8)]
    PSB = [PS[i][:, :].bitcast(BF16) for i in range(8)]

    ident_bf = A.alloc([128, 128], BF16)
    ones_bf = A.alloc([128, 128], BF16)
    onesdiv_f = A.alloc([128, 128], F32)
    ropeP_f = A.alloc([128, 128], F32)
    modT = A.alloc([128, 2, 9, 8], F32)
    gT = A.alloc([128, 3, 8], F32)
    psT = A.alloc([128, 4], F32)
    gsubs = A.alloc([128, 1], F32)
    neglam = A.alloc([128, 1], F32)
    ABt = A.alloc([128, 12, 8], F32)
    stat = A.alloc([128, 16], F32)
    xt = [A.alloc([128, D], F32) for _ in range(4)]
    xn = [A.alloc([128, D], BF16) for _ in range(2)]
    junk = A.alloc([128, D], BF16)
    hT = A.alloc([128, 8, G], BF16)
    stage = [A.alloc([128, D], F32) for _ in range(4)]
    tmpf = [A.alloc([128, 512], F32) for _ in range(2)]
    R_ident, R_ones, R_onesdiv, R_ropeP = Res("ident"), Res("ones"), Res("onesdiv"), Res("ropeP")
    R_modT, R_gT, R_psT, R_gsubs, R_neglam, R_AB, R_stat = (Res("modT"), Res("gT"), Res("psT"),
                                                           Res("gsubs"), Res("neglam"), Res("AB"), Res("stat"))
    R_xt = [Res("xt%d" % i) for i in range(4)]
    R_xn = [Res("xn%d" % i) for i in range(2)]
    R_junk = Res("junk")
    R_hT = Res("hT")
    R_stage = [Res("stage%d" % i) for i in range(4)]
    R_tmpf = [Res("tmpf%d" % i) for i in range(2)]
    R_x1s = [Res("x1s%d" % i) for i in range(NKT)]
    R_mscr = Res("mscr")
    persist_off = A.off

    stage_ctr = [0]

    def load_cast(dst_ap, src_ap, shape3, R_dst):
        s = stage_ctr[0] % 4
        stage_ctr[0] += 1
        ne = int(np.prod(shape3[1:]))
        sv = stage[s][:, 0:ne]
        if len(shape3) == 3:
            sv = sv.rearrange("p (a b) -> p a b", a=shape3[1])
        P.dma("stage%d" % s, sv, src_ap, writes=[R_stage[s]])
        P.op("pool", lambda e, d=dst_ap, v=sv: e.tensor_copy(d, v), reads=[R_stage[s]], writes=[R_dst])

    P.dma("c_rp", ropeP_f[:, :], ropeP_d, writes=[R_ropeP])
    load_cast(ident_bf[:, :], ident_d, [128, 128], R_ident)
    P.op("pool", lambda e: e.memset(ones_bf[:, :], 1.0), writes=[R_ones])
    P.op("pool", lambda e: e.memset(onesdiv_f[:, :], 1.0 / 128), writes=[R_onesdiv])
    P.dma("c_g", gT[:, 0, :], g1_d.rearrange("(c p) -> p c", p=128), writes=[R_gT], allow_slow_non_contiguous=True)
    P.dma("c_g", gT[:, 1, :], gmix_d.rearrange("(c p) -> p c", p=128), writes=[R_gT], allow_slow_non_contiguous=True)
    P.dma("c_g", gT[:, 2, :], g2_d.rearrange("(c p) -> p c", p=128), writes=[R_gT], allow_slow_non_contiguous=True)
    P.dma("c_ps", psT[:, :], pscale_d.rearrange("(c p) -> p c", p=128), writes=[R_psT], allow_slow_non_contiguous=True)
    P.dma("c_gs", gsubs[:, :], gsub_d.rearrange("(p o) -> p o", o=1), writes=[R_gsubs])
    P.op("dve", lambda e: e.tensor_scalar(gsubs[:, :], gsubs[:, :], 1.0 - LAM_INIT, None, ALU.mult),
         reads=[R_gsubs], writes=[R_gsubs])

    lamt = tmpf[0][:, 0:256]
    P.dma("c_lam", lamt, lam_d.partition_broadcast(128), writes=[R_tmpf[0]])
    P.op("dve", lambda e: e.scalar_tensor_tensor(junk[:, 0:64], lamt[:, 0:64], 1.0, lamt[:, 64:128], ALU.mult, ALU.mult,
                                                 accum_out=stat[:, 0:1]), reads=[R_tmpf[0]], writes=[R_stat, R_junk])
    P.op("dve", lambda e: e.scalar_tensor_tensor(junk[:, 0:64], lamt[:, 128:192], 1.0, lamt[:, 192:256], ALU.mult, ALU.mult,
                                                 accum_out=stat[:, 1:2]), reads=[R_tmpf[0], R_junk, R_stat], writes=[R_stat, R_junk])
    P.op("act", lambda e: e.activation(stat[:, 2:4], stat[:, 0:2], AF.Exp), reads=[R_stat], writes=[R_stat])
    P.op("dve", lambda e: e.tensor_tensor(neglam[:, :], stat[:, 3:4], stat[:, 2:3], ALU.subtract),
         reads=[R_stat], writes=[R_neglam])
    P.op("dve", lambda e: e.tensor_scalar(neglam[:, :], neglam[:, :], -LAM_INIT, None, ALU.add),
         reads=[R_neglam], writes=[R_neglam])

    craw = tmpf[1][:, 16:32].rearrange("p (k c) -> p k c", k=2)
    P.dma("c_c", craw[:, 0, :], cvec_d[0, :].rearrange("(c p) -> p c", p=128), writes=[R_tmpf[1]], allow_slow_non_contiguous=True)
    P.dma("c_c2", craw[:, 1, :], cvec_d[1, :].rearrange("(c p) -> p c", p=128), writes=[R_tmpf[1]], allow_slow_non_contiguous=True)

    p0_mark = A.off
    scT = A.alloc([128, 8, 128], F32)
    R_scT = Res("scT")
    P.op("pool", lambda e: e.memset(scT[:, :, :], 0.0), writes=[R_scT])
    for k in range(2):
        P.op("act", lambda e, k=k: e.activation(scT[:, :, k], craw[:, k, :], AF.Silu), reads=[R_tmpf[1], R_scT], writes=[R_scT])
    bm2 = A.alloc([2, 9 * D], F32)
    msb = A.alloc([2, 3 * D], F32)
    wms = [A.alloc([128, 3 * D], F32) for _ in range(3)]
    R_bm2, R_msb = Res("bm2"), Res("msb")
    R_wms = [Res("wms%d" % i) for i in range(3)]
    for k in range(2):
        P.dma("c_bm%d" % k, bm2[k:k + 1, :], bmod_d.rearrange("(o n) -> o n", o=1), writes=[R_bm2])
    wm_ctr = 0
    for gq in range(3):
        for kc in range(8):
            s = wm_ctr % 3
            wm_ctr += 1
            P.dma("wms%d" % s, wms[s][:, :], wmod_d[kc * 128:(kc + 1) * 128, gq * 3 * D:(gq + 1) * 3 * D],
                  writes=[R_wms[s]])
            for j in range(6):
                P.op("pe", lambda e, j=j, s=s, kc=kc: e.matmul(PS[j][:, :], scT[:, kc, :], wms[s][:, j * 512:(j + 1) * 512],
                                                                 start=(kc == 0), stop=(kc == 7)),
                     reads=[R_scT, R_wms[s]], writes=[PSR[j]])
        for j in range(6):
            P.op("dve", lambda e, j=j, gq=gq: e.tensor_tensor(msb[:, j * 512:(j + 1) * 512], PS[j][0:2, :],
                                                             bm2[:, gq * 3 * D + j * 512: gq * 3 * D + (j + 1) * 512], ALU.add),
                 reads=[PSR[j], R_bm2], writes=[R_msb])
        P.dma("mscr_w", mscr_d[:, gq * 3 * D:(gq + 1) * 3 * D], msb[:, :], reads=[R_msb], writes=[R_mscr])
        for k in range(2):
            P.dma("modT%d" % k, modT[:, k, 3 * gq:3 * gq + 3, :],
                  mscr_d[k, gq * 3 * D:(gq + 1) * 3 * D].rearrange("(v c p) -> p v c", p=128, c=8),
                  reads=[R_mscr], writes=[R_modT], allow_slow_non_contiguous=True)
        for k in range(2):
            ia = (k * 3 + gq) * 2
            P.op("dve", lambda e, k=k, gq=gq, ia=ia: e.scalar_tensor_tensor(ABt[:, ia, :], modT[:, k, 3 * gq + 1, :], 1.0,
                                                                           gT[:, gq, :], ALU.add, ALU.mult),
                 reads=[R_modT, R_gT], writes=[R_AB])
            P.op("dve", lambda e, k=k, gq=gq, ia=ia: e.tensor_copy(ABt[:, ia + 1, :], modT[:, k, 3 * gq, :]),
                 reads=[R_modT], writes=[R_AB])
    P.barrier()
    A.off = p0_mark

    def prep_items(slots, kind, s, hTv, R_hTv, banks=(6, 7)):
        ia = (kind * 3 + s) * 2
        items = []

        def stats(tt):
            sl = slots[tt]
            xs = xt[sl]
            c0 = 4 + 3 * tt
            P.op("dve", lambda e, xs=xs, c0=c0: e.scalar_tensor_tensor(junk[:, :], xs[:, :], 1.0, xs[:, :], ALU.mult, ALU.mult,
                                                                      accum_out=stat[:, c0:c0 + 1]),
                 reads=[R_xt[sl]], writes=[R_junk, R_stat])
            P.op("act", lambda e, c0=c0: e.activation(stat[:, c0 + 1:c0 + 2], stat[:, c0:c0 + 1], AF.Ln, bias=EPS, scale=1.0 / D),
                 reads=[R_stat], writes=[R_stat])
            P.op("act", lambda e, c0=c0: e.activation(stat[:, c0 + 2:c0 + 3], stat[:, c0 + 1:c0 + 2], AF.Exp, scale=-0.5),
                 reads=[R_stat], writes=[R_stat])

        def norm(tt):
            sl = slots[tt]
            xs = xt[sl]
            c0 = 4 + 3 * tt
            P.op("dve", lambda e, xs=xs, c0=c0, tt=tt: e.tensor_scalar(xn[tt][:, :], xs[:, :], stat[:, c0 + 2:c0 + 3], None, ALU.mult),
                 reads=[R_xt[sl], R_stat], writes=[R_xn[tt]])

        def transp(tt):
            bk = banks[tt]
            for c in range(8):
                P.op("pe", lambda e, c=c, tt=tt, bk=bk: e.transpose(PSB[bk][:, c * 128:(c + 1) * 128], xn[tt][:, c * 128:(c + 1) * 128],
                                                                    ident_bf[:, :]),
                     reads=[R_xn[tt], R_ident], writes=[PSR[bk]])

        def evac(tt, c0_, c1_):
            bk = banks[tt]
            for c in range(c0_, c1_):
                P.op("dve", lambda e, c=c, tt=tt, bk=bk: e.tensor_scalar(hTv[:, c, tt * 128:(tt + 1) * 128],
                                                                        PSB[bk][:, c * 128:(c + 1) * 128],
                                                                        ABt[:, ia, c:c + 1], ABt[:, ia + 1, c:c + 1],
                                                                        ALU.mult, ALU.add),
                     reads=[PSR[bk], R_AB], writes=[R_hTv])

        from functools import partial as _pp
        items = [_pp(stats, 0), _pp(norm, 0), _pp(stats, 1), _pp(norm, 1),
                 _pp(transp, 0), _pp(evac, 0, 0, 4), _pp(evac, 0, 4, 8),
                 _pp(transp, 1), _pp(evac, 1, 0, 4), _pp(evac, 1, 4, 8)]
        return items

    def prep(ntile, slots, kind, s, hTv, R_hTv=None, banks=(6, 7)):
        for it in prep_items(slots, kind, s, hTv, R_hT if R_hTv is None else R_hTv, banks):
            it()

    def tok_rows(g, tt):
        r0 = g * G + tt * 128
        return r0

    def ffn_phase(which):
        fin_d, fout_d = (f1in_d, f1out_d) if which == 1 else (f2in_d, f2out_d)
        s_idx = 0 if which == 1 else 2
        groups = list(range(NG_LAT + 1)) if which == 1 else list(range(NG_LAT))
        mark = A.off
        w1 = A.alloc([128, 8, 2 * DFF], BF16)
        w2 = A.alloc([128, NF, D], BF16)
        gate_bc = [A.alloc([128, D], F32) for _ in range(2 if which == 1 else 1)]
        gfin_bc = A.alloc([128, D], F32) if which == 2 else None
        sg = [A.alloc([128, G], F32) for _ in range(2)]
        aT = [A.alloc([128, G], BF16) for _ in range(2)]
        hT2 = A.alloc([128, 8, G], BF16)
        hTs = [hT, hT2]
        R_hTs = [R_hT, Res("hT2")]
        R_w1g = [Res("w1g%d" % f) for f in range(NF)]
        R_w1u = [Res("w1u%d" % f) for f in range(NF)]
        R_w2 = [Res("w2_%d" % f) for f in range(NF)]
        R_gbc = [Res("gbc0"), Res("gbc1")]
        R_gfin = Res("gfin")
        R_sg = [Res("sg0"), Res("sg1")]
        R_aT = [Res("aT0"), Res("aT1")]

        def src_rows(g, tt):
            if which == 1:
                if g < NG_LAT:
                    return x_d[g * G + tt * 128: g * G + (tt + 1) * 128, :]
                return ctx_d[tt * 128:(tt + 1) * 128, :]
            r0 = g * G + tt * 128
            return x1s_d[r0:r0 + 128, :]

        def load_group(g):
            for tt in range(2):
                sl = (g % 2) * 2 + tt
                rds = [R_x1s[g * 2 + tt]] if which == 2 else []
                P.dma("xt%d" % sl, xt[sl][:, :], src_rows(g, tt), reads=rds, writes=[R_xt[sl]])

        v_gate = 3 * s_idx + 2
        P.dma("gbc0", gate_bc[0][:, :], mscr_d[0, v_gate * D:(v_gate + 1) * D].partition_broadcast(128),
              reads=[R_mscr], writes=[R_gbc[0]])
        P.op("dve", lambda e: e.tensor_scalar(gate_bc[0][:, :], gate_bc[0][:, :], 0.5, None, ALU.mult),
             reads=[R_gbc[0]], writes=[R_gbc[0]])
        if which == 1:
            P.dma("gbc1", gate_bc[1][:, :], mscr_d[1, v_gate * D:(v_gate + 1) * D].partition_broadcast(128),
                  reads=[R_mscr], writes=[R_gbc[1]])
            P.op("dve", lambda e: e.tensor_scalar(gate_bc[1][:, :], gate_bc[1][:, :], 0.5, None, ALU.mult),
                 reads=[R_gbc[1]], writes=[R_gbc[1]])
        else:
            P.dma("gfin", gfin_bc[:, :], gfin_d.partition_broadcast(128), writes=[R_gfin])

        load_group(groups[0])
        load_group(groups[1])
        fin_v = fin_d.rearrange("(kc p) n -> p kc n", p=128)
        for f in range(NF):
            load_cast(w1[:, :, f * 128:(f + 1) * 128], fin_v[:, :, f * 128:(f + 1) * 128], [128, 8, 128], R_w1g[f])
            load_cast(w1[:, :, DFF + f * 128:DFF + (f + 1) * 128], fin_v[:, :, DFF + f * 128:DFF + (f + 1) * 128],
                      [128, 8, 128], R_w1u[f])
            load_cast(w2[:, f, :], fout_d[f * 128:(f + 1) * 128, :], [128, D], R_w2[f])

        PREP_AT = {5: 0, 6: 1, 7: 2, 8: 3, 10: 4, 11: 5, 12: 6, 13: 7, 14: 8, 15: 9}
        pending_tail = [None]
        tmp4 = [tmpf[0], tmpf[1], A.alloc([128, 512], F32), A.alloc([128, 512], F32)]
        R_tmp4 = [R_tmpf[0], R_tmpf[1], Res("tmp4_2"), Res("tmp4_3")]

        def kind_of(g):
            return 1 if (which == 1 and g == NG_LAT) else 0

        def slots_of(g):
            return [(g % 2) * 2, (g % 2) * 2 + 1]

        prep(2, slots_of(groups[0]), kind_of(groups[0]), s_idx, hTs[0], R_hTs[0])
        for gi_, g in enumerate(groups):
            kind = kind_of(g)
            slots = slots_of(g)
            hTc = hTs[gi_ % 2]
            R_hTc = R_hTs[gi_ % 2]
            nxt = groups[gi_ + 1] if gi_ + 1 < len(groups) else None

            def GU(f, hTc=hTc, R_hTc=R_hTc):
                b = 4 + (f % 2)
                for kc in range(8):
                    P.op("pe", lambda e, b=b, kc=kc, f=f: e.matmul(PS[b][:, 0:G], w1[:, kc, f * 128:(f + 1) * 128], hTc[:, kc, :],
                                                                   start=(kc == 0), stop=(kc == 7)),
                         reads=[R_w1g[f], R_hTc], writes=[PSR[b]])
                for kc in range(8):
                    P.op("pe", lambda e, b=b, kc=kc, f=f: e.matmul(PS[b][:, G:2 * G], w1[:, kc, DFF + f * 128:DFF + (f + 1) * 128],
                                                                   hTc[:, kc, :], start=(kc == 0), stop=(kc == 7)),
                         reads=[R_w1u[f], R_hTc], writes=[PSR[b]])
                P.op("act", lambda e, b=b, f=f: e.activation(sg[f % 2][:, :], PS[b][:, 0:G], AF.Silu),
                     reads=[PSR[b]], writes=[R_sg[f % 2]])
                P.op("dve", lambda e, b=b, f=f: e.tensor_tensor(aT[f % 2][:, :], PS[b][:, G:2 * G], sg[f % 2][:, :], ALU.mult),
                     reads=[PSR[b], R_sg[f % 2]], writes=[R_aT[f % 2]])

            def OUT(f):
                for tt in range(2):
                    for dh in range(2):
                        b = tt * 2 + dh
                        P.op("pe", lambda e, b=b, tt=tt, dh=dh, f=f: e.matmul(PS[b][:, :], aT[f % 2][:, tt * 128:(tt + 1) * 128],
                                                                              w2[:, f, dh * 512:(dh + 1) * 512],
                                                                              start=(f == 0), stop=(f == NF - 1)),
                             reads=[R_aT[f % 2], R_w2[f]], writes=[PSR[b]])

            if nxt is not None:
                nitems = prep_items(slots_of(nxt), kind_of(nxt), s_idx, hTs[(gi_ + 1) % 2], R_hTs[(gi_ + 1) % 2])
            GU(0)
            for f in range(NF):
                if f + 1 < NF:
                    GU(f + 1)
                OUT(f)
                if f < 4 and pending_tail[0] is not None:
                    pending_tail[0][f]()
                if f == 3:
                    pending_tail[0] = None
                if nxt is not None and f in PREP_AT:
                    nitems[PREP_AT[f]]()
            gb = gate_bc[kind]
            for tt in range(2):
                for dh in range(2):
                    b = tt * 2 + dh
                    tb = tmp4[b]
                    P.op("dve", lambda e, b=b, dh=dh, gb=gb, tb=tb: e.tensor_tensor(tb[:, :], PS[b][:, :], gb[:, dh * 512:(dh + 1) * 512],
                                                                                     ALU.mult),
                         reads=[PSR[b], R_gbc[kind]], writes=[R_tmp4[b]])
            def mk_tail(g=g, slots=slots, gi_=gi_):
                def adds(tt):
                    sl = slots[tt]
                    for dh in range(2):
                        bb = tt * 2 + dh
                        tb = tmp4[bb]
                        P.op("dve", lambda e, sl=sl, dh=dh, tb=tb: e.tensor_tensor(xt[sl][:, dh * 512:(dh + 1) * 512], tb[:, :],
                                                                                  xt[sl][:, dh * 512:(dh + 1) * 512], ALU.add),
                             reads=[R_tmp4[bb], R_xt[sl]], writes=[R_xt[sl]])

                def fin(tt):
                    sl = slots[tt]
                    r0 = g * G + tt * 128
                    if which == 1:
                        P.dma("xt%d" % sl, x1s_d[r0:r0 + 128, :], xt[sl][:, :], reads=[R_xt[sl]], writes=[R_x1s[g * 2 + tt]])
                    else:
                        c0 = 10 + 3 * tt
                        P.op("dve", lambda e, sl=sl, c0=c0: e.scalar_tensor_tensor(junk[:, :], xt[sl][:, :], 1.0, xt[sl][:, :], ALU.mult, ALU.mult,
                                                                                  accum_out=stat[:, c0:c0 + 1]),
                             reads=[R_xt[sl]], writes=[R_junk, R_stat])
                        P.op("act", lambda e, c0=c0: e.activation(stat[:, c0 + 1:c0 + 2], stat[:, c0:c0 + 1], AF.Ln, bias=EPS, scale=1.0 / D),
                             reads=[R_stat], writes=[R_stat])
                        P.op("act", lambda e, c0=c0: e.activation(stat[:, c0 + 2:c0 + 3], stat[:, c0 + 1:c0 + 2], AF.Exp, scale=-0.5),
                             reads=[R_stat], writes=[R_stat])
                        P.op("dve", lambda e, sl=sl, c0=c0: e.scalar_tensor_tensor(xt[sl][:, :], xt[sl][:, :], stat[:, c0 + 2:c0 + 3],
                                                                                  gfin_bc[:, :], ALU.mult, ALU.mult),
                             reads=[R_xt[sl], R_stat, R_gfin], writes=[R_xt[sl]])
                        P.dma("xt%d" % sl, out_d[r0:r0 + 128, :], xt[sl][:, :], reads=[R_xt[sl]])

                def last():
                    fin(1)
                    if gi_ + 2 < len(groups):
                        load_group(groups[gi_ + 2])

                from functools import partial as _pt
                return [_pt(adds, 0), _pt(fin, 0), _pt(adds, 1), last]

            pending_tail[0] = mk_tail()
        for it in pending_tail[0]:
            it()
        P.barrier()
        A.off = mark

    ffn_phase(1)

    if phases >= 2:
        p23_mark = A.off
        KT = A.alloc([128, 4, NTOK], BF16)
        Vt = A.alloc([128, NKT, 512], BF16)
        Ut = A.alloc([128, SEQ // 128, 512], BF16)
        R_KT = [Res("KT%d" % g) for g in range(NG_LAT + 1)]
        R_V = [Res("V%d" % i) for i in range(NKT)]
        R_U = [Res("U%d" % i) for i in range(SEQ // 128)]
        p2_mark = A.off
        wkvu = A.alloc([128, 8, 1536], BF16)
        rope_sb = [A.alloc([128, 2, G], F32) for _ in range(2)]
        kf4 = [A.alloc([128, G], F32) for _ in range(4)]
        kt1 = [A.alloc([128, G], F32) for _ in range(4)]
        kt2 = [A.alloc([128, G], F32) for _ in range(4)]
        hT2b = A.alloc([128, 8, G], BF16)
        hT2s = [hT, hT2b]
        R_hT2s = [R_hT, Res("hT2b")]
        R_wkvu = [Res("wkvu%d" % j) for j in range(12)]
        R_rope = [Res("rope0"), Res("rope1")]
        R_kf4 = [Res("kf%d" % h) for h in range(4)]
        R_kt1 = [Res("kt1_%d" % h) for h in range(4)]
        R_kt2 = [Res("kt2_%d" % h) for h in range(4)]
        win_v = win_d.rearrange("(kc p) n -> p kc n", p=128)

        def load_x1_group(g):
            for tt in range(2):
                sl = (g % 2) * 2 + tt
                r0 = g * G + tt * 128
                P.dma("xt%d" % sl, xt[sl][:, :], x1s_d[r0:r0 + 128, :], reads=[R_x1s[g * 2 + tt]], writes=[R_xt[sl]])

        def load_rope(g):
            s = g % 2
            P.dma("rope%d" % s, rope_sb[s][:, 0, :], cos_d[:, g * G:(g + 1) * G], writes=[R_rope[s]])
            P.dma("rope%d" % s, rope_sb[s][:, 1, :], sin_d[:, g * G:(g + 1) * G], reads=[], writes=[R_rope[s]])

        load_x1_group(0)
        load_x1_group(1)
        load_rope(0)
        load_rope(1)
        for j in range(12):
            load_cast(wkvu[:, :, j * 128:(j + 1) * 128], win_v[:, :, 512 + j * 128:512 + (j + 1) * 128], [128, 8, 128], R_wkvu[j])

        NG2 = NG_LAT + 1
        prep(2, [0, 1], 0, 1, hT2s[0], R_hT2s[0])
        for g in range(NG2):
            is_ctx = g == NG_LAT
            hc = hT2s[g % 2]
            R_hc = R_hT2s[g % 2]
            rs_ = g % 2
            if g + 1 < NG2:
                kind_n = 1 if (g + 1) == NG_LAT else 0
                nit = prep_items([((g + 1) % 2) * 2, ((g + 1) % 2) * 2 + 1], kind_n, 1, hT2s[(g + 1) % 2], R_hT2s[(g + 1) % 2])
            else:
                nit = [lambda: None] * 10
            for h in range(4):
                b = h // 2
                cs = (h % 2) * G
                for kc in range(8):
                    P.op("pe", lambda e, b=b, cs=cs, kc=kc, h=h, hc=hc: e.matmul(PS[b][:, cs:cs + G], wkvu[:, kc, h * 128:(h + 1) * 128],
                                                                               hc[:, kc, :], start=(kc == 0), stop=(kc == 7)),
                         reads=[R_wkvu[h], R_hc], writes=[PSR[b]])
            for h in range(4):
                b = h // 2
                cs = (h % 2) * G
                if is_ctx:
                    P.op("act", lambda e, b=b, cs=cs, h=h, g=g: e.activation(KT[:, h, g * G:(g + 1) * G], PS[b][:, cs:cs + G], AF.Copy),
                         reads=[PSR[b]], writes=[R_KT[g]])
                else:
                    P.op("act", lambda e, b=b, cs=cs, h=h: e.activation(kf4[h][:, :], PS[b][:, cs:cs + G], AF.Copy),
                         reads=[PSR[b]], writes=[R_kf4[h]])
            nit[0]()
            nit[1]()
            for tt in range(2):
                ti = g * 2 + tt
                for kc in range(8):
                    P.op("pe", lambda e, kc=kc, tt=tt, hc=hc: e.matmul(PS[4][:, :], hc[:, kc, tt * 128:(tt + 1) * 128], wkvu[:, kc, 512:1024],
                                                                       start=(kc == 0), stop=(kc == 7)),
                         reads=[R_hc] + R_wkvu[4:8], writes=[PSR[4]])
                P.op("act", lambda e, ti=ti: e.activation(Vt[:, ti, :], PS[4][:, :], AF.Copy), reads=[PSR[4]], writes=[R_V[ti]])
                if not is_ctx:
                    for kc in range(8):
                        P.op("pe", lambda e, kc=kc, tt=tt, hc=hc: e.matmul(PS[5][:, :], hc[:, kc, tt * 128:(tt + 1) * 128],
                                                                           wkvu[:, kc, 1024:1536], start=(kc == 0), stop=(kc == 7)),
                             reads=[R_hc] + R_wkvu[8:12], writes=[PSR[5]])
                    P.op("act", lambda e, ti=ti: e.activation(Ut[:, ti, :], PS[5][:, :], AF.Copy), reads=[PSR[5]], writes=[R_U[ti]])
                nit[2 + tt]()
            if not is_ctx:
                for h in range(4):
                    b = 2 + h // 2
                    cs = (h % 2) * G
                    P.op("pe", lambda e, b=b, cs=cs, h=h: e.matmul(PS[b][:, cs:cs + G], ropeP_f[:, :], kf4[h][:, :], start=True, stop=True),
                         reads=[R_ropeP, R_kf4[h]], writes=[PSR[b]])
                for h in range(4):
                    b = 2 + h // 2
                    cs = (h % 2) * G
                    P.op("dve", lambda e, b=b, cs=cs, rs_=rs_, h=h: e.tensor_tensor(kt1[h][:, :], PS[b][:, cs:cs + G], rope_sb[rs_][:, 1, :], ALU.mult),
                         reads=[PSR[b], R_rope[rs_]], writes=[R_kt1[h]])
                    P.op("pool", lambda e, h=h, rs_=rs_: e.tensor_tensor(kt2[h][:, :], kf4[h][:, :], rope_sb[rs_][:, 0, :], ALU.mult),
                         reads=[R_kf4[h], R_rope[rs_]], writes=[R_kt2[h]])
                    P.op("pool", lambda e, h=h, g=g: e.tensor_tensor(KT[:, h, g * G:(g + 1) * G], kt1[h][:, :], kt2[h][:, :], ALU.add),
                         reads=[R_kt1[h], R_kt2[h]], writes=[R_KT[g]])
                    if h == 1:
                        nit[4]()
                        nit[5]()
            else:
                nit[4]()
                nit[5]()
            for k_ in range(6, 10):
                nit[k_]()
            if g + 2 < NG_LAT:
                load_rope(g + 2)
            if g + 2 < NG2:
                load_x1_group(g + 2)
        P.barrier()
        A.off = p2_mark

    if phases >= 3:
        wq = A.alloc([128, 8, 512], BF16)
        wo = A.alloc([128, 8, D], BF16)
        wpl = A.alloc([128, 4, 128], BF16)
        band = A.alloc([128, 20, 128], BF16)
        gt2_bc = A.alloc([128, D], F32)
        Qpad = A.alloc([128, 4, 2 * G], BF16)
        Et = [A.alloc([128, 2 * G], BF16) for _ in range(3)]
        Rz = A.alloc([128, 2 * G], F32)
        o0 = A.alloc([128, G], F32)
        o1 = A.alloc([128, G], F32)
        osq = A.alloc([128, G], F32)
        rr = A.alloc([128, G], F32)
        attT = A.alloc([128, 8, G], BF16)
        dT = A.alloc([128, G], BF16)
        rope3_sb = [A.alloc([128, 2, G], F32) for _ in range(2)]
        qf = A.alloc([128, G], F32)
        q_t1 = A.alloc([128, G], F32)
        q_t2 = A.alloc([128, G], F32)
        R_wq = [Res("wq%d" % j) for j in range(4)]
        R_wo = [Res("wo%d" % j) for j in range(8)]
        R_wpl, R_band, R_gt2, R_Qpad = Res("wpl"), Res("band"), Res("gt2"), [Res("Qpad%d" % h) for h in range(4)]
        R_E = [Res("E0"), Res("E1"), Res("E2")]
        R_Rz, R_o0, R_o1, R_osq, R_rr = Res("Rz"), Res("o0"), Res("o1"), Res("osq"), Res("rr")
        R_attT = [Res("attT%d" % c) for c in range(8)]
        R_dT = Res("dT")
        R_rope3 = [Res("rope0"), Res("rope1")]
        R_qf, R_qt1, R_qt2 = Res("qf"), Res("q_t1"), Res("q_t2")
        win_v3 = win_d.rearrange("(kc p) n -> p kc n", p=128)
        SBK = (0, 1, 7)
        NT = SEQ // 128

        def slots3(g):
            return [(g % 2) * 2, (g % 2) * 2 + 1]

        def load_x1_group3(g):
            for tt in range(2):
                sl = (g % 2) * 2 + tt
                r0 = g * G + tt * 128
                P.dma("xt%d" % sl, xt[sl][:, :], x1s_d[r0:r0 + 128, :], reads=[R_x1s[g * 2 + tt]], writes=[R_xt[sl]])

        def load_rope3(g):
            s = g % 2
            P.dma("rope%d" % s, rope3_sb[s][:, 0, :], cos_d[:, g * G:(g + 1) * G], writes=[R_rope3[s]])
            P.dma("rope%d" % s, rope3_sb[s][:, 1, :], sin_d[:, g * G:(g + 1) * G], writes=[R_rope3[s]])

        load_x1_group3(0)
        load_x1_group3(1)
        load_rope3(0)
        load_rope3(1)
        P.dma("gt2", gt2_bc[:, :], mscr_d[0, 5 * D:6 * D].partition_broadcast(128), reads=[R_mscr], writes=[R_gt2])
        for j in range(4):
            load_cast(wq[:, :, j * 128:(j + 1) * 128], win_v3[:, :, j * 128:(j + 1) * 128], [128, 8, 128], R_wq[j])
        for c in range(8):
            load_cast(wo[:, c, :], wout_d[c * 128:(c + 1) * 128, :], [128, D], R_wo[c])
        load_cast(wpl[:, :, :], wpool_d.rearrange("g i o -> i g o"), [128, 4, 128], R_wpl)
        band_v = band_d.rearrange("p (b t) -> p b t", t=128)
        for j in range(3):
            nb = 8 if j < 2 else 4
            load_cast(band[:, j * 8:j * 8 + nb, :], band_v[:, j * 8:j * 8 + nb, :], [128, nb, 128], R_band)
        for h in range(4):
            P.op("pool", lambda e, h=h: e.memset(Qpad[:, h, :], 0.0), writes=[R_Qpad[h]])

        def q_proj_1(g, h):
            for kc in range(8):
                P.op("pe", lambda e, kc=kc, h=h: e.matmul(PS[5][:, 0:G], wq[:, kc, h * 128:(h + 1) * 128], hT[:, kc, :],
                                                          start=(kc == 0), stop=(kc == 7)),
                     reads=[R_wq[h], R_hT], writes=[PSR[5]])
            P.op("dve", lambda e: e.tensor_copy(qf[:, :], PS[5][:, 0:G]), reads=[PSR[5]], writes=[R_qf])

        def q_proj_2(g, h):
            s = g % 2
            P.op("pe", lambda e: e.matmul(PS[6][:, 0:G], ropeP_f[:, :], qf[:, :], start=True, stop=True),
                 reads=[R_ropeP, R_qf], writes=[PSR[6]])
            P.op("dve", lambda e, s=s: e.tensor_tensor(q_t1[:, :], PS[6][:, 0:G], rope3_sb[s][:, 1, :], ALU.mult),
                 reads=[PSR[6], R_rope3[s]], writes=[R_qt1])
            P.op("dve", lambda e, s=s: e.tensor_tensor(q_t2[:, :], qf[:, :], rope3_sb[s][:, 0, :], ALU.mult),
                 reads=[R_qf, R_rope3[s]], writes=[R_qt2])
            P.op("dve", lambda e, h=h: e.tensor_tensor(Qpad[0:64, h, 0:G], q_t1[0:64, :], q_t2[0:64, :], ALU.add),
                 reads=[R_qt1, R_qt2], writes=[R_Qpad[h]])
            P.op("dve", lambda e, h=h: e.tensor_tensor(Qpad[64:128, h, G:2 * G], q_t1[64:128, :], q_t2[64:128, :], ALU.add),
                 reads=[R_qt1, R_qt2], writes=[R_Qpad[h]])

        def epi_A():
            P.op("dve", lambda e: e.tensor_copy(o0[:, :], PS[2][:, 0:G]), reads=[PSR[2]], writes=[R_o0])
            P.op("dve", lambda e: e.tensor_copy(o1[:, :], PS[3][:, 0:G]), reads=[PSR[3]], writes=[R_o1])
            P.op("dve", lambda e: e.reciprocal(Rz[:, :], PS[4][:, :]), reads=[PSR[4]], writes=[R_Rz])

        def epi_B1():
            P.op("dve", lambda e: e.tensor_tensor(o0[:, :], o0[:, :], Rz[:, 0:G], ALU.mult),
                 reads=[R_o0, R_Rz], writes=[R_o0])
            P.op("dve", lambda e: e.tensor_tensor(o1[:, :], o1[:, :], Rz[:, G:2 * G], ALU.mult),
                 reads=[R_o1, R_Rz], writes=[R_o1])
            P.op("dve", lambda e: e.scalar_tensor_tensor(o0[:, :], o1[:, :], neglam[:, 0:1], o0[:, :], ALU.mult, ALU.add),
                 reads=[R_o0, R_o1, R_neglam], writes=[R_o0])
            P.op("dve", lambda e: e.tensor_tensor(osq[:, :], o0[:, :], o0[:, :], ALU.mult), reads=[R_o0], writes=[R_osq])

        def epi_B2(h):
            P.op("pe", lambda e: e.matmul(PS[5][:, 0:G], onesdiv_f[:, :], osq[:, :], start=True, stop=True),
                 reads=[R_onesdiv, R_osq], writes=[PSR[5]])
            P.op("act", lambda e: e.activation(rr[:, :], PS[5][:, 0:G], AF.Ln, bias=EPS), reads=[PSR[5]], writes=[R_rr])
            P.op("act", lambda e: e.activation(rr[:, :], rr[:, :], AF.Exp, scale=-0.5), reads=[R_rr], writes=[R_rr])
            P.op("dve", lambda e, h=h: e.scalar_tensor_tensor(attT[:, h, :], o0[:, :], gsubs[:, 0:1], rr[:, :], ALU.mult, ALU.mult),
                 reads=[R_o0, R_gsubs, R_rr], writes=[R_attT[h]])

        def pool_1(g, gi):
            for tt in range(2):
                ti = g * 2 + tt
                rs = [r for r in (-1, 0, 1) if 0 <= ti + r < NT]
                for n_, r in enumerate(rs):
                    blk = r + 1
                    if r == 0 and ti == 0:
                        blk = 3
                    if r == 0 and ti == NT - 1:
                        blk = 4
                    P.op("pe", lambda e, gi=gi, tt=tt, ti=ti, r=r, blk=blk, n_=n_, last=(n_ == len(rs) - 1):
                         e.matmul(PS[6][:, tt * 128:(tt + 1) * 128], Ut[:, ti + r, gi * 128:(gi + 1) * 128],
                                  band[:, gi * 5 + blk, :], start=(n_ == 0), stop=last),
                         reads=[R_U[ti + r], R_band], writes=[PSR[6]])
            P.op("dve", lambda e: e.tensor_copy(dT[:, :], PS[6][:, 0:G]), reads=[PSR[6]], writes=[R_dT])

        def pool_2(g, gi):
            P.op("pe", lambda e, gi=gi: e.matmul(PS[5][:, 0:G], wpl[:, gi, :], dT[:, :], start=True, stop=True),
                 reads=[R_wpl, R_dT], writes=[PSR[5]])
            P.op("dve", lambda e, gi=gi: e.tensor_scalar(attT[:, 4 + gi, :], PS[5][:, 0:G], psT[:, gi:gi + 1], None, ALU.mult),
                 reads=[PSR[5], R_psT], writes=[R_attT[4 + gi]])

        def out_proj(g, tt):
            sl = slots3(g)[tt]
            for dh in range(2):
                b = 5 + dh
                for c in range(8):
                    P.op("pe", lambda e, b=b, c=c, tt=tt, dh=dh: e.matmul(PS[b][:, :], attT[:, c, tt * 128:(tt + 1) * 128],
                                                                          wo[:, c, dh * 512:(dh + 1) * 512],
                                                                          start=(c == 0), stop=(c == 7)),
                         reads=[R_attT[c], R_wo[c]], writes=[PSR[b]])
                tb = tmpf[dh]
                P.op("dve", lambda e, b=b, dh=dh, tb=tb: e.tensor_tensor(tb[:, :], PS[b][:, :], gt2_bc[:, dh * 512:(dh + 1) * 512], ALU.mult),
                     reads=[PSR[b], R_gt2], writes=[R_tmpf[dh]])
                P.op("dve", lambda e, sl=sl, dh=dh, tb=tb: e.tensor_tensor(xt[sl][:, dh * 512:(dh + 1) * 512], tb[:, :],
                                                                          xt[sl][:, dh * 512:(dh + 1) * 512], ALU.add),
                     reads=[R_tmpf[dh], R_xt[sl]], writes=[R_xt[sl]])
            r0 = g * G + tt * 128
            P.dma("xt%d" % sl, x1s_d[r0:r0 + 128, :], xt[sl][:, :], reads=[R_xt[sl]], writes=[R_x1s[g * 2 + tt]])
            if tt == 1 and g + 2 < NG_LAT:
                load_x1_group3(g + 2)
                load_rope3(g + 2)

        def head_loop(h, hooks):
            def QK(kt):
                b = SBK[kt % 3]
                kg = kt // 2
                for m in range(2):
                    P.op("pe", lambda e, b=b, m=m, kt=kt: e.matmul(PS[b][:, m * G:(m + 1) * G], KT[:, h, kt * 128:(kt + 1) * 128],
                                                                   Qpad[:, h, m * G:(m + 1) * G], start=True, stop=True),
                         reads=[R_KT[kg], R_Qpad[h]], writes=[PSR[b]])
                P.op("act", lambda e, b=b, kt=kt: e.activation(Et[kt % 3][:, :], PS[b][:, :], AF.Exp, scale=0.125),
                     reads=[PSR[b]], writes=[R_E[kt % 3]])

            def AVZ(kt):
                eb = kt % 3
                for m in range(2):
                    P.op("pe", lambda e, eb=eb, m=m, kt=kt: e.matmul(PS[2 + m][:, 0:G], Vt[:, kt, h * 128:(h + 1) * 128],
                                                                     Et[eb][:, m * G:(m + 1) * G], start=(kt == 0), stop=(kt == NKT - 1)),
                         reads=[R_V[kt], R_E[eb]], writes=[PSR[2 + m]])
                P.op("pe", lambda e, eb=eb, kt=kt: e.matmul(PS[4][:, :], ones_bf[:, :], Et[eb][:, :],
                                                            start=(kt == 0), stop=(kt == NKT - 1)),
                     reads=[R_ones, R_E[eb]], writes=[PSR[4]])

            QK(0)
            QK(1)
            for kt in range(NKT):
                if kt + 2 < NKT:
                    QK(kt + 2)
                AVZ(kt)
                for fn in hooks.get(kt, ()):
                    fn()

        prep(2, slots3(0), 0, 1, hT, R_hT, banks=(5, 6))
        for h in range(4):
            q_proj_1(0, h)
            q_proj_2(0, h)

        from functools import partial as _p
        for g in range(NG_LAT):
            for h in range(4):
                hooks = {}

                def add(kt, fn):
                    hooks.setdefault(kt, []).append(fn)

                if h > 0 or g > 0:
                    ph = (h - 1) % 4
                    add(1, epi_B1)
                    add(4, _p(epi_B2, ph))
                if h == 0 and g > 0:
                    add(6, _p(q_proj_1, g, 3))
                    add(8, _p(q_proj_2, g, 3))
                    for gi in range(4):
                        add(10 + 3 * gi, _p(pool_1, g - 1, gi))
                        add(12 + 3 * gi, _p(pool_2, g - 1, gi))
                    add(24, _p(out_proj, g - 1, 0))
                    add(29, _p(out_proj, g - 1, 1))
                if h == 3 and g + 1 < NG_LAT:
                    its = prep_items(slots3(g + 1), 0, 1, hT, R_hT, (5, 6))
                    for n_, kt_ in enumerate((5, 6, 7, 8, 10, 11, 12, 13, 14, 15)):
                        add(kt_, its[n_])
                    for hh in range(3):
                        add(17 + 5 * hh, _p(q_proj_1, g + 1, hh))
                        add(19 + 5 * hh, _p(q_proj_2, g + 1, hh))
                head_loop(h, hooks)
                epi_A()
        epi_B1()
        epi_B2(3)
        for gi in range(4):
            pool_1(NG_LAT - 1, gi)
            pool_2(NG_LAT - 1, gi)
        out_proj(NG_LAT - 1, 0)
        out_proj(NG_LAT - 1, 1)
        P.barrier()
        A.off = p23_mark

    if phases >= 4:
        ffn_phase(2)

    P.emit()
    return nc, A.peak


def _consts():
    ident = np.eye(128, dtype=np.float32)
    Pm = np.zeros((128, 128), np.float32)
    for dst in range(128):
        if dst % 32 < 16:
            Pm[dst, dst + 16] = -1.0
        else:
            Pm[dst, dst - 16] = 1.0
    ropeP = np.ascontiguousarray(Pm.T)
    rows = SEQ // 64
    row = np.repeat(np.arange(rows, dtype=np.float32), 64)
    col = np.tile(np.arange(64, dtype=np.float32), rows)
    nf = 16
    freqs = (np.float32(10000.0) ** (-np.arange(nf, dtype=np.float32) / np.float32(nf))).astype(np.float32)
    ar = row[:, None] * freqs
    ac = col[:, None] * freqs
    ang = np.concatenate([ar, ar, ac, ac], axis=-1).astype(np.float32)
    cos = np.cos(ang).astype(np.float32).T
    sin = np.sin(ang).astype(np.float32).T
    cos128 = np.ascontiguousarray(np.concatenate([cos, cos], axis=0))
    sin128 = np.ascontiguousarray(np.concatenate([sin, sin], axis=0))
    n = SEQ
    band = np.zeros((128, 20, 128), np.float32)

    def mval(w, t, s):
        lo = w // 2
        hi = w - w // 2 - 1
        st = max(t - lo, 0)
        en = min(t + hi, n - 1)
        v = 0.0
        if st <= s <= en:
            v = 1.0 / (en - st + 1)
        if s == t:
            v -= 1.0
        return v

    for gi, w in enumerate(POOL_W):
        for blk, (ti, r) in enumerate([(5, -1), (5, 0), (5, 1), (0, 0), (n // 128 - 1, 0)]):
            for tp in range(128):
                t = ti * 128 + tp
                for s in range(max(0, t - 17), min(n, t + 17)):
                    sp = s - (ti + r) * 128
                    if 0 <= sp < 128:
                        band[sp, gi * 5 + blk, tp] = mval(w, t, s)
    return dict(c_ident=ident, c_ropeP=ropeP, c_cos=cos128, c_sin=sin128,
                c_band=np.ascontiguousarray(band.reshape(128, 20 * 128)))


_CACHE = {}


def _get_nc(debug=False, phases=4):
    key = (debug, phases)
    if key not in _CACHE:
        _CACHE[key] = build_nc(debug=debug, phases=phases)[0]
    return _CACHE[key]


def make_in_maps(inputs):
    f = lambda a: np.ascontiguousarray(np.asarray(a, dtype=np.float32))
    consts = _consts()
    shared = dict(
        w_mod=f(inputs["w_mod"][0]), b_mod=f(inputs["b_mod"][0]), g_ffn1=f(inputs["g_ffn1"][0]),
        ffn1_w_in=f(inputs["ffn1_w_in"][0]), ffn1_w_out=f(inputs["ffn1_w_out"][0]), g_mix=f(inputs["g_mix"][0]),
        w_in=f(inputs["w_in"][0]),
        lam4=f(np.concatenate([np.asarray(inputs["lambda_q1"][0]), np.asarray(inputs["lambda_k1"][0]),
                               np.asarray(inputs["lambda_q2"][0]), np.asarray(inputs["lambda_k2"][0])])),
        g_sub=f(inputs["g_sub"][0]), w_pool=f(inputs["w_pool"][0]), pool_scale=f(inputs["pool_scale"][0]),
        w_out=f(inputs["w_out"][0]), g_ffn2=f(inputs["g_ffn2"][0]), ffn2_w_in=f(inputs["ffn2_w_in"][0]),
        ffn2_w_out=f(inputs["ffn2_w_out"][0]), g_final=f(inputs["g_final"]),
    )
    shared.update(consts)
    x = np.asarray(inputs["x"], dtype=np.float32)
    c = np.asarray(inputs["c"], dtype=np.float32)
    ctx = np.asarray(inputs["ctx"], dtype=np.float32)
    c_ctx = np.asarray(inputs["c_ctx"], dtype=np.float32)
    maps = []
    for b in range(8):
        m = dict(shared)
        m["x"] = np.ascontiguousarray(x[b])
        m["ctx"] = np.ascontiguousarray(ctx[b])
        m["cvec"] = np.ascontiguousarray(np.stack([c[b], c_ctx], axis=0))
        maps.append(m)
    return maps


def kernel(**inputs):
    nc = _get_nc()
    in_maps = make_in_maps(inputs)
    res = run_bass_kernel_spmd(nc, in_maps, core_ids=list(range(8)))
    return np.stack([np.asarray(r["out"], dtype=np.float32) for r in res.results], axis=0)
```

```python
import math
from contextlib import ExitStack
import numpy as np
import concourse.bass as bass
import concourse.mybir as mybir
from concourse.bass_utils import run_bass_kernel_spmd

F32 = mybir.dt.float32
BF16 = mybir.dt.bfloat16
ALU = mybir.AluOpType
AF = mybir.ActivationFunctionType

D = 1024
SEQ = 4096
CTX = 256
NTOK = SEQ + CTX
DFF = 2816
NF = DFF // 128
EPS = 1e-6
LAM_INIT = 0.8 - 0.6 * math.exp(-0.3 * 0)
POOL_W = (2, 4, 8, 16)
G = 256
NG_LAT = SEQ // G
NKT = NTOK // 128

QATTR = {"pe": "tensor", "act": "scalar", "dve": "vector", "pool": "gpsimd", "sp": "sync"}
COMPUTE = ("pe", "act", "dve", "pool")


class Res:
    __slots__ = ("name", "lw", "rs")

    def __init__(self, name):
        self.name = name
        self.lw = None
        self.rs = []


class Op:
    __slots__ = ("q", "stream", "fn", "deps", "signal", "ms", "is_dma")


class Prog:
    def __init__(self, nc):
        self.nc = nc
        self.queues = {q: [] for q in QATTR}
        self.dma_count = {}
        self.last_dma = {}

    def _track(self, op, reads, writes):
        deps = []
        for r in reads:
            if r.lw is not None:
                deps.append(r.lw)
        for w in writes:
            if w.lw is not None:
                deps.append(w.lw)
            deps.extend(w.rs)
        out = []
        seen = set()
        for d in deps:
            if id(d) in seen or d is op:
                continue
            seen.add(id(d))
            if (not op.is_dma) and (not d.is_dma) and d.q == op.q and op.q == "pe":
                continue
            d.signal = True
            out.append(d)
        op.deps = out
        for r in reads:
            r.rs.append(op)
        for w in writes:
            w.lw = op
            w.rs = []

    def op(self, q, fn, reads=(), writes=()):
        o = Op()
        o.q = q
        o.stream = q
        o.fn = fn
        o.signal = False
        o.ms = None
        o.is_dma = False
        self._track(o, reads, writes)
        self.queues[q].append(o)
        return o

    def dma(self, stream, out, in_, reads=(), writes=(), q="sp", **kw):
        o = Op()
        o.q = q
        o.stream = "d_" + stream
        o.fn = lambda eng, out=out, in_=in_, kw=kw: eng.dma_start(out=out, in_=in_, **kw)
        o.signal = True
        o.is_dma = True
        n = self.dma_count.get(o.stream, 0) + 1
        self.dma_count[o.stream] = n
        o.ms = 16 * n
        self._track(o, reads, writes)
        self.queues[q].append(o)
        self.last_dma[o.stream] = o
        return o

    def barrier(self):
        lasts = []
        for q in COMPUTE:
            for o in reversed(self.queues[q]):
                if not o.is_dma and o.fn is not None:
                    lasts.append(o)
                    break
        lasts.extend(self.last_dma.values())
        for q in QATTR:
            o = Op()
            o.q = q
            o.stream = q
            o.fn = None
            o.signal = False
            o.ms = None
            o.is_dma = False
            o.deps = []
            for d in lasts:
                if d.q == q and not d.is_dma:
                    continue
                d.signal = True
                o.deps.append(d)
            self.queues[q].append(o)

    def emit(self):
        nc = self.nc
        for q in COMPUTE:
            c = 0
            for o in self.queues[q]:
                if o.is_dma or o.fn is None:
                    continue
                if o.signal:
                    c += 1
                    o.ms = c
        with ExitStack() as es:
            sems = {}
            for q in COMPUTE:
                sems[q] = es.enter_context(nc.semaphore("s_" + q))
            for s in self.dma_count:
                sems[s] = es.enter_context(nc.semaphore("s_" + s))
            block = es.enter_context(nc.Block())
            for q, attr in QATTR.items():
                ops = self.queues[q]
                final_waits = []
                if q == "sp":
                    final_waits = [(s, 16 * n) for s, n in self.dma_count.items()]

                def section(eng, ops=ops, final_waits=final_waits):
                    seen = {}
                    for o in ops:
                        for d in o.deps:
                            if seen.get(d.stream, 0) >= d.ms:
                                continue
                            seen[d.stream] = d.ms
                            eng.wait_ge(sems[d.stream], d.ms)
                        if o.fn is None:
                            continue
                        ins = o.fn(eng)
                        if o.is_dma:
                            ins.then_inc(sems[o.stream], 16)
                        elif o.signal:
                            ins.then_inc(sems[o.stream], 1)
                    for s, v in final_waits:
                        eng.wait_ge(sems[s], v)

                getattr(block, attr)(section)


class Arena:
    def __init__(self, nc, nbytes):
        self.t = nc.alloc_sbuf_tensor("arena", [128, nbytes // 4], F32)
        self.cap = nbytes
        self.off = 0
        self.peak = 0

    def alloc(self, shape, dt):
        ne = int(np.prod(shape[1:]))
        n = ne * (4 if dt == F32 else 2)
        n = (n + 63) // 64 * 64
        assert self.off + n <= self.cap, ("SBUF arena overflow", self.off, n, self.cap)
        v = self.t[0:shape[0], self.off // 4:(self.off + n) // 4]
        if dt != F32:
            v = v.bitcast(dt)
        v = v[:, 0:ne]
        if len(shape) == 3:
            v = v.rearrange("p (a b) -> p a b", a=shape[1])
        elif len(shape) == 4:
            v = v.rearrange("p (a b c) -> p a b c", a=shape[1], b=shape[2])
        self.off += n
        self.peak = max(self.peak, self.off)
        return v


def build_nc(debug=False, phases=4):
    nc = bass.Bass("TRN2", target_bir_lowering=False)

    def din(name, shape):
        return nc.dram_tensor(name, list(shape), F32, kind="ExternalInput").ap()

    x_d = din("x", [SEQ, D])
    ctx_d = din("ctx", [CTX, D])
    cvec_d = din("cvec", [2, D])
    wmod_d = din("w_mod", [D, 9 * D])
    bmod_d = din("b_mod", [9 * D])
    g1_d = din("g_ffn1", [D])
    f1in_d = din("ffn1_w_in", [D, 2 * DFF])
    f1out_d = din("ffn1_w_out", [DFF, D])
    gmix_d = din("g_mix", [D])
    win_d = din("w_in", [D, 2048])
    lam_d = din("lam4", [4 * 64])
    gsub_d = din("g_sub", [128])
    wpool_d = din("w_pool", [4, 128, 128])
    pscale_d = din("pool_scale", [512])
    wout_d = din("w_out", [D, D])
    g2_d = din("g_ffn2", [D])
    f2in_d = din("ffn2_w_in", [D, 2 * DFF])
    f2out_d = din("ffn2_w_out", [DFF, D])
    gfin_d = din("g_final", [D])
    ident_d = din("c_ident", [128, 128])
    ropeP_d = din("c_ropeP", [128, 128])
    cos_d = din("c_cos", [128, SEQ])
    sin_d = din("c_sin", [128, SEQ])
    band_d = din("c_band", [128, 20 * 128])
    out_d = nc.dram_tensor("out", [SEQ, D], F32, kind="ExternalOutput").ap()
    x1s_d = nc.dram_tensor("x1s", [NTOK, D], F32,
                           kind="ExternalOutput" if debug else "Internal").ap()
    mscr_d = nc.dram_tensor("mscr", [2, 9 * D], F32,
                            kind="ExternalOutput" if debug else "Internal").ap()

    P = Prog(nc)
    cap = nc.sbuf_bytes_remaining - 256
    cap = cap // 64 * 64
    A = Arena(nc, cap)
    PS = [nc.alloc_psum_tensor("ps%d" % i, [128, 512], F32) for i in range(8)]
    PSR = [Res("ps%d" % i) for i in range(8)]
    PSB = [PS[i][:, :].bitcast(BF16) for i in range(8)]

    ident_bf = A.alloc([128, 128], BF16)
    ones_bf = A.alloc([128, 128], BF16)
    onesdiv_f = A.alloc([128, 128], F32)
    ropeP_f = A.alloc([128, 128], F32)
    modT = A.alloc([128, 2, 9, 8], F32)
    gT = A.alloc([128, 3, 8], F32)
    psT = A.alloc([128, 4], F32)
    gsubs = A.alloc([128, 1], F32)
    neglam = A.alloc([128, 1], F32)
    ABt = A.alloc([128, 12, 8], F32)
    stat = A.alloc([128, 16], F32)
    xt = [A.alloc([128, D], F32) for _ in range(4)]
    xn = [A.alloc([128, D], BF16) for _ in range(2)]
    junk = A.alloc([128, D], BF16)
    hT = A.alloc([128, 8, G], BF16)
    stage = [A.alloc([128, D], F32) for _ in range(4)]
    tmpf = [A.alloc([128, 512], F32) for _ in range(2)]
    R_ident, R_ones, R_onesdiv, R_ropeP = Res("ident"), Res("ones"), Res("onesdiv"), Res("ropeP")
    R_modT, R_gT, R_psT, R_gsubs, R_neglam, R_AB, R_stat = (Res("modT"), Res("gT"), Res("psT"),
                                                           Res("gsubs"), Res("neglam"), Res("AB"), Res("stat"))
    R_xt = [Res("xt%d" % i) for i in range(4)]
    R_xn = [Res("xn%d" % i) for i in range(2)]
    R_junk = Res("junk")
    R_hT = Res("hT")
    R_stage = [Res("stage%d" % i) for i in range(4)]
    R_tmpf = [Res("tmpf%d" % i) for i in range(2)]
    R_x1s = [Res("x1s%d" % i) for i in range(NKT)]
    R_mscr = Res("mscr")
    persist_off = A.off

    stage_ctr = [0]

    R_wstream = [Res("wstream%d" % i) for i in range(8)]

    def load_cast(dst_ap, src_ap, shape3, R_dst):
        k = stage_ctr[0] % 8
        stage_ctr[0] += 1
        P.dma("wq%d" % k, dst_ap, src_ap, writes=[R_dst, R_wstream[k]], q="pool")

    P.dma("c_rp", ropeP_f[:, :], ropeP_d, writes=[R_ropeP])
    load_cast(ident_bf[:, :], ident_d, [128, 128], R_ident)
    P.op("pool", lambda e: e.memset(ones_bf[:, :], 1.0), writes=[R_ones])
    P.op("pool", lambda e: e.memset(onesdiv_f[:, :], 1.0 / 128), writes=[R_onesdiv])
    P.dma("c_g", gT[:, 0, :], g1_d.rearrange("(c p) -> p c", p=128), writes=[R_gT], allow_slow_non_contiguous=True)
    P.dma("c_g", gT[:, 1, :], gmix_d.rearrange("(c p) -> p c", p=128), writes=[R_gT], allow_slow_non_contiguous=True)
    P.dma("c_g", gT[:, 2, :], g2_d.rearrange("(c p) -> p c", p=128), writes=[R_gT], allow_slow_non_contiguous=True)
    P.dma("c_ps", psT[:, :], pscale_d.rearrange("(c p) -> p c", p=128), writes=[R_psT], allow_slow_non_contiguous=True)
    P.dma("c_gs", gsubs[:, :], gsub_d.rearrange("(p o) -> p o", o=1), writes=[R_gsubs])
    P.op("dve", lambda e: e.tensor_scalar(gsubs[:, :], gsubs[:, :], 1.0 - LAM_INIT, None, ALU.mult),
         reads=[R_gsubs], writes=[R_gsubs])

    lamt = tmpf[0][:, 0:256]
    P.dma("c_lam", lamt, lam_d.partition_broadcast(128), writes=[R_tmpf[0]])
    P.op("dve", lambda e: e.scalar_tensor_tensor(junk[:, 0:64], lamt[:, 0:64], 1.0, lamt[:, 64:128], ALU.mult, ALU.mult,
                                                 accum_out=stat[:, 0:1]), reads=[R_tmpf[0]], writes=[R_stat, R_junk])
    P.op("dve", lambda e: e.scalar_tensor_tensor(junk[:, 0:64], lamt[:, 128:192], 1.0, lamt[:, 192:256], ALU.mult, ALU.mult,
                                                 accum_out=stat[:, 1:2]), reads=[R_tmpf[0], R_junk, R_stat], writes=[R_stat, R_junk])
    P.op("act", lambda e: e.activation(stat[:, 2:4], stat[:, 0:2], AF.Exp), reads=[R_stat], writes=[R_stat])
    P.op("dve", lambda e: e.tensor_tensor(neglam[:, :], stat[:, 3:4], stat[:, 2:3], ALU.subtract),
         reads=[R_stat], writes=[R_neglam])
    P.op("dve", lambda e: e.tensor_scalar(neglam[:, :], neglam[:, :], -LAM_INIT, None, ALU.add),
         reads=[R_neglam], writes=[R_neglam])

    craw = tmpf[1][:, 16:32].rearrange("p (k c) -> p k c", k=2)
    P.dma("c_c", craw[:, 0, :], cvec_d[0, :].rearrange("(c p) -> p c", p=128), writes=[R_tmpf[1]], allow_slow_non_contiguous=True)
    P.dma("c_c2", craw[:, 1, :], cvec_d[1, :].rearrange("(c p) -> p c", p=128), writes=[R_tmpf[1]], allow_slow_non_contiguous=True)

    p0_mark = A.off
    scT = A.alloc([128, 8, 128], F32)
    R_scT = Res("scT")
    P.op("pool", lambda e: e.memset(scT[:, :, :], 0.0), writes=[R_scT])
    for k in range(2):
        P.op("act", lambda e, k=k: e.activation(scT[:, :, k], craw[:, k, :], AF.Silu), reads=[R_tmpf[1], R_scT], writes=[R_scT])
    bm2 = A.alloc([2, 9 * D], F32)
    msb = A.alloc([2, 3 * D], F32)
    wms = [A.alloc([128, 3 * D], F32) for _ in range(3)]
    R_bm2, R_msb = Res("bm2"), Res("msb")
    R_wms = [Res("wms%d" % i) for i in range(3)]
    for k in range(2):
        P.dma("c_bm%d" % k, bm2[k:k + 1, :], bmod_d.rearrange("(o n) -> o n", o=1), writes=[R_bm2])
    wm_ctr = 0
    for gq in range(3):
        for kc in range(8):
            s = wm_ctr % 3
            wm_ctr += 1
            P.dma("wms%d" % s, wms[s][:, :], wmod_d[kc * 128:(kc + 1) * 128, gq * 3 * D:(gq + 1) * 3 * D],
                  writes=[R_wms[s]])
            for j in range(6):
                P.op("pe", lambda e, j=j, s=s, kc=kc: e.matmul(PS[j][:, :], scT[:, kc, :], wms[s][:, j * 512:(j + 1) * 512],
                                                                 start=(kc == 0), stop=(kc == 7)),
                     reads=[R_scT, R_wms[s]], writes=[PSR[j]])
        for j in range(6):
            P.op("dve", lambda e, j=j, gq=gq: e.tensor_tensor(msb[:, j * 512:(j + 1) * 512], PS[j][0:2, :],
                                                             bm2[:, gq * 3 * D + j * 512: gq * 3 * D + (j + 1) * 512], ALU.add),
                 reads=[PSR[j], R_bm2], writes=[R_msb])
        P.dma("mscr_w", mscr_d[:, gq * 3 * D:(gq + 1) * 3 * D], msb[:, :], reads=[R_msb], writes=[R_mscr])
        for k in range(2):
            P.dma("modT%d" % k, modT[:, k, 3 * gq:3 * gq + 3, :],
                  mscr_d[k, gq * 3 * D:(gq + 1) * 3 * D].rearrange("(v c p) -> p v c", p=128, c=8),
                  reads=[R_mscr], writes=[R_modT], allow_slow_non_contiguous=True)
        for k in range(2):
            ia = (k * 3 + gq) * 2
            P.op("dve", lambda e, k=k, gq=gq, ia=ia: e.scalar_tensor_tensor(ABt[:, ia, :], modT[:, k, 3 * gq + 1, :], 1.0,
                                                                           gT[:, gq, :], ALU.add, ALU.mult),
                 reads=[R_modT, R_gT], writes=[R_AB])
            P.op("dve", lambda e, k=k, gq=gq, ia=ia: e.tensor_copy(ABt[:, ia + 1, :], modT[:, k, 3 * gq, :]),
                 reads=[R_modT], writes=[R_AB])
    P.barrier()
    A.off = p0_mark

    def prep_items(slots, kind, s, hTv, R_hTv, banks=(6, 7)):
        ia = (kind * 3 + s) * 2
        items = []

        def stats(tt):
            sl = slots[tt]
            xs = xt[sl]
            c0 = 4 + 3 * tt
            P.op("dve", lambda e, xs=xs, c0=c0: e.scalar_tensor_tensor(junk[:, :], xs[:, :], 1.0, xs[:, :], ALU.mult, ALU.mult,
                                                                      accum_out=stat[:, c0:c0 + 1]),
                 reads=[R_xt[sl]], writes=[R_junk, R_stat])
            P.op("act", lambda e, c0=c0: e.activation(stat[:, c0 + 1:c0 + 2], stat[:, c0:c0 + 1], AF.Ln, bias=EPS, scale=1.0 / D),
                 reads=[R_stat], writes=[R_stat])
            P.op("act", lambda e, c0=c0: e.activation(stat[:, c0 + 2:c0 + 3], stat[:, c0 + 1:c0 + 2], AF.Exp, scale=-0.5),
                 reads=[R_stat], writes=[R_stat])

        def norm(tt):
            sl = slots[tt]
            xs = xt[sl]
            c0 = 4 + 3 * tt
            P.op("dve", lambda e, xs=xs, c0=c0, tt=tt: e.tensor_scalar(xn[tt][:, :], xs[:, :], stat[:, c0 + 2:c0 + 3], None, ALU.mult),
                 reads=[R_xt[sl], R_stat], writes=[R_xn[tt]])

        def transp(tt):
            bk = banks[tt]
            for c in range(8):
                P.op("pe", lambda e, c=c, tt=tt, bk=bk: e.transpose(PSB[bk][:, c * 128:(c + 1) * 128], xn[tt][:, c * 128:(c + 1) * 128],
                                                                    ident_bf[:, :]),
                     reads=[R_xn[tt], R_ident], writes=[PSR[bk]])

        def evac(tt, c0_, c1_):
            bk = banks[tt]
            for c in range(c0_, c1_):
                P.op("dve", lambda e, c=c, tt=tt, bk=bk: e.tensor_scalar(hTv[:, c, tt * 128:(tt + 1) * 128],
                                                                        PSB[bk][:, c * 128:(c + 1) * 128],
                                                                        ABt[:, ia, c:c + 1], ABt[:, ia + 1, c:c + 1],
                                                                        ALU.mult, ALU.add),
                     reads=[PSR[bk], R_AB], writes=[R_hTv])

        from functools import partial as _pp
        items = [_pp(stats, 0), _pp(norm, 0), _pp(stats, 1), _pp(norm, 1),
                 _pp(transp, 0), _pp(evac, 0, 0, 4), _pp(evac, 0, 4, 8),
                 _pp(transp, 1), _pp(evac, 1, 0, 4), _pp(evac, 1, 4, 8)]
        return items

    def prep(ntile, slots, kind, s, hTv, R_hTv=None, banks=(6, 7)):
        for it in prep_items(slots, kind, s, hTv, R_hT if R_hTv is None else R_hTv, banks):
            it()

    def tok_rows(g, tt):
        r0 = g * G + tt * 128
        return r0

    def ffn_phase(which):
        fin_d, fout_d = (f1in_d, f1out_d) if which == 1 else (f2in_d, f2out_d)
        s_idx = 0 if which == 1 else 2
        groups = list(range(NG_LAT + 1)) if which == 1 else list(range(NG_LAT))
        mark = A.off
        w1 = A.alloc([128, 8, 2 * DFF], BF16)
        w2 = A.alloc([128, NF, D], BF16)
        gate_bc = [A.alloc([128, D], F32) for _ in range(2 if which == 1 else 1)]
        gfin_bc = A.alloc([128, D], F32) if which == 2 else None
        sg = [A.alloc([128, G], F32) for _ in range(2)]
        aT = [A.alloc([128, G], BF16) for _ in range(2)]
        hT2 = A.alloc([128, 8, G], BF16)
        hTs = [hT, hT2]
        R_hTs = [R_hT, Res("hT2")]
        R_w1g = [Res("w1g%d" % f) for f in range(NF)]
        R_w1u = [Res("w1u%d" % f) for f in range(NF)]
        R_w2 = [Res("w2_%d" % f) for f in range(NF)]
        R_gbc = [Res("gbc0"), Res("gbc1")]
        R_gfin = Res("gfin")
        R_sg = [Res("sg0"), Res("sg1")]
        R_aT = [Res("aT0"), Res("aT1")]

        def src_rows(g, tt):
            if which == 1:
                if g < NG_LAT:
                    return x_d[g * G + tt * 128: g * G + (tt + 1) * 128, :]
                return ctx_d[tt * 128:(tt + 1) * 128, :]
            r0 = g * G + tt * 128
            return x1s_d[r0:r0 + 128, :]

        def load_group(g):
            for tt in range(2):
                sl = (g % 2) * 2 + tt
                rds = [R_x1s[g * 2 + tt]] if which == 2 else []
                P.dma("xt%d" % sl, xt[sl][:, :], src_rows(g, tt), reads=rds, writes=[R_xt[sl]])

        v_gate = 3 * s_idx + 2
        P.dma("gbc0", gate_bc[0][:, :], mscr_d[0, v_gate * D:(v_gate + 1) * D].partition_broadcast(128),
              reads=[R_mscr], writes=[R_gbc[0]])
        P.op("dve", lambda e: e.tensor_scalar(gate_bc[0][:, :], gate_bc[0][:, :], 0.5, None, ALU.mult),
             reads=[R_gbc[0]], writes=[R_gbc[0]])
        if which == 1:
            P.dma("gbc1", gate_bc[1][:, :], mscr_d[1, v_gate * D:(v_gate + 1) * D].partition_broadcast(128),
                  reads=[R_mscr], writes=[R_gbc[1]])
            P.op("dve", lambda e: e.tensor_scalar(gate_bc[1][:, :], gate_bc[1][:, :], 0.5, None, ALU.mult),
                 reads=[R_gbc[1]], writes=[R_gbc[1]])
        else:
            P.dma("gfin", gfin_bc[:, :], gfin_d.partition_broadcast(128), writes=[R_gfin])

        load_group(groups[0])
        load_group(groups[1])
        fin_v = fin_d.rearrange("(kc p) n -> p kc n", p=128)
        for f in range(NF):
            load_cast(w1[:, :, f * 128:(f + 1) * 128], fin_v[:, :, f * 128:(f + 1) * 128], [128, 8, 128], R_w1g[f])
            load_cast(w1[:, :, DFF + f * 128:DFF + (f + 1) * 128], fin_v[:, :, DFF + f * 128:DFF + (f + 1) * 128],
                      [128, 8, 128], R_w1u[f])
            load_cast(w2[:, f, :], fout_d[f * 128:(f + 1) * 128, :], [128, D], R_w2[f])

        PREP_AT = {5: 0, 6: 1, 7: 2, 8: 3, 10: 4, 11: 5, 12: 6, 13: 7, 14: 8, 15: 9}
        pending_tail = [None]
        tmp4 = [tmpf[0], tmpf[1], A.alloc([128, 512], F32), A.alloc([128, 512], F32)]
        R_tmp4 = [R_tmpf[0], R_tmpf[1], Res("tmp4_2"), Res("tmp4_3")]

        def kind_of(g):
            return 1 if (which == 1 and g == NG_LAT) else 0

        def slots_of(g):
            return [(g % 2) * 2, (g % 2) * 2 + 1]

        prep(2, slots_of(groups[0]), kind_of(groups[0]), s_idx, hTs[0], R_hTs[0])
        for gi_, g in enumerate(groups):
            kind = kind_of(g)
            slots = slots_of(g)
            hTc = hTs[gi_ % 2]
            R_hTc = R_hTs[gi_ % 2]
            nxt = groups[gi_ + 1] if gi_ + 1 < len(groups) else None

            def GU(f, hTc=hTc, R_hTc=R_hTc):
                b = 4 + (f % 2)
                for kc in range(8):
                    P.op("pe", lambda e, b=b, kc=kc, f=f: e.matmul(PS[b][:, 0:G], w1[:, kc, f * 128:(f + 1) * 128], hTc[:, kc, :],
                                                                   start=(kc == 0), stop=(kc == 7)),
                         reads=[R_w1g[f], R_hTc], writes=[PSR[b]])
                for kc in range(8):
                    P.op("pe", lambda e, b=b, kc=kc, f=f: e.matmul(PS[b][:, G:2 * G], w1[:, kc, DFF + f * 128:DFF + (f + 1) * 128],
                                                                   hTc[:, kc, :], start=(kc == 0), stop=(kc == 7)),
                         reads=[R_w1u[f], R_hTc], writes=[PSR[b]])
                P.op("act", lambda e, b=b, f=f: e.activation(sg[f % 2][:, :], PS[b][:, 0:G], AF.Silu),
                     reads=[PSR[b]], writes=[R_sg[f % 2]])
                P.op("dve", lambda e, b=b, f=f: e.tensor_tensor(aT[f % 2][:, :], PS[b][:, G:2 * G], sg[f % 2][:, :], ALU.mult),
                     reads=[PSR[b], R_sg[f % 2]], writes=[R_aT[f % 2]])

            def OUT(f):
                for tt in range(2):
                    for dh in range(2):
                        b = tt * 2 + dh
                        P.op("pe", lambda e, b=b, tt=tt, dh=dh, f=f: e.matmul(PS[b][:, :], aT[f % 2][:, tt * 128:(tt + 1) * 128],
                                                                              w2[:, f, dh * 512:(dh + 1) * 512],
                                                                              start=(f == 0), stop=(f == NF - 1)),
                             reads=[R_aT[f % 2], R_w2[f]], writes=[PSR[b]])

            if nxt is not None:
                nitems = prep_items(slots_of(nxt), kind_of(nxt), s_idx, hTs[(gi_ + 1) % 2], R_hTs[(gi_ + 1) % 2])
            GU(0)
            for f in range(NF):
                if f + 1 < NF:
                    GU(f + 1)
                OUT(f)
                if f < 4 and pending_tail[0] is not None:
                    pending_tail[0][f]()
                if f == 3:
                    pending_tail[0] = None
                if nxt is not None and f in PREP_AT:
                    nitems[PREP_AT[f]]()
            gb = gate_bc[kind]
            for tt in range(2):
                for dh in range(2):
                    b = tt * 2 + dh
                    tb = tmp4[b]
                    P.op("dve", lambda e, b=b, dh=dh, gb=gb, tb=tb: e.tensor_tensor(tb[:, :], PS[b][:, :], gb[:, dh * 512:(dh + 1) * 512],
                                                                                     ALU.mult),
                         reads=[PSR[b], R_gbc[kind]], writes=[R_tmp4[b]])
            def mk_tail(g=g, slots=slots, gi_=gi_):
                def adds(tt):
                    sl = slots[tt]
                    for dh in range(2):
                        bb = tt * 2 + dh
                        tb = tmp4[bb]
                        P.op("dve", lambda e, sl=sl, dh=dh, tb=tb: e.tensor_tensor(xt[sl][:, dh * 512:(dh + 1) * 512], tb[:, :],
                                                                                  xt[sl][:, dh * 512:(dh + 1) * 512], ALU.add),
                             reads=[R_tmp4[bb], R_xt[sl]], writes=[R_xt[sl]])

                def fin(tt):
                    sl = slots[tt]
                    r0 = g * G + tt * 128
                    if which == 1:
                        P.dma("xt%d" % sl, x1s_d[r0:r0 + 128, :], xt[sl][:, :], reads=[R_xt[sl]], writes=[R_x1s[g * 2 + tt]])
                    else:
                        c0 = 10 + 3 * tt
                        P.op("dve", lambda e, sl=sl, c0=c0: e.scalar_tensor_tensor(junk[:, :], xt[sl][:, :], 1.0, xt[sl][:, :], ALU.mult, ALU.mult,
                                                                                  accum_out=stat[:, c0:c0 + 1]),
                             reads=[R_xt[sl]], writes=[R_junk, R_stat])
                        P.op("act", lambda e, c0=c0: e.activation(stat[:, c0 + 1:c0 + 2], stat[:, c0:c0 + 1], AF.Ln, bias=EPS, scale=1.0 / D),
                             reads=[R_stat], writes=[R_stat])
                        P.op("act", lambda e, c0=c0: e.activation(stat[:, c0 + 2:c0 + 3], stat[:, c0 + 1:c0 + 2], AF.Exp, scale=-0.5),
                             reads=[R_stat], writes=[R_stat])
                        P.op("dve", lambda e, sl=sl, c0=c0: e.scalar_tensor_tensor(xt[sl][:, :], xt[sl][:, :], stat[:, c0 + 2:c0 + 3],
                                                                                  gfin_bc[:, :], ALU.mult, ALU.mult),
                             reads=[R_xt[sl], R_stat, R_gfin], writes=[R_xt[sl]])
                        P.dma("xt%d" % sl, out_d[r0:r0 + 128, :], xt[sl][:, :], reads=[R_xt[sl]])

                def last():
                    fin(1)
                    if gi_ + 2 < len(groups):
                        load_group(groups[gi_ + 2])

                from functools import partial as _pt
                return [_pt(adds, 0), _pt(fin, 0), _pt(adds, 1), last]

            pending_tail[0] = mk_tail()
        for it in pending_tail[0]:
            it()
        P.barrier()
        A.off = mark

    ffn_phase(1)

    if phases >= 2:
        p23_mark = A.off
        KT = A.alloc([128, 4, NTOK], BF16)
        Vt = A.alloc([128, NKT, 512], BF16)
        Ut = A.alloc([128, SEQ // 128, 512], BF16)
        R_KT = [Res("KT%d" % g) for g in range(NG_LAT + 1)]
        R_V = [Res("V%d" % i) for i in range(NKT)]
        R_U = [Res("U%d" % i) for i in range(SEQ // 128)]
        p2_mark = A.off
        wkvu = A.alloc([128, 8, 1536], BF16)
        rope_sb = [A.alloc([128, 2, G], F32) for _ in range(2)]
        kf4 = [A.alloc([128, G], F32) for _ in range(4)]
        kt1 = [A.alloc([128, G], F32) for _ in range(4)]
        kt2 = [A.alloc([128, G], F32) for _ in range(4)]
        hT2b = A.alloc([128, 8, G], BF16)
        hT2s = [hT, hT2b]
        R_hT2s = [R_hT, Res("hT2b")]
        R_wkvu = [Res("wkvu%d" % j) for j in range(12)]
        R_rope = [Res("rope0"), Res("rope1")]
        R_kf4 = [Res("kf%d" % h) for h in range(4)]
        R_kt1 = [Res("kt1_%d" % h) for h in range(4)]
        R_kt2 = [Res("kt2_%d" % h) for h in range(4)]
        win_v = win_d.rearrange("(kc p) n -> p kc n", p=128)

        def load_x1_group(g):
            for tt in range(2):
                sl = (g % 2) * 2 + tt
                r0 = g * G + tt * 128
                P.dma("xt%d" % sl, xt[sl][:, :], x1s_d[r0:r0 + 128, :], reads=[R_x1s[g * 2 + tt]], writes=[R_xt[sl]])

        def load_rope(g):
            s = g % 2
            P.dma("rope%d" % s, rope_sb[s][:, 0, :], cos_d[:, g * G:(g + 1) * G], writes=[R_rope[s]])
            P.dma("rope%d" % s, rope_sb[s][:, 1, :], sin_d[:, g * G:(g + 1) * G], reads=[], writes=[R_rope[s]])

        load_x1_group(0)
        load_x1_group(1)
        load_rope(0)
        load_rope(1)
        for j in range(12):
            load_cast(wkvu[:, :, j * 128:(j + 1) * 128], win_v[:, :, 512 + j * 128:512 + (j + 1) * 128], [128, 8, 128], R_wkvu[j])

        NG2 = NG_LAT + 1
        prep(2, [0, 1], 0, 1, hT2s[0], R_hT2s[0])
        for g in range(NG2):
            is_ctx = g == NG_LAT
            hc = hT2s[g % 2]
            R_hc = R_hT2s[g % 2]
            rs_ = g % 2
            if g + 1 < NG2:
                kind_n = 1 if (g + 1) == NG_LAT else 0
                nit = prep_items([((g + 1) % 2) * 2, ((g + 1) % 2) * 2 + 1], kind_n, 1, hT2s[(g + 1) % 2], R_hT2s[(g + 1) % 2])
            else:
                nit = [lambda: None] * 10
            for h in range(4):
                b = h // 2
                cs = (h % 2) * G
                for kc in range(8):
                    P.op("pe", lambda e, b=b, cs=cs, kc=kc, h=h, hc=hc: e.matmul(PS[b][:, cs:cs + G], wkvu[:, kc, h * 128:(h + 1) * 128],
                                                                               hc[:, kc, :], start=(kc == 0), stop=(kc == 7)),
                         reads=[R_wkvu[h], R_hc], writes=[PSR[b]])
            for h in range(4):
                b = h // 2
                cs = (h % 2) * G
                if is_ctx:
                    P.op("act", lambda e, b=b, cs=cs, h=h, g=g: e.activation(KT[:, h, g * G:(g + 1) * G], PS[b][:, cs:cs + G], AF.Copy),
                         reads=[PSR[b]], writes=[R_KT[g]])
                else:
                    P.op("act", lambda e, b=b, cs=cs, h=h: e.activation(kf4[h][:, :], PS[b][:, cs:cs + G], AF.Copy),
                         reads=[PSR[b]], writes=[R_kf4[h]])
            nit[0]()
            nit[1]()
            for tt in range(2):
                ti = g * 2 + tt
                for kc in range(8):
                    P.op("pe", lambda e, kc=kc, tt=tt, hc=hc: e.matmul(PS[4][:, :], hc[:, kc, tt * 128:(tt + 1) * 128], wkvu[:, kc, 512:1024],
                                                                       start=(kc == 0), stop=(kc == 7)),
                         reads=[R_hc] + R_wkvu[4:8], writes=[PSR[4]])
                P.op("act", lambda e, ti=ti: e.activation(Vt[:, ti, :], PS[4][:, :], AF.Copy), reads=[PSR[4]], writes=[R_V[ti]])
                if not is_ctx:
                    for kc in range(8):
                        P.op("pe", lambda e, kc=kc, tt=tt, hc=hc: e.matmul(PS[5][:, :], hc[:, kc, tt * 128:(tt + 1) * 128],
                                                                           wkvu[:, kc, 1024:1536], start=(kc == 0), stop=(kc == 7)),
                             reads=[R_hc] + R_wkvu[8:12], writes=[PSR[5]])
                    P.op("act", lambda e, ti=ti: e.activation(Ut[:, ti, :], PS[5][:, :], AF.Copy), reads=[PSR[5]], writes=[R_U[ti]])
                nit[2 + tt]()
            if not is_ctx:
                for h in range(4):
                    b = 2 + h // 2
                    cs = (h % 2) * G
                    P.op("pe", lambda e, b=b, cs=cs, h=h: e.matmul(PS[b][:, cs:cs + G], ropeP_f[:, :], kf4[h][:, :], start=True, stop=True),
                         reads=[R_ropeP, R_kf4[h]], writes=[PSR[b]])
                for h in range(4):
                    b = 2 + h // 2
                    cs = (h % 2) * G
                    P.op("dve", lambda e, b=b, cs=cs, rs_=rs_, h=h: e.tensor_tensor(kt1[h][:, :], PS[b][:, cs:cs + G], rope_sb[rs_][:, 1, :], ALU.mult),
                         reads=[PSR[b], R_rope[rs_]], writes=[R_kt1[h]])
                    P.op("pool", lambda e, h=h, rs_=rs_: e.tensor_tensor(kt2[h][:, :], kf4[h][:, :], rope_sb[rs_][:, 0, :], ALU.mult),
                         reads=[R_kf4[h], R_rope[rs_]], writes=[R_kt2[h]])
                    P.op("pool", lambda e, h=h, g=g: e.tensor_tensor(KT[:, h, g * G:(g + 1) * G], kt1[h][:, :], kt2[h][:, :], ALU.add),
                         reads=[R_kt1[h], R_kt2[h]], writes=[R_KT[g]])
                    if h == 1:
                        nit[4]()
                        nit[5]()
            else:
                nit[4]()
                nit[5]()
            for k_ in range(6, 10):
                nit[k_]()
            if g + 2 < NG_LAT:
                load_rope(g + 2)
            if g + 2 < NG2:
                load_x1_group(g + 2)
        P.barrier()
        A.off = p2_mark

    if phases >= 3:
        wq = A.alloc([128, 8, 512], BF16)
        wo = A.alloc([128, 8, D], BF16)
        wpl = A.alloc([128, 4, 128], BF16)
        band = A.alloc([128, 20, 128], BF16)
        gt2_bc = A.alloc([128, D], F32)
        Qpad = A.alloc([128, 4, 2 * G], BF16)
        Et = [A.alloc([128, 2 * G], BF16) for _ in range(3)]
        Rz = A.alloc([128, 2 * G], F32)
        o0 = A.alloc([128, G], F32)
        o1 = A.alloc([128, G], F32)
        osq = A.alloc([128, G], F32)
        rr = A.alloc([128, G], F32)
        attT = A.alloc([128, 8, G], BF16)
        dT = A.alloc([128, G], BF16)
        rope3_sb = [A.alloc([128, 2, G], F32) for _ in range(2)]
        qf = A.alloc([128, G], F32)
        q_t1 = A.alloc([128, G], F32)
        q_t2 = A.alloc([128, G], F32)
        R_wq = [Res("wq%d" % j) for j in range(4)]
        R_wo = [Res("wo%d" % j) for j in range(8)]
        R_wpl, R_band, R_gt2, R_Qpad = Res("wpl"), Res("band"), Res("gt2"), [Res("Qpad%d" % h) for h in range(4)]
        R_E = [Res("E0"), Res("E1"), Res("E2")]
        R_Rz, R_o0, R_o1, R_osq, R_rr = Res("Rz"), Res("o0"), Res("o1"), Res("osq"), Res("rr")
        R_attT = [Res("attT%d" % c) for c in range(8)]
        R_dT = Res("dT")
        R_rope3 = [Res("rope0"), Res("rope1")]
        R_qf, R_qt1, R_qt2 = Res("qf"), Res("q_t1"), Res("q_t2")
        win_v3 = win_d.rearrange("(kc p) n -> p kc n", p=128)
        SBK = (0, 1, 7)
        NT = SEQ // 128

        def slots3(g):
            return [(g % 2) * 2, (g % 2) * 2 + 1]

        def load_x1_group3(g):
            for tt in range(2):
                sl = (g % 2) * 2 + tt
                r0 = g * G + tt * 128
                P.dma("xt%d" % sl, xt[sl][:, :], x1s_d[r0:r0 + 128, :], reads=[R_x1s[g * 2 + tt]], writes=[R_xt[sl]])

        def load_rope3(g):
            s = g % 2
            P.dma("rope%d" % s, rope3_sb[s][:, 0, :], cos_d[:, g * G:(g + 1) * G], writes=[R_rope3[s]])
            P.dma("rope%d" % s, rope3_sb[s][:, 1, :], sin_d[:, g * G:(g + 1) * G], writes=[R_rope3[s]])

        load_x1_group3(0)
        load_x1_group3(1)
        load_rope3(0)
        load_rope3(1)
        P.dma("gt2", gt2_bc[:, :], mscr_d[0, 5 * D:6 * D].partition_broadcast(128), reads=[R_mscr], writes=[R_gt2])
        for j in range(4):
            load_cast(wq[:, :, j * 128:(j + 1) * 128], win_v3[:, :, j * 128:(j + 1) * 128], [128, 8, 128], R_wq[j])
        for c in range(8):
            load_cast(wo[:, c, :], wout_d[c * 128:(c + 1) * 128, :], [128, D], R_wo[c])
        load_cast(wpl[:, :, :], wpool_d.rearrange("g i o -> i g o"), [128, 4, 128], R_wpl)
        band_v = band_d.rearrange("p (b t) -> p b t", t=128)
        for j in range(3):
            nb = 8 if j < 2 else 4
            load_cast(band[:, j * 8:j * 8 + nb, :], band_v[:, j * 8:j * 8 + nb, :], [128, nb, 128], R_band)
        for h in range(4):
            P.op("pool", lambda e, h=h: e.memset(Qpad[:, h, :], 0.0), writes=[R_Qpad[h]])

        def q_proj_1(g, h):
            for kc in range(8):
                P.op("pe", lambda e, kc=kc, h=h: e.matmul(PS[5][:, 0:G], wq[:, kc, h * 128:(h + 1) * 128], hT[:, kc, :],
                                                          start=(kc == 0), stop=(kc == 7)),
                     reads=[R_wq[h], R_hT], writes=[PSR[5]])
            P.op("dve", lambda e: e.tensor_copy(qf[:, :], PS[5][:, 0:G]), reads=[PSR[5]], writes=[R_qf])

        def q_proj_2(g, h):
            s = g % 2
            P.op("pe", lambda e: e.matmul(PS[6][:, 0:G], ropeP_f[:, :], qf[:, :], start=True, stop=True),
                 reads=[R_ropeP, R_qf], writes=[PSR[6]])
            P.op("dve", lambda e, s=s: e.tensor_tensor(q_t1[:, :], PS[6][:, 0:G], rope3_sb[s][:, 1, :], ALU.mult),
                 reads=[PSR[6], R_rope3[s]], writes=[R_qt1])
            P.op("dve", lambda e, s=s: e.tensor_tensor(q_t2[:, :], qf[:, :], rope3_sb[s][:, 0, :], ALU.mult),
                 reads=[R_qf, R_rope3[s]], writes=[R_qt2])
            P.op("dve", lambda e, h=h: e.tensor_tensor(Qpad[0:64, h, 0:G], q_t1[0:64, :], q_t2[0:64, :], ALU.add),
                 reads=[R_qt1, R_qt2], writes=[R_Qpad[h]])
            P.op("dve", lambda e, h=h: e.tensor_tensor(Qpad[64:128, h, G:2 * G], q_t1[64:128, :], q_t2[64:128, :], ALU.add),
                 reads=[R_qt1, R_qt2], writes=[R_Qpad[h]])

        def epi_A():
            P.op("dve", lambda e: e.tensor_copy(o0[:, :], PS[2][:, 0:G]), reads=[PSR[2]], writes=[R_o0])
            P.op("dve", lambda e: e.tensor_copy(o1[:, :], PS[3][:, 0:G]), reads=[PSR[3]], writes=[R_o1])
            P.op("dve", lambda e: e.reciprocal(Rz[:, :], PS[4][:, :]), reads=[PSR[4]], writes=[R_Rz])

        def epi_B1():
            P.op("dve", lambda e: e.tensor_tensor(o0[:, :], o0[:, :], Rz[:, 0:G], ALU.mult),
                 reads=[R_o0, R_Rz], writes=[R_o0])
            P.op("dve", lambda e: e.tensor_tensor(o1[:, :], o1[:, :], Rz[:, G:2 * G], ALU.mult),
                 reads=[R_o1, R_Rz], writes=[R_o1])
            P.op("dve", lambda e: e.scalar_tensor_tensor(o0[:, :], o1[:, :], neglam[:, 0:1], o0[:, :], ALU.mult, ALU.add),
                 reads=[R_o0, R_o1, R_neglam], writes=[R_o0])
            P.op("dve", lambda e: e.tensor_tensor(osq[:, :], o0[:, :], o0[:, :], ALU.mult), reads=[R_o0], writes=[R_osq])

        def epi_B2(h):
            P.op("pe", lambda e: e.matmul(PS[5][:, 0:G], onesdiv_f[:, :], osq[:, :], start=True, stop=True),
                 reads=[R_onesdiv, R_osq], writes=[PSR[5]])
            P.op("act", lambda e: e.activation(rr[:, :], PS[5][:, 0:G], AF.Ln, bias=EPS), reads=[PSR[5]], writes=[R_rr])
            P.op("act", lambda e: e.activation(rr[:, :], rr[:, :], AF.Exp, scale=-0.5), reads=[R_rr], writes=[R_rr])
            P.op("dve", lambda e, h=h: e.scalar_tensor_tensor(attT[:, h, :], o0[:, :], gsubs[:, 0:1], rr[:, :], ALU.mult, ALU.mult),
                 reads=[R_o0, R_gsubs, R_rr], writes=[R_attT[h]])

        def pool_1(g, gi):
            for tt in range(2):
                ti = g * 2 + tt
                rs = [r for r in (-1, 0, 1) if 0 <= ti + r < NT]
                for n_, r in enumerate(rs):
                    blk = r + 1
                    if r == 0 and ti == 0:
                        blk = 3
                    if r == 0 and ti == NT - 1:
                        blk = 4
                    P.op("pe", lambda e, gi=gi, tt=tt, ti=ti, r=r, blk=blk, n_=n_, last=(n_ == len(rs) - 1):
                         e.matmul(PS[6][:, tt * 128:(tt + 1) * 128], Ut[:, ti + r, gi * 128:(gi + 1) * 128],
                                  band[:, gi * 5 + blk, :], start=(n_ == 0), stop=last),
                         reads=[R_U[ti + r], R_band], writes=[PSR[6]])
            P.op("dve", lambda e: e.tensor_copy(dT[:, :], PS[6][:, 0:G]), reads=[PSR[6]], writes=[R_dT])

        def pool_2(g, gi):
            P.op("pe", lambda e, gi=gi: e.matmul(PS[5][:, 0:G], wpl[:, gi, :], dT[:, :], start=True, stop=True),
                 reads=[R_wpl, R_dT], writes=[PSR[5]])
            P.op("dve", lambda e, gi=gi: e.tensor_scalar(attT[:, 4 + gi, :], PS[5][:, 0:G], psT[:, gi:gi + 1], None, ALU.mult),
                 reads=[PSR[5], R_psT], writes=[R_attT[4 + gi]])

        def out_proj(g, tt):
            sl = slots3(g)[tt]
            for dh in range(2):
                b = 5 + dh
                for c in range(8):
                    P.op("pe", lambda e, b=b, c=c, tt=tt, dh=dh: e.matmul(PS[b][:, :], attT[:, c, tt * 128:(tt + 1) * 128],
                                                                          wo[:, c, dh * 512:(dh + 1) * 512],
                                                                          start=(c == 0), stop=(c == 7)),
                         reads=[R_attT[c], R_wo[c]], writes=[PSR[b]])
                tb = tmpf[dh]
                P.op("dve", lambda e, b=b, dh=dh, tb=tb: e.tensor_tensor(tb[:, :], PS[b][:, :], gt2_bc[:, dh * 512:(dh + 1) * 512], ALU.mult),
                     reads=[PSR[b], R_gt2], writes=[R_tmpf[dh]])
                P.op("dve", lambda e, sl=sl, dh=dh, tb=tb: e.tensor_tensor(xt[sl][:, dh * 512:(dh + 1) * 512], tb[:, :],
                                                                          xt[sl][:, dh * 512:(dh + 1) * 512], ALU.add),
                     reads=[R_tmpf[dh], R_xt[sl]], writes=[R_xt[sl]])
            r0 = g * G + tt * 128
            P.dma("xt%d" % sl, x1s_d[r0:r0 + 128, :], xt[sl][:, :], reads=[R_xt[sl]], writes=[R_x1s[g * 2 + tt]])
            if tt == 1 and g + 2 < NG_LAT:
                load_x1_group3(g + 2)
                load_rope3(g + 2)

        def head_loop(h, hooks):
            def QK(kt):
                b = SBK[kt % 3]
                kg = kt // 2
                for m in range(2):
                    P.op("pe", lambda e, b=b, m=m, kt=kt: e.matmul(PS[b][:, m * G:(m + 1) * G], KT[:, h, kt * 128:(kt + 1) * 128],
                                                                   Qpad[:, h, m * G:(m + 1) * G], start=True, stop=True),
                         reads=[R_KT[kg], R_Qpad[h]], writes=[PSR[b]])
                P.op("act", lambda e, b=b, kt=kt: e.activation(Et[kt % 3][:, :], PS[b][:, :], AF.Exp, scale=0.125),
                     reads=[PSR[b]], writes=[R_E[kt % 3]])

            def AVZ(kt):
                eb = kt % 3
                for m in range(2):
                    P.op("pe", lambda e, eb=eb, m=m, kt=kt: e.matmul(PS[2 + m][:, 0:G], Vt[:, kt, h * 128:(h + 1) * 128],
                                                                     Et[eb][:, m * G:(m + 1) * G], start=(kt == 0), stop=(kt == NKT - 1)),
                         reads=[R_V[kt], R_E[eb]], writes=[PSR[2 + m]])
                P.op("pe", lambda e, eb=eb, kt=kt: e.matmul(PS[4][:, :], ones_bf[:, :], Et[eb][:, :],
                                                            start=(kt == 0), stop=(kt == NKT - 1)),
                     reads=[R_ones, R_E[eb]], writes=[PSR[4]])

            QK(0)
            QK(1)
            for kt in range(NKT):
                if kt + 2 < NKT:
                    QK(kt + 2)
                AVZ(kt)
                for fn in hooks.get(kt, ()):
                    fn()

        prep(2, slots3(0), 0, 1, hT, R_hT, banks=(5, 6))
        for h in range(4):
            q_proj_1(0, h)
            q_proj_2(0, h)

        from functools import partial as _p
        for g in range(NG_LAT):
            for h in range(4):
                hooks = {}

                def add(kt, fn):
                    hooks.setdefault(kt, []).append(fn)

                if h > 0 or g > 0:
                    ph = (h - 1) % 4
                    add(1, epi_B1)
                    add(4, _p(epi_B2, ph))
                if h == 0 and g > 0:
                    add(6, _p(q_proj_1, g, 3))
                    add(8, _p(q_proj_2, g, 3))
                    for gi in range(4):
                        add(10 + 3 * gi, _p(pool_1, g - 1, gi))
                        add(12 + 3 * gi, _p(pool_2, g - 1, gi))
                    add(24, _p(out_proj, g - 1, 0))
                    add(29, _p(out_proj, g - 1, 1))
                if h == 3 and g + 1 < NG_LAT:
                    its = prep_items(slots3(g + 1), 0, 1, hT, R_hT, (5, 6))
                    for n_, kt_ in enumerate((5, 6, 7, 8, 10, 11, 12, 13, 14, 15)):
                        add(kt_, its[n_])
                    for hh in range(3):
                        add(17 + 5 * hh, _p(q_proj_1, g + 1, hh))
                        add(19 + 5 * hh, _p(q_proj_2, g + 1, hh))
                head_loop(h, hooks)
                epi_A()
        epi_B1()
        epi_B2(3)
        for gi in range(4):
            pool_1(NG_LAT - 1, gi)
            pool_2(NG_LAT - 1, gi)
        out_proj(NG_LAT - 1, 0)
        out_proj(NG_LAT - 1, 1)
        P.barrier()
        A.off = p23_mark

    if phases >= 4:
        ffn_phase(2)

    P.emit()
    return nc, A.peak


def _consts():
    ident = np.eye(128, dtype=np.float32)
    Pm = np.zeros((128, 128), np.float32)
    for dst in range(128):
        if dst % 32 < 16:
            Pm[dst, dst + 16] = -1.0
        else:
            Pm[dst, dst - 16] = 1.0
    ropeP = np.ascontiguousarray(Pm.T)
    rows = SEQ // 64
    row = np.repeat(np.arange(rows, dtype=np.float32), 64)
    col = np.tile(np.arange(64, dtype=np.float32), rows)
    nf = 16
    freqs = (np.float32(10000.0) ** (-np.arange(nf, dtype=np.float32) / np.float32(nf))).astype(np.float32)
    ar = row[:, None] * freqs
    ac = col[:, None] * freqs
    ang = np.concatenate([ar, ar, ac, ac], axis=-1).astype(np.float32)
    cos = np.cos(ang).astype(np.float32).T
    sin = np.sin(ang).astype(np.float32).T
    cos128 = np.ascontiguousarray(np.concatenate([cos, cos], axis=0))
    sin128 = np.ascontiguousarray(np.concatenate([sin, sin], axis=0))
    n = SEQ
    band = np.zeros((128, 20, 128), np.float32)

    def mval(w, t, s):
        lo = w // 2
        hi = w - w // 2 - 1
        st = max(t - lo, 0)
        en = min(t + hi, n - 1)
        v = 0.0
        if st <= s <= en:
            v = 1.0 / (en - st + 1)
        if s == t:
            v -= 1.0
        return v

    for gi, w in enumerate(POOL_W):
        for blk, (ti, r) in enumerate([(5, -1), (5, 0), (5, 1), (0, 0), (n // 128 - 1, 0)]):
            for tp in range(128):
                t = ti * 128 + tp
                for s in range(max(0, t - 17), min(n, t + 17)):
                    sp = s - (ti + r) * 128
                    if 0 <= sp < 128:
                        band[sp, gi * 5 + blk, tp] = mval(w, t, s)
    return dict(c_ident=ident, c_ropeP=ropeP, c_cos=cos128, c_sin=sin128,
                c_band=np.ascontiguousarray(band.reshape(128, 20 * 128)))


_CACHE = {}


def _get_nc(debug=False, phases=4):
    key = (debug, phases)
    if key not in _CACHE:
        _CACHE[key] = build_nc(debug=debug, phases=phases)[0]
    return _CACHE[key]


def make_in_maps(inputs):
    f = lambda a: np.ascontiguousarray(np.asarray(a, dtype=np.float32))
    consts = _consts()
    shared = dict(
        w_mod=f(inputs["w_mod"][0]), b_mod=f(inputs["b_mod"][0]), g_ffn1=f(inputs["g_ffn1"][0]),
        ffn1_w_in=f(inputs["ffn1_w_in"][0]), ffn1_w_out=f(inputs["ffn1_w_out"][0]), g_mix=f(inputs["g_mix"][0]),
        w_in=f(inputs["w_in"][0]),
        lam4=f(np.concatenate([np.asarray(inputs["lambda_q1"][0]), np.asarray(inputs["lambda_k1"][0]),
                               np.asarray(inputs["lambda_q2"][0]), np.asarray(inputs["lambda_k2"][0])])),
        g_sub=f(inputs["g_sub"][0]), w_pool=f(inputs["w_pool"][0]), pool_scale=f(inputs["pool_scale"][0]),
        w_out=f(inputs["w_out"][0]), g_ffn2=f(inputs["g_ffn2"][0]), ffn2_w_in=f(inputs["ffn2_w_in"][0]),
        ffn2_w_out=f(inputs["ffn2_w_out"][0]), g_final=f(inputs["g_final"]),
    )
    shared.update(consts)
    x = np.asarray(inputs["x"], dtype=np.float32)
    c = np.asarray(inputs["c"], dtype=np.float32)
    ctx = np.asarray(inputs["ctx"], dtype=np.float32)
    c_ctx = np.asarray(inputs["c_ctx"], dtype=np.float32)
    maps = []
    for b in range(8):
        m = dict(shared)
        m["x"] = np.ascontiguousarray(x[b])
        m["ctx"] = np.ascontiguousarray(ctx[b])
        m["cvec"] = np.ascontiguousarray(np.stack([c[b], c_ctx], axis=0))
        maps.append(m)
    return maps


def kernel(**inputs):
    nc = _get_nc()
    in_maps = make_in_maps(inputs)
    res = run_bass_kernel_spmd(nc, in_maps, core_ids=list(range(8)))
    return np.stack([np.asarray(r["out"], dtype=np.float32) for r in res.results], axis=0)
```
